# Optimizing a Trainium2 kernel written in Bass

```python
import jax, jax.numpy as jnp
from jax import lax
import numpy as np

D_MODEL = 1024
BATCH = 8
SEQ = 4096
DEPTH = 4

GRID_W = 64
BLOCK = 128
EPS = 1e-6

RET_HEADS = 4
RET_DIM = 64
RET_W = RET_HEADS * RET_DIM
LRU_BLOCKS = 4
LRU_BLOCK_DIM = 64
LRU_W = LRU_BLOCKS * LRU_BLOCK_DIM
LRU_CONV = 4
LRU_C = 8.0
ATT_HEADS = 8
ATT_KV_HEADS = 2
ATT_GROUP = ATT_HEADS // ATT_KV_HEADS
ATT_DIM = 64
ATT_W = ATT_HEADS * ATT_DIM
KV_W = ATT_KV_HEADS * ATT_DIM
ROPE_BASE = 10000.0

MIX_W = RET_W + LRU_W + ATT_W
IN_W = 4 * RET_W + 2 * LRU_W + ATT_W + 2 * KV_W

PEER_HEADS = 8
PEER_KEYS = 128
PEER_EXPERTS = PEER_KEYS * PEER_KEYS
PEER_HALF = 128
PEER_QDIM = 2 * PEER_HALF
PEER_TOPK = 16
PEER_CHUNK = 128

kernel_name = "hybrid_ret_lru_gqa_peer_encoder"


def rms_norm(x, w):
    xf = x.astype(jnp.float32)
    y = xf * lax.rsqrt(jnp.mean(xf * xf, axis=-1, keepdims=True) + EPS)
    return (y * w.astype(jnp.float32)).astype(x.dtype)


def rotate_half(x):
    x1, x2 = jnp.split(x, 2, axis=-1)
    return jnp.concatenate([-x2, x1], axis=-1)


def axial_rotate_half(x):
    xr, xc = jnp.split(x, 2, axis=-1)
    return jnp.concatenate([rotate_half(xr), rotate_half(xc)], axis=-1)


def rope_table(pos, inv_freq):
    ang = pos[:, None] * inv_freq[None, :]
    ang = jnp.concatenate([ang, ang], axis=-1)
    return jnp.cos(ang), jnp.sin(ang)


def split_cols(t, widths):
    return jnp.split(t, [int(o) for o in np.cumsum(widths)[:-1]], axis=-1)


def retention_dir(q, k, v, log_g, include_diag):
    B, H, S, Dk = q.shape
    Dv = v.shape[-1]
    n = S // BLOCK
    qc = q.reshape(B, H, n, BLOCK, Dk)
    kc = k.reshape(B, H, n, BLOCK, Dk)
    vc = v.reshape(B, H, n, BLOCK, Dv)
    idx = jnp.arange(BLOCK, dtype=jnp.float32)
    diff = idx[:, None] - idx[None, :]
    mask = (diff >= 0) if include_diag else (diff > 0)
    lg = log_g.astype(jnp.float32)
    dmat = jnp.where(mask[None], jnp.exp(lg[:, None, None] * jnp.where(mask, diff, 0.0)[None]), 0.0)
    scores = jnp.einsum('bhncd,bhnmd->bhncm', qc, kc) * dmat[None, :, None]
    o_intra = jnp.einsum('bhncm,bhnme->bhnce', scores, vc)
    k_w = jnp.exp(lg[:, None] * (BLOCK - 1 - idx)[None])
    kv = jnp.einsum('bhncd,bhnce->bhnde', kc * k_w[None, :, None, :, None], vc)
    chunk_decay = jnp.exp(lg * BLOCK)[None, :, None, None]

    def step(state, kv_c):
        return state * chunk_decay + kv_c, state

    _, prev = lax.scan(step, jnp.zeros((B, H, Dk, Dv), jnp.float32), jnp.moveaxis(kv, 2, 0))
    prev = jnp.moveaxis(prev, 0, 2)
    q_w = jnp.exp(lg[:, None] * (idx + 1.0)[None])
    o_cross = jnp.einsum('bhncd,bhnde->bhnce', qc * q_w[None, :, None, :, None], prev)
    return (o_intra + o_cross).reshape(B, H, S, Dv)


def retention_group(q, k, v, g, log_decay, gn_w, cos, sin):
    B, S, _ = q.shape

    def heads(t):
        return t.reshape(B, S, RET_HEADS, RET_DIM).transpose(0, 2, 1, 3).astype(jnp.float32)

    qh = heads(q)
    kh = heads(k)
    qh = qh * cos + rotate_half(qh) * sin
    kh = (kh * cos + rotate_half(kh) * sin) * (RET_DIM ** -0.5)
    vh = heads(v)
    fwd = retention_dir(qh, kh, vh, log_decay[0], True)
    bwd = jnp.flip(retention_dir(jnp.flip(qh, 2), jnp.flip(kh, 2), jnp.flip(vh, 2), log_decay[1], False), 2)
    o = fwd + bwd
    o = o * lax.rsqrt(jnp.mean(o * o, axis=-1, keepdims=True) + EPS)
    o = o.transpose(0, 2, 1, 3).reshape(B, S, RET_W) * gn_w.astype(jnp.float32)
    return (jax.nn.silu(g.astype(jnp.float32)) * o).astype(q.dtype)


def _lin_combine(left, right):
    a_l, b_l = left
    a_r, b_r = right
    return a_l * a_r, a_r * b_l + b_r


def rglru_group(xb, gb, conv_w, conv_b, gate_w, gate_b, lam):
    B, S, W = xb.shape
    left = LRU_CONV // 2
    xc = lax.conv_general_dilated(xb, conv_w[:, None, :], window_strides=(1,),
                                  padding=[(left, LRU_CONV - 1 - left)],
                                  dimension_numbers=('NWC', 'WIO', 'NWC'),
                                  feature_group_count=W) + conv_b
    xf = xc.astype(jnp.float32)
    blk = xf.reshape(B, S, LRU_BLOCKS, LRU_BLOCK_DIM)
    gates = jnp.einsum('bsnk,dgnkm->dgbsnm', blk, gate_w.astype(jnp.float32)).reshape(2, 2, B, S, W)
    gates = gates + gate_b.astype(jnp.float32)[:, :, None, None, :]
    r = jax.nn.sigmoid(gates[:, 0])
    i = jax.nn.sigmoid(gates[:, 1])
    log_a = -LRU_C * r * jax.nn.softplus(-lam.astype(jnp.float32))[:, None, None, :]
    a = jnp.exp(log_a)
    b = jnp.sqrt(-jnp.expm1(2.0 * log_a)) * (i * xf[None])
    _, h_f = lax.associative_scan(_lin_combine, (a[0], b[0]), axis=1)
    _, h_b = lax.associative_scan(_lin_combine, (jnp.flip(a[1], 1), jnp.flip(b[1], 1)), axis=1)
    h = h_f + jnp.flip(h_b, 1)
    return (h * jax.nn.gelu(gb.astype(jnp.float32))).astype(xb.dtype)


def attention_group(q, k, v, qn_w, kn_w, cos, sin):
    B, S, _ = q.shape
    qh = rms_norm(q.reshape(B, S, ATT_KV_HEADS, ATT_GROUP, ATT_DIM), qn_w)
    kh = rms_norm(k.reshape(B, S, ATT_KV_HEADS, ATT_DIM), kn_w)
    vh = v.reshape(B, S, ATT_KV_HEADS, ATT_DIM)
    qh = (qh * cos[:, None, None, :] + axial_rotate_half(qh) * sin[:, None, None, :]).astype(q.dtype)
    kh = (kh * cos[:, None, :] + axial_rotate_half(kh) * sin[:, None, :]).astype(k.dtype)
    scale = ATT_DIM ** -0.5
    nb = S // BLOCK
    qb = qh.reshape(B, nb, BLOCK, ATT_KV_HEADS, ATT_GROUP, ATT_DIM).transpose(1, 0, 2, 3, 4, 5)

    def block(qblk):
        s = jnp.einsum('bckgd,bskd->bkgcs', qblk, kh, preferred_element_type=jnp.float32) * scale
        p = jax.nn.softmax(s, axis=-1).astype(vh.dtype)
        return jnp.einsum('bkgcs,bskd->bckgd', p, vh)

    o = lax.map(block, qb)
    return o.transpose(1, 0, 2, 3, 4, 5).reshape(B, S, ATT_W)


def peer(x, wq, keys, u, v):
    B, S, D = x.shape
    T = B * S
    xt = x.reshape(T, D)
    q = (xt @ wq).reshape(T, PEER_HEADS, 2, PEER_HALF)
    s = jnp.einsum('thpk,hpnk->thpn', q, keys, preferred_element_type=jnp.float32)
    sv, si = lax.top_k(s, PEER_TOPK)
    cand = (sv[:, :, 0, :, None] + sv[:, :, 1, None, :]).reshape(T, PEER_HEADS, PEER_TOPK * PEER_TOPK)
    cand_idx = (si[:, :, 0, :, None] * PEER_KEYS + si[:, :, 1, None, :]).reshape(T, PEER_HEADS, PEER_TOPK * PEER_TOPK)
    fv, fi = lax.top_k(cand, PEER_TOPK)
    idx = jnp.take_along_axis(cand_idx, fi, axis=-1)
    g = jax.nn.softmax(fv, axis=-1).astype(x.dtype)
    nc = T // PEER_CHUNK

    def chunk(args):
        xc, ic, gc = args
        ue = jnp.take(u, ic, axis=0)
        hid = jax.nn.gelu(jnp.einsum('cd,chkd->chk', xc, ue))
        ve = jnp.take(v, ic, axis=0)
        return jnp.einsum('chk,chkd->cd', gc * hid, ve)

    y = lax.map(chunk, (xt.reshape(nc, PEER_CHUNK, D),
                        idx.reshape(nc, PEER_CHUNK, PEER_HEADS, PEER_TOPK),
                        g.reshape(nc, PEER_CHUNK, PEER_HEADS, PEER_TOPK)))
    return y.reshape(B, S, D)


def setup_inputs(seed: int = 0) -> dict:
    key = jax.random.key(seed)
    ks = jax.random.split(key, 24)
    L, D = DEPTH, D_MODEL
    f32 = jnp.float32
    nrm = lambda k, shape, s: jax.random.normal(k, shape, f32) * s
    x = nrm(ks[0], (BATCH, SEQ, D), 1.0)
    ln1_w = 1.0 + nrm(ks[1], (L, D), 0.01)
    w_in = nrm(ks[2], (L, D, IN_W), D ** -0.5)
    base = jnp.log(1.0 - 2.0 ** (-5.0 - jnp.arange(RET_HEADS, dtype=f32)))
    ret_log_decay = base[None, None, :] * (1.0 + nrm(ks[3], (L, 2, RET_HEADS), 0.05))
    ret_gn_w = 1.0 + nrm(ks[4], (L, RET_W), 0.01)
    lru_conv_w = nrm(ks[5], (L, LRU_CONV, LRU_W), LRU_CONV ** -0.5)
    lru_conv_b = nrm(ks[6], (L, LRU_W), 0.01)
    lru_gate_w = nrm(ks[7], (L, 2, 2, LRU_BLOCKS, LRU_BLOCK_DIM, LRU_BLOCK_DIM), LRU_BLOCK_DIM ** -0.5)
    lru_gate_b = nrm(ks[8], (L, 2, 2, LRU_W), 0.01)
    a_c = jax.random.uniform(ks[9], (L, 2, LRU_W), f32, 0.9, 0.999)
    a0 = a_c ** (1.0 / LRU_C)
    lru_lambda = jnp.log(a0) - jnp.log1p(-a0)
    attn_q_norm = 1.0 + nrm(ks[10], (L, ATT_DIM), 0.01)
    attn_k_norm = 1.0 + nrm(ks[11], (L, ATT_DIM), 0.01)
    w_out = nrm(ks[12], (L, MIX_W, D), (2.0 * MIX_W) ** -0.5)
    ln2_w = 1.0 + nrm(ks[13], (L, D), 0.01)
    peer_wq = nrm(ks[14], (L, D, PEER_HEADS * PEER_QDIM), D ** -0.5)
    peer_keys = nrm(ks[15], (L, PEER_HEADS, 2, PEER_KEYS, PEER_HALF), PEER_HALF ** -0.5)
    peer_u = nrm(ks[16], (L, PEER_EXPERTS, D), D ** -0.5)
    peer_v = nrm(ks[17], (L, PEER_EXPERTS, D), D ** -0.5)
    lnf_w = 1.0 + nrm(ks[18], (D,), 0.01)
    return {"x": x, "ln1_w": ln1_w, "w_in": w_in, "ret_log_decay": ret_log_decay,
            "ret_gn_w": ret_gn_w, "lru_conv_w": lru_conv_w, "lru_conv_b": lru_conv_b,
            "lru_gate_w": lru_gate_w, "lru_gate_b": lru_gate_b, "lru_lambda": lru_lambda,
            "attn_q_norm": attn_q_norm, "attn_k_norm": attn_k_norm, "w_out": w_out,
            "ln2_w": ln2_w, "peer_wq": peer_wq, "peer_keys": peer_keys, "peer_u": peer_u,
            "peer_v": peer_v, "lnf_w": lnf_w}


def reference(x, ln1_w, w_in, ret_log_decay, ret_gn_w, lru_conv_w, lru_conv_b, lru_gate_w,
              lru_gate_b, lru_lambda, attn_q_norm, attn_k_norm, w_out, ln2_w, peer_wq,
              peer_keys, peer_u, peer_v, lnf_w):
    S = x.shape[1]
    n_rows = S // GRID_W
    f32 = jnp.float32
    t = jnp.arange(S, dtype=f32)
    rows = jnp.repeat(jnp.arange(n_rows, dtype=f32), GRID_W)
    cols = jnp.tile(jnp.arange(GRID_W, dtype=f32), n_rows)
    ret_inv = 1.0 / (10000.0 ** jnp.linspace(0.0, 1.0, RET_DIM // 2, dtype=f32))
    ret_cos, ret_sin = rope_table(t, ret_inv)
    ax_n = ATT_DIM // 4
    ax_inv = ROPE_BASE ** (-jnp.arange(ax_n, dtype=f32) / ax_n)
    cr, sr = rope_table(rows, ax_inv)
    cc, sc = rope_table(cols, ax_inv)
    ax_cos = jnp.concatenate([cr, cc], axis=-1)
    ax_sin = jnp.concatenate([sr, sc], axis=-1)
    widths = [RET_W] * 4 + [LRU_W] * 2 + [ATT_W, KV_W, KV_W]
    for l in range(DEPTH):
        h = rms_norm(x, ln1_w[l])
        proj = h @ w_in[l]
        rq, rk, rv, rg, lx, lgt, aq, ak, av = split_cols(proj, widths)
        o_ret = retention_group(rq, rk, rv, rg, ret_log_decay[l], ret_gn_w[l], ret_cos, ret_sin)
        o_lru = rglru_group(lx, lgt, lru_conv_w[l], lru_conv_b[l], lru_gate_w[l], lru_gate_b[l], lru_lambda[l])
        o_att = attention_group(aq, ak, av, attn_q_norm[l], attn_k_norm[l], ax_cos, ax_sin)
        x = x + jnp.concatenate([o_ret, o_lru, o_att], axis=-1) @ w_out[l]
        x = x + peer(rms_norm(x, ln2_w[l]), peer_wq[l], peer_keys[l], peer_u[l], peer_v[l])
    return rms_norm(x, lnf_w)
```

```python
import numpy as np
from contextlib import ExitStack
import concourse.bass as bass
import concourse.mybir as mybir
from concourse.bass_utils import run_bass_kernel_spmd

F32 = mybir.dt.float32
BF16 = mybir.dt.bfloat16
U32 = mybir.dt.uint32
I32 = mybir.dt.int32
AF = mybir.ActivationFunctionType
ALU = mybir.AluOpType
AX = mybir.AxisListType


class Buf:
    __slots__ = ("name", "w", "r", "ap")

    def __init__(self, name, ap=None):
        self.name = name
        self.w = None
        self.r = {}
        self.ap = ap


class Prog:
    NDMA = 8
    CAP = 30000

    def __init__(self, nc, stack):
        self.nc = nc
        self.stack = stack
        self.E = {"pe": nc.tensor, "act": nc.scalar, "dve": nc.vector, "pool": nc.gpsimd, "sp": nc.sync}
        self.nsem = 0
        self.sem = {}
        self.cnt = {}
        self.last = {}
        for k in ("pe", "act", "dve", "pool"):
            self.sem[k] = self._newsem("s_" + k)
            self.cnt[k] = 0
            self.last[k] = None
        self.dslot = {}
        self.dcnt = {}
        for q in ("sp", "pool", "act"):
            self.dslot[q] = [{"sem": self._newsem("d_%s%d" % (q, i)), "val": 0, "last": None} for i in range(self.NDMA)]
            self.dcnt[q] = 0
        self.seen = {k: {} for k in self.E}
        self.nwait = 0
        self.nins = 0
        self.uid = 0

    def _newsem(self, name):
        self.nsem += 1
        return self.stack.enter_context(self.nc.semaphore("%s_%d" % (name, self.nsem)))

    def _uid(self):
        self.uid += 1
        return self.uid

    def sb(self, name, shape, dtype=F32, stack=None):
        t = (stack or self.stack).enter_context(self.nc.sbuf_tensor("sb%d_%s" % (self._uid(), name), list(shape), dtype))
        return Buf(name, t)

    def ps(self, name, shape, dtype=F32, stack=None):
        t = (stack or self.stack).enter_context(self.nc.psum_tensor("ps%d_%s" % (self._uid(), name), list(shape), dtype))
        return Buf(name, t)

    def buf(self, name):
        return Buf(name)

    def _wait(self, e, ev):
        sem, val = ev
        if self.seen[e].get(sem.num, 0) >= val:
            return
        self.E[e].wait_ge(sem, val)
        self.seen[e][sem.num] = val
        self.nwait += 1

    def _deps(self, e, reads, writes, skip_sem=None):
        for b in reads:
            if b.w is not None and (skip_sem is None or b.w[0].num != skip_sem):
                self._wait(e, b.w)
        for b in writes:
            if b.w is not None and (skip_sem is None or b.w[0].num != skip_sem):
                self._wait(e, b.w)
            for ev in b.r.values():
                if skip_sem is None or ev[0].num != skip_sem:
                    self._wait(e, ev)

    def _mark(self, ev, reads, writes):
        for b in reads:
            b.r[ev[0].num] = ev
        for b in writes:
            b.w = ev
            b.r = {}

    def op(self, e, fn, reads=(), writes=(), same_engine_sync=True):
        if self.cnt[e] >= self.CAP:
            self.sem[e] = self._newsem("s_" + e)
            self.cnt[e] = 0
        skip = None if (same_engine_sync and e != "pe") else self.sem[e].num
        self._deps(e, reads, writes, skip)
        ins = fn()
        self.cnt[e] += 1
        ins.then_inc(self.sem[e], 1)
        ev = (self.sem[e], self.cnt[e])
        self.last[e] = ev
        self._mark(ev, reads, writes)
        self.nins += 1
        return ins

    def dma(self, q, out, in_, reads=(), writes=(), fn=None):
        slot = self.dslot[q][self.dcnt[q] % self.NDMA]
        if slot["last"] is not None:
            self._wait(q, slot["last"])
        if slot["val"] + 16 > self.CAP:
            slot["sem"] = self._newsem("d_" + q)
            slot["val"] = 0
        self._deps(q, reads, writes)
        if fn is None:
            ins = self.E[q].dma_start(out=out, in_=in_)
        else:
            ins = fn()
        ins.then_inc(slot["sem"], 16)
        slot["val"] += 16
        ev = (slot["sem"], slot["val"])
        slot["last"] = ev
        self.dcnt[q] += 1
        self._mark(ev, reads, writes)
        self.nins += 1
        return ins

    def barrier(self):
        evs = [ev for ev in self.last.values() if ev is not None]
        for q in self.dslot:
            evs += [sl["last"] for sl in self.dslot[q] if sl["last"] is not None]
        for e in self.E:
            for ev in evs:
                self._wait(e, ev)

    def finish(self):
        self.barrier()


DM = 1024
INW = 2304
EPS = 1e-6
NEG = -1.0e30


class Cfg:
    def __init__(self, T=4096, L=4, debug=False):
        self.T = T
        self.L = L
        self.NT = T // 128
        self.debug = debug


def bc_mid(ap, n):
    a = ap.ap
    return bass.AP(ap.tensor, ap.offset, [list(a[0]), [0, n]] + [list(x) for x in a[1:]])


def bc_last(ap, n):
    a = ap.ap
    return bass.AP(ap.tensor, ap.offset, [list(x) for x in a] + [[0, n]])


def host_constants(T):
    f32 = np.float32
    t = np.arange(T, dtype=f32)
    ret_inv = (1.0 / (10000.0 ** np.linspace(0.0, 1.0, 32, dtype=f32))).astype(f32)
    ang = t[:, None] * ret_inv[None, :]
    rc = np.concatenate([np.cos(ang), np.cos(ang)], -1).astype(f32)
    rs = np.concatenate([-np.sin(ang), np.sin(ang)], -1).astype(f32)
    rows = np.floor(t / 64.0).astype(f32)
    cols = (t - rows * 64.0).astype(f32)
    ax_inv = (10000.0 ** (-np.arange(16, dtype=f32) / 16.0)).astype(f32)
    ar = rows[:, None] * ax_inv[None, :]
    ac = cols[:, None] * ax_inv[None, :]
    acos = np.concatenate([np.cos(ar), np.cos(ar), np.cos(ac), np.cos(ac)], -1).astype(f32)
    asin = np.concatenate([-np.sin(ar), np.sin(ar), -np.sin(ac), np.sin(ac)], -1).astype(f32)
    tab = np.stack([rc, rs, rc * f32(0.125), rs * f32(0.125), acos, asin], 1).astype(f32)
    j = np.arange(128, dtype=f32)[:, None]
    i = np.arange(128, dtype=f32)[None, :]
    cm = np.stack([np.eye(128, dtype=f32), np.maximum(i - j, 0) + 0 * j, np.maximum(j - i, 0) + 0 * i,
                   (i >= j).astype(f32), (j > i).astype(f32)], 1).astype(f32)
    rw = np.stack([np.broadcast_to(i + 1.0, (64, 128)), np.broadcast_to(128.0 - i, (64, 128))], 1).astype(f32)
    cw = np.concatenate([127.0 - j, j], 1).astype(f32)
    return {"c_tab": np.ascontiguousarray(tab), "c_mat": np.ascontiguousarray(cm),
            "c_rw": np.ascontiguousarray(rw), "c_cw": np.ascontiguousarray(cw)}


class D:
    pass


def declare_dram(nc, cfg):
    T, L = cfg.T, cfg.L
    d = D()

    def inp(name, shape, dt=F32):
        return nc.dram_tensor(name, list(shape), dt, kind="ExternalInput").ap()

    def scr(name, shape, dt=F32):
        kind = "ExternalOutput" if cfg.debug else "Internal"
        return nc.dram_tensor(name, list(shape), dt, kind=kind).ap()

    d.x = inp("x", [T, DM])
    d.ln1_w = inp("ln1_w", [L, DM])
    d.w_in = inp("w_in", [L, DM, INW])
    d.ret_log_decay = inp("ret_log_decay", [L, 8])
    d.ret_gn_w = inp("ret_gn_w", [L, 256])
    d.lru_conv_w = inp("lru_conv_w", [L, 4, 256])
    d.lru_conv_b = inp("lru_conv_b", [L, 256])
    d.lru_bd = inp("lru_bd", [L, 2, 2, 2, 128, 128])
    d.lru_gate_b = inp("lru_gate_b", [L, 2, 2, 256])
    d.lru_lambda = inp("lru_lambda", [L, 2, 256])
    d.attn_q_norm = inp("attn_q_norm", [L, 64])
    d.attn_k_norm = inp("attn_k_norm", [L, 64])
    d.w_out = inp("w_out", [L, DM, DM])
    d.ln2_w = inp("ln2_w", [L, DM])
    d.peer_wq = inp("peer_wq", [L, DM, 2048])
    d.peer_keysT = inp("peer_keysT", [L, 16, 128, 128])
    d.peer_u = [inp("peer_u%d" % i, [16384, DM]) for i in range(L)]
    d.peer_v = [inp("peer_v%d" % i, [16384, DM]) for i in range(L)]
    d.lnf_w = inp("lnf_w", [1, DM])
    d.c_tab = inp("c_tab", [T, 6, 64])
    d.c_mat = inp("c_mat", [128, 5, 128])
    d.c_rw = inp("c_rw", [64, 2, 128])
    d.c_cw = inp("c_cw", [128, 2])
    d.out = nc.dram_tensor("out", [T, DM], F32, kind="ExternalOutput").ap()
    d.XR = scr("XR", [T, DM])
    d.X2 = scr("X2", [T, DM])
    d.XN = scr("XN", [T, DM])
    d.TM = scr("TM", [T, INW])
    d.FT = scr("FT", [1664, T])
    d.MIXT = scr("MIXT", [DM, T])
    d.IDX = scr("IDX", [T, 128], I32)
    d.GW = scr("GW", [T, 128])
    return d


def load_consts(P, d, cfg):
    nc = P.nc
    C = D()
    C.mat = P.sb("c_mat", [128, 5, 128])
    P.dma("sp", C.mat.ap[:], d.c_mat, writes=[C.mat])
    C.ident = C.mat.ap[:, 0, :]
    C.rw = P.sb("c_rw", [64, 2, 128])
    P.dma("sp", C.rw.ap[:], d.c_rw, writes=[C.rw])
    C.cw = P.sb("c_cw", [128, 2])
    P.dma("sp", C.cw.ap[:], d.c_cw, writes=[C.cw])
    return C


def transpose_blocks(P, C, src_buf, src_aps, pT, dst_buf, dst_aps, evac_engs=("act", "dve")):
    nc = P.nc
    n = len(src_aps)
    for g0 in range(0, n, 8):
        g1 = min(n, g0 + 8)
        for k in range(g0, g1):
            s = (k - g0) * 128
            P.op("pe", lambda: nc.tensor.transpose(pT.ap[:, s:s + 128], src_aps[k], C.ident),
                 reads=[src_buf, C.mat], writes=[pT])
        for k in range(g0, g1):
            s = (k - g0) * 128
            e = evac_engs[k % len(evac_engs)]
            if e == "act":
                P.op("act", lambda: nc.scalar.copy(dst_aps[k], pT.ap[:, s:s + 128]), reads=[pT], writes=[dst_buf])
            else:
                P.op(e, lambda: P.E[e].tensor_copy(dst_aps[k], pT.ap[:, s:s + 128]), reads=[pT], writes=[dst_buf])


def rope(P, eng, out_ap, x_ap, cos_ap, ss_ap, H, R, tA, tB, rbufs, wbufs):
    nc = P.nc
    E = P.E[eng]
    hs = 32 // R
    W = H * 64
    x3 = x_ap.rearrange("p (h d) -> p h d", d=64)
    a3 = tA.ap[:, 0:W].rearrange("p (h d) -> p h d", d=64)
    P.op(eng, lambda: E.tensor_tensor(a3, x3, bc_mid(cos_ap, H), ALU.mult), reads=rbufs, writes=[tA])
    x5 = x_ap.rearrange("p (h r two d) -> p h r two d", r=R, two=2, d=hs)
    b5 = tB.ap[:, 0:W].rearrange("p (h r two d) -> p h r two d", r=R, two=2, d=hs)
    s4 = ss_ap.rearrange("p (r two d) -> p r two d", r=R, two=2, d=hs)
    for two in range(2):
        sv = s4[:, :, two, :]
        if R == 1:
            o_ = b5[:, :, 0, two, :]
            i_ = x5[:, :, 0, 1 - two, :]
            s_ = bc_mid(sv[:, 0, :], H)
        else:
            o_ = b5[:, :, :, two, :]
            i_ = x5[:, :, :, 1 - two, :]
            s_ = bc_mid(sv, H)
        P.op(eng, lambda: E.tensor_tensor(o_, i_, s_, ALU.mult), reads=rbufs, writes=[tB])
    P.op(eng, lambda: E.tensor_tensor(out_ap, tA.ap[:, 0:W], tB.ap[:, 0:W], ALU.add), reads=[tA, tB], writes=wbufs)


def rms_rstd(P, x_buf, x_ap, junk, ssq, n):
    nc = P.nc
    P.op("act", lambda: nc.scalar.activation(junk.ap[:, 0:n], x_ap, AF.Square, accum_out=ssq.ap[:, 0:1]),
         reads=[x_buf], writes=[junk, ssq])
    P.op("dve", lambda: nc.vector.tensor_scalar(ssq.ap[:, 0:1], ssq.ap[:, 0:1], 1.0 / n, EPS, ALU.mult, ALU.add),
         reads=[ssq], writes=[ssq])
    P.op("act", lambda: nc.scalar.activation(ssq.ap[:, 0:1], ssq.ap[:, 0:1], AF.Sqrt), reads=[ssq], writes=[ssq])
    P.op("dve", lambda: nc.vector.reciprocal(ssq.ap[:, 0:1], ssq.ap[:, 0:1]), reads=[ssq], writes=[ssq])


def stage_a(P, C, d, cfg, l, xsrc):
    nc = P.nc
    NT = cfg.NT
    with ExitStack() as st:
        win = P.sb("a_win", [128, 8, INW], stack=st)
        for c in range(8):
            P.dma("sp" if c % 2 == 0 else "act", win.ap[:, c, :], d.w_in[l, c * 128:(c + 1) * 128, :], writes=[win])
        lnw = P.sb("a_lnw", [128, DM], stack=st)
        P.dma("sp", lnw.ap[:], d.ln1_w[l:l + 1, :].to_broadcast([128, DM]), writes=[lnw])
        qkw = P.sb("a_qkw", [128, 2, 64], stack=st)
        P.dma("sp", qkw.ap[:, 0, :], d.attn_q_norm[l:l + 1, :].to_broadcast([128, 64]), writes=[qkw])
        P.dma("sp", qkw.ap[:, 1, :], d.attn_k_norm[l:l + 1, :].to_broadcast([128, 64]), writes=[qkw])
        P.op("dve", lambda: nc.vector.tensor_scalar(qkw.ap[:, 0, :], qkw.ap[:, 0, :], 0.125, None, ALU.mult),
             reads=[qkw], writes=[qkw])
        xts = [P.sb("a_x%d" % i, [128, DM], stack=st) for i in range(2)]
        tabs = [P.sb("a_tab%d" % i, [128, 6, 64], stack=st) for i in range(2)]
        h = P.sb("a_h", [128, DM], stack=st)
        junk = P.sb("a_junk", [128, DM], stack=st)
        ssq = P.sb("a_ssq", [128, 1], stack=st)
        hT = P.sb("a_hT", [128, 8, 128], stack=st)
        preps = [P.sb("a_prep%d" % i, [128, INW], stack=st) for i in range(2)]
        tA = P.sb("a_tA", [128, 512], stack=st)
        tB = P.sb("a_tB", [128, 512], stack=st)
        tC = P.sb("a_tC", [128, 640], stack=st)
        s10 = P.sb("a_s10", [128, 10], stack=st)
        tA2 = P.sb("a_tA2", [128, 512], stack=st)
        tB2 = P.sb("a_tB2", [128, 512], stack=st)
        ftts = [P.sb("a_ftt%d" % i, [128, 13, 128], stack=st) for i in range(2)]
        pT = P.ps("a_pT", [128, 1024], stack=st)
        pP = P.ps("a_pP", [128, 2560], stack=st)
        tmb = P.buf("TM")
        ftb = P.buf("FT")
        FTv = d.FT.rearrange("(b p) t -> p b t", p=128)

        for t in range(NT):
            xt = xts[t % 2]
            tab = tabs[t % 2]
            prep = preps[t % 2]
            ftt = ftts[t % 2]
            rows = slice(t * 128, (t + 1) * 128)
            P.dma("sp", xt.ap[:], xsrc[rows, :], writes=[xt])
            P.dma("sp", tab.ap[:], d.c_tab[rows, :, :], writes=[tab])
            rms_rstd(P, xt, xt.ap[:], junk, ssq, DM)
            P.op("dve", lambda: nc.vector.scalar_tensor_tensor(h.ap[:], xt.ap[:], ssq.ap[:, 0:1], lnw.ap[:], ALU.mult, ALU.mult),
                 reads=[xt, ssq, lnw], writes=[h])
            transpose_blocks(P, C, h, [h.ap[:, c * 128:(c + 1) * 128] for c in range(8)], pT,
                             hT, [hT.ap[:, c, :] for c in range(8)])
            for nb in range(5):
                n0, n1 = nb * 512, min(INW, nb * 512 + 512)
                for c in range(8):
                    P.op("pe", lambda: nc.tensor.matmul(pP.ap[:, n0:n1], hT.ap[:, c, :], win.ap[:, c, n0:n1],
                                                        start=(c == 0), stop=(c == 7)), reads=[hT, win], writes=[pP])
            for nb in range(5):
                n0, n1 = nb * 512, min(INW, nb * 512 + 512)
                if nb % 2 == 0:
                    P.op("act", lambda: nc.scalar.copy(prep.ap[:, n0:n1], pP.ap[:, n0:n1]), reads=[pP], writes=[prep])
                else:
                    P.op("dve", lambda: nc.vector.tensor_copy(prep.ap[:, n0:n1], pP.ap[:, n0:n1]), reads=[pP], writes=[prep])
            rope(P, "pool", prep.ap[:, 0:256], prep.ap[:, 0:256], tab.ap[:, 0, :], tab.ap[:, 1, :], 4, 1, tA, tB,
                 [prep, tab], [prep])
            rope(P, "pool", prep.ap[:, 256:512], prep.ap[:, 256:512], tab.ap[:, 2, :], tab.ap[:, 3, :], 4, 1, tA, tB,
                 [prep, tab], [prep])
            qk = prep.ap[:, 1536:2176]
            P.op("dve", lambda: nc.vector.tensor_tensor(tC.ap[:], qk, qk, ALU.mult), reads=[prep], writes=[tC])
            P.op("dve", lambda: nc.vector.tensor_reduce(s10.ap[:], tC.ap[:].rearrange("p (h d) -> p h d", d=64), AX.X, ALU.add),
                 reads=[tC], writes=[s10])
            P.op("dve", lambda: nc.vector.tensor_scalar(s10.ap[:], s10.ap[:], 1.0 / 64, EPS, ALU.mult, ALU.add),
                 reads=[s10], writes=[s10])
            P.op("act", lambda: nc.scalar.activation(s10.ap[:], s10.ap[:], AF.Sqrt), reads=[s10], writes=[s10])
            P.op("dve", lambda: nc.vector.reciprocal(s10.ap[:], s10.ap[:]), reads=[s10], writes=[s10])
            P.op("dve", lambda: nc.vector.tensor_tensor(tC.ap[:].rearrange("p (h d) -> p h d", d=64),
                                                        qk.rearrange("p (h d) -> p h d", d=64),
                                                        bc_last(s10.ap[:], 64), ALU.mult), reads=[prep, s10], writes=[tC])
            P.op("dve", lambda: nc.vector.tensor_tensor(tC.ap[:, 0:512].rearrange("p (h d) -> p h d", d=64),
                                                        tC.ap[:, 0:512].rearrange("p (h d) -> p h d", d=64),
                                                        bc_mid(qkw.ap[:, 0, :], 8), ALU.mult), reads=[tC, qkw], writes=[tC])
            P.op("dve", lambda: nc.vector.tensor_tensor(tC.ap[:, 512:640].rearrange("p (h d) -> p h d", d=64),
                                                        tC.ap[:, 512:640].rearrange("p (h d) -> p h d", d=64),
                                                        bc_mid(qkw.ap[:, 1, :], 2), ALU.mult), reads=[tC, qkw], writes=[tC])
            rope(P, "dve", prep.ap[:, 1536:2048], tC.ap[:, 0:512], tab.ap[:, 4, :], tab.ap[:, 5, :], 8, 2, tA2, tB2,
                 [tC, tab], [prep])
            rope(P, "dve", prep.ap[:, 2048:2176], tC.ap[:, 512:640], tab.ap[:, 4, :], tab.ap[:, 5, :], 2, 2, tA2, tB2,
                 [tC, tab], [prep])
            P.dma("pool", d.TM[rows, :], prep.ap[:], reads=[prep], writes=[tmb])
            cols = [k * 128 for k in range(4)] + [1024 + k * 128 for k in range(9)]
            transpose_blocks(P, C, prep, [prep.ap[:, c0:c0 + 128] for c0 in cols], pT,
                             ftt, [ftt.ap[:, k, :] for k in range(13)])
            P.dma("pool", FTv[:, :, t * 128:(t + 1) * 128], ftt.ap[:], reads=[ftt], writes=[ftb])
    P.barrier()
    return


def make_in_map(inputs, core, cfg):
    T, L = cfg.T, cfg.L
    f = lambda a: np.ascontiguousarray(np.asarray(a, dtype=np.float32))
    m = {}
    m["x"] = f(inputs["x"][core, :T])
    for k in ("ln1_w", "w_in", "ret_gn_w", "lru_conv_w", "lru_conv_b", "lru_gate_b", "lru_lambda",
              "attn_q_norm", "attn_k_norm", "w_out", "ln2_w", "peer_wq"):
        m[k] = f(inputs[k][:L])
    for i in range(L):
        m["peer_u%d" % i] = f(inputs["peer_u"][i])
        m["peer_v%d" % i] = f(inputs["peer_v"][i])
    m["ret_log_decay"] = f(np.asarray(inputs["ret_log_decay"])[:L].reshape(L, 8))
    gw = np.asarray(inputs["lru_gate_w"], dtype=np.float32)[:L]
    bd = np.zeros((L, 2, 2, 2, 128, 128), np.float32)
    for g in range(2):
        for b in range(2):
            bd[:, :, :, g, b * 64:(b + 1) * 64, b * 64:(b + 1) * 64] = gw[:, :, :, g * 2 + b]
    m["lru_bd"] = bd
    keys = np.asarray(inputs["peer_keys"], dtype=np.float32)[:L]
    m["peer_keysT"] = np.ascontiguousarray(keys.reshape(L, 16, 128, 128).transpose(0, 1, 3, 2))
    m["lnf_w"] = f(np.asarray(inputs["lnf_w"]).reshape(1, DM))
    m.update(host_constants(T))
    return m


def gelu_tanh(P, eng, out_ap, x_ap, tmp_ap, xb, ob, tb):
    nc = P.nc
    E = P.E[eng]
    P.op(eng, lambda: E.tensor_tensor(tmp_ap, x_ap, x_ap, ALU.mult), reads=[xb], writes=[tb])
    P.op(eng, lambda: E.tensor_scalar(tmp_ap, tmp_ap, 0.044715, 1.0, ALU.mult, ALU.add), reads=[tb], writes=[tb])
    P.op(eng, lambda: E.tensor_tensor(tmp_ap, tmp_ap, x_ap, ALU.mult), reads=[tb, xb], writes=[tb])
    P.op("act", lambda: nc.scalar.activation(tmp_ap, tmp_ap, AF.Sigmoid, scale=1.5957691216057308), reads=[tb], writes=[tb])
    P.op(eng, lambda: E.tensor_tensor(out_ap, x_ap, tmp_ap, ALU.mult), reads=[tb, xb], writes=[ob])


def stage_b(P, C, d, cfg, l):
    nc = P.nc
    NT = cfg.NT
    MX = d.MIXT.rearrange("(b p) t -> p b t", p=128)
    with ExitStack() as st:
        lg = P.sb("b_lg", [128, 8], stack=st)
        P.dma("sp", lg.ap[:], d.ret_log_decay[l:l + 1, :].to_broadcast([128, 8]), writes=[lg])
        gnw = P.sb("b_gnw", [128, 256], stack=st)
        P.dma("sp", gnw.ap[:], d.ret_gn_w[l:l + 1, :].to_broadcast([128, 256]), writes=[gnw])
        dm = P.sb("b_dm", [128, 4, 128], stack=st)
        t1 = P.sb("b_t1", [128, 128], stack=st)
        t2 = P.sb("b_t2", [128, 128], stack=st)
        qw = P.sb("b_qw", [64, 8, 128], stack=st)
        kw = P.sb("b_kw", [128, 8], stack=st)
        dec = P.sb("b_dec", [64, 8], stack=st)
        for h in range(4):
            P.op("act", lambda: nc.scalar.activation(t1.ap[:], C.mat.ap[:, 1, :], AF.Exp, scale=lg.ap[:, h:h + 1]),
                 reads=[C.mat, lg], writes=[t1])
            P.op("dve", lambda: nc.vector.tensor_tensor(t1.ap[:], t1.ap[:], C.mat.ap[:, 3, :], ALU.mult), reads=[t1, C.mat], writes=[t1])
            P.op("act", lambda: nc.scalar.activation(t2.ap[:], C.mat.ap[:, 2, :], AF.Exp, scale=lg.ap[:, 4 + h:5 + h]),
                 reads=[C.mat, lg], writes=[t2])
            P.op("dve", lambda: nc.vector.tensor_tensor(t2.ap[:], t2.ap[:], C.mat.ap[:, 4, :], ALU.mult), reads=[t2, C.mat], writes=[t2])
            P.op("dve", lambda: nc.vector.tensor_tensor(dm.ap[:, h, :], t1.ap[:], t2.ap[:], ALU.add), reads=[t1, t2], writes=[dm])
            for dr in range(2):
                k = dr * 4 + h
                P.op("act", lambda: nc.scalar.activation(qw.ap[:, k, :], C.rw.ap[:, dr, :], AF.Exp, scale=lg.ap[0:64, k:k + 1]),
                     reads=[C.rw, lg], writes=[qw])
                P.op("act", lambda: nc.scalar.activation(kw.ap[:, k:k + 1], C.cw.ap[:, dr:dr + 1], AF.Exp, scale=lg.ap[:, k:k + 1]),
                     reads=[C.cw, lg], writes=[kw])
        P.op("act", lambda: nc.scalar.activation(dec.ap[:], lg.ap[0:64, :], AF.Exp, scale=128.0), reads=[lg], writes=[dec])

        KV = P.sb("b_KV", [64, NT, 8, 64], stack=st)
        ST = P.sb("b_ST", [64, NT, 8, 64], stack=st)
        kvs = [P.sb("b_kv%d" % i, [128, 512], stack=st) for i in range(2)]
        kwk = P.sb("b_kwk", [128, 8, 64], stack=st)
        pKV = P.ps("b_pKV", [64, 8, 64], stack=st)
        for c in range(NT):
            rows = slice(c * 128, (c + 1) * 128)
            kv = kvs[c % 2]
            P.dma("sp", kv.ap[:], d.TM[rows, 256:768], writes=[kv])
            for k in range(8):
                h = k % 4
                e = "dve" if k % 2 == 0 else "pool"
                P.op(e, lambda: P.E[e].tensor_scalar(kwk.ap[:, k, :], kv.ap[:, h * 64:(h + 1) * 64], kw.ap[:, k:k + 1], None, ALU.mult),
                     reads=[kv, kw], writes=[kwk])
            for k in range(8):
                h = k % 4
                P.op("pe", lambda: nc.tensor.matmul(pKV.ap[:, k, :], kwk.ap[:, k, :], kv.ap[:, 256 + h * 64:256 + (h + 1) * 64],
                                                    start=True, stop=True), reads=[kwk, kv], writes=[pKV])
            P.op("act", lambda: nc.scalar.copy(KV.ap[:, c, :, :], pKV.ap[:]), reads=[pKV], writes=[KV])
        P.op("dve", lambda: nc.vector.memset(ST.ap[:, 0, 0:4, :], 0.0), writes=[ST])
        P.op("dve", lambda: nc.vector.memset(ST.ap[:, NT - 1, 4:8, :], 0.0), writes=[ST])
        for c in range(1, NT):
            for h in range(4):
                P.op("dve", lambda: nc.vector.scalar_tensor_tensor(ST.ap[:, c, h, :], ST.ap[:, c - 1, h, :], dec.ap[:, h:h + 1],
                                                                   KV.ap[:, c - 1, h, :], ALU.mult, ALU.add),
                     reads=[ST, KV, dec], writes=[ST])
            cb = NT - 1 - c
            for h in range(4, 8):
                P.op("dve", lambda: nc.vector.scalar_tensor_tensor(ST.ap[:, cb, h, :], ST.ap[:, cb + 1, h, :], dec.ap[:, h:h + 1],
                                                                   KV.ap[:, cb + 1, h, :], ALU.mult, ALU.add),
                     reads=[ST, KV, dec], writes=[ST])
        qks = [P.sb("b_qk%d" % i, [64, 8, 128], stack=st) for i in range(2)]
        vgs = [P.sb("b_vg%d" % i, [128, 512], stack=st) for i in range(2)]
        qwq = P.sb("b_qwq", [64, 8, 128], stack=st)
        SM = P.sb("b_SM", [128, 4, 128], stack=st)
        o = P.sb("b_o", [128, 256], stack=st)
        sq = P.sb("b_sq", [128, 256], stack=st)
        s4 = P.sb("b_s4", [128, 4], stack=st)
        sg = P.sb("b_sg", [128, 256], stack=st)
        mts = [P.sb("b_mt%d" % i, [128, 2, 128], stack=st) for i in range(2)]
        pS = P.ps("b_pS", [128, 4, 128], stack=st)
        pO = P.ps("b_pO", [128, 4, 64], stack=st)
        pT = P.ps("b_pT", [128, 1024], stack=st)
        mxb = P.buf("MIXT")
        FTq = d.FT[0:512, :].rearrange("(a dd) t -> dd a t", dd=64)
        for c in range(NT):
            rows = slice(c * 128, (c + 1) * 128)
            qk = qks[c % 2]
            vg = vgs[c % 2]
            mt = mts[c % 2]
            P.dma("sp", qk.ap[:], FTq[:, :, c * 128:(c + 1) * 128], writes=[qk])
            P.dma("sp", vg.ap[:], d.TM[rows, 512:1024], writes=[vg])
            for dr in range(2):
                P.op("pool", lambda: nc.gpsimd.tensor_tensor(qwq.ap[:, dr * 4:dr * 4 + 4, :], qk.ap[:, 0:4, :], qw.ap[:, dr * 4:dr * 4 + 4, :], ALU.mult),
                     reads=[qk, qw], writes=[qwq])
            for h in range(4):
                P.op("pe", lambda: nc.tensor.matmul(pS.ap[:, h, :], qk.ap[:, 4 + h, :], qk.ap[:, h, :], start=True, stop=True),
                     reads=[qk], writes=[pS])
            P.op("dve", lambda: nc.vector.tensor_tensor(SM.ap[:], pS.ap[:], dm.ap[:], ALU.mult), reads=[pS, dm], writes=[SM])
            for h in range(4):
                P.op("pe", lambda: nc.tensor.matmul(pO.ap[:, h, :], SM.ap[:, h, :], vg.ap[:, h * 64:(h + 1) * 64], start=True, stop=False),
                     reads=[SM, vg], writes=[pO])
                P.op("pe", lambda: nc.tensor.matmul(pO.ap[:, h, :], qwq.ap[:, h, :], ST.ap[:, c, h, :], start=False, stop=False),
                     reads=[qwq, ST], writes=[pO])
                P.op("pe", lambda: nc.tensor.matmul(pO.ap[:, h, :], qwq.ap[:, 4 + h, :], ST.ap[:, c, 4 + h, :], start=False, stop=True),
                     reads=[qwq, ST], writes=[pO])
            P.op("act", lambda: nc.scalar.copy(o.ap[:], pO.ap[:].rearrange("p h e -> p (h e)")), reads=[pO], writes=[o])
            P.op("dve", lambda: nc.vector.tensor_tensor(sq.ap[:], o.ap[:], o.ap[:], ALU.mult), reads=[o], writes=[sq])
            P.op("dve", lambda: nc.vector.tensor_reduce(s4.ap[:], sq.ap[:].rearrange("p (h e) -> p h e", e=64), AX.X, ALU.add),
                 reads=[sq], writes=[s4])
            P.op("dve", lambda: nc.vector.tensor_scalar(s4.ap[:], s4.ap[:], 1.0 / 64, EPS, ALU.mult, ALU.add), reads=[s4], writes=[s4])
            P.op("act", lambda: nc.scalar.activation(s4.ap[:], s4.ap[:], AF.Sqrt), reads=[s4], writes=[s4])
            P.op("dve", lambda: nc.vector.reciprocal(s4.ap[:], s4.ap[:]), reads=[s4], writes=[s4])
            P.op("dve", lambda: nc.vector.tensor_tensor(o.ap[:].rearrange("p (h e) -> p h e", e=64),
                                                        o.ap[:].rearrange("p (h e) -> p h e", e=64), bc_last(s4.ap[:], 64), ALU.mult),
                 reads=[o, s4], writes=[o])
            P.op("dve", lambda: nc.vector.tensor_tensor(o.ap[:], o.ap[:], gnw.ap[:], ALU.mult), reads=[o, gnw], writes=[o])
            P.op("act", lambda: nc.scalar.activation(sg.ap[:], vg.ap[:, 256:512], AF.Sigmoid), reads=[vg], writes=[sg])
            P.op("pool", lambda: nc.gpsimd.tensor_tensor(sg.ap[:], sg.ap[:], vg.ap[:, 256:512], ALU.mult), reads=[sg, vg], writes=[sg])
            P.op("dve", lambda: nc.vector.tensor_tensor(o.ap[:], o.ap[:], sg.ap[:], ALU.mult), reads=[o, sg], writes=[o])
            transpose_blocks(P, C, o, [o.ap[:, 0:128], o.ap[:, 128:256]], pT, mt, [mt.ap[:, 0, :], mt.ap[:, 1, :]])
            P.dma("pool", MX[:, 0:2, c * 128:(c + 1) * 128], mt.ap[:], reads=[mt], writes=[mxb])
    P.barrier()


def stage_c(P, C, d, cfg, l):
    nc = P.nc
    T = cfg.T
    with ExitStack() as st:
        xp = P.sb("c_xp", [128, T + 3], stack=st)
        xc = P.sb("c_xc", [128, T], stack=st)
        rr = P.sb("c_rr", [128, T], stack=st)
        ii = P.sb("c_ii", [128, T], stack=st)
        aa = P.sb("c_aa", [128, T], stack=st)
        bb = P.sb("c_bb", [128, T], stack=st)
        hf = P.sb("c_hf", [128, T], stack=st)
        gg = P.sb("c_gg", [128, T], stack=st)
        cwt = P.sb("c_cwt", [128, 5], stack=st)
        gb = P.sb("c_gb", [128, 4], stack=st)
        lam = P.sb("c_lam", [128, 2], stack=st)
        nsp = P.sb("c_nsp", [128, 4], stack=st)
        bd = P.sb("c_bd", [128, 4, 128], stack=st)
        pG = [P.ps("c_pG%d" % i, [128, 512], stack=st) for i in range(2)]
        mxb = P.buf("MIXT")
        col = lambda ap1d: ap1d.rearrange("(c o) -> c o", o=1)
        for g in range(2):
            ch = slice(g * 128, (g + 1) * 128)
            for j in range(4):
                P.dma("sp", cwt.ap[:, j:j + 1], col(d.lru_conv_w[l, j, ch]), writes=[cwt])
            P.dma("sp", cwt.ap[:, 4:5], col(d.lru_conv_b[l, ch]), writes=[cwt])
            for dr in range(2):
                P.dma("sp", lam.ap[:, dr:dr + 1], col(d.lru_lambda[l, dr, ch]), writes=[lam])
                for gt in range(2):
                    P.dma("sp", gb.ap[:, dr * 2 + gt:dr * 2 + gt + 1], col(d.lru_gate_b[l, dr, gt, ch]), writes=[gb])
                    P.dma("sp", bd.ap[:, dr * 2 + gt, :], d.lru_bd[l, dr, gt, g, :, :], writes=[bd])
            P.op("dve", lambda: nc.vector.memset(xp.ap[:, 0:2], 0.0), writes=[xp])
            P.op("dve", lambda: nc.vector.memset(xp.ap[:, T + 2:T + 3], 0.0), writes=[xp])
            P.dma("sp", xp.ap[:, 2:T + 2], d.FT[512 + g * 128:512 + (g + 1) * 128, :], writes=[xp])
            P.dma("act", gg.ap[:], d.FT[768 + g * 128:768 + (g + 1) * 128, :], writes=[gg])
            P.op("act", lambda: nc.scalar.activation(nsp.ap[:, 0:2], lam.ap[:], AF.Exp, scale=-1.0), reads=[lam], writes=[nsp])
            P.op("act", lambda: nc.scalar.activation(nsp.ap[:, 0:2], nsp.ap[:, 0:2], AF.Ln, bias=1.0), reads=[nsp], writes=[nsp])
            P.op("dve", lambda: nc.vector.tensor_scalar(nsp.ap[:, 2:4], nsp.ap[:, 0:2], -16.0, None, ALU.mult), reads=[nsp], writes=[nsp])
            P.op("dve", lambda: nc.vector.tensor_scalar(nsp.ap[:, 0:2], nsp.ap[:, 0:2], -8.0, None, ALU.mult), reads=[nsp], writes=[nsp])
            P.op("dve", lambda: nc.vector.tensor_scalar(xc.ap[:], xp.ap[:, 0:T], cwt.ap[:, 0:1], cwt.ap[:, 4:5], ALU.mult, ALU.add),
                 reads=[xp, cwt], writes=[xc])
            for j in range(1, 4):
                P.op("dve", lambda: nc.vector.scalar_tensor_tensor(xc.ap[:], xp.ap[:, j:j + T], cwt.ap[:, j:j + 1], xc.ap[:], ALU.mult, ALU.add),
                     reads=[xp, cwt, xc], writes=[xc])
            for dr in range(2):
                for gt in range(2):
                    dst = rr if gt == 0 else ii
                    for s in range(0, T, 512):
                        pg = pG[(s // 512) % 2]
                        P.op("pe", lambda: nc.tensor.matmul(pg.ap[:], bd.ap[:, dr * 2 + gt, :], xc.ap[:, s:s + 512], start=True, stop=True),
                             reads=[bd, xc], writes=[pg])
                        P.op("act", lambda: nc.scalar.activation(dst.ap[:, s:s + 512], pg.ap[:], AF.Sigmoid,
                                                                 bias=gb.ap[:, dr * 2 + gt:dr * 2 + gt + 1]),
                             reads=[pg, gb], writes=[dst])
                P.op("act", lambda: nc.scalar.activation(aa.ap[:], rr.ap[:], AF.Exp, scale=nsp.ap[:, dr:dr + 1]), reads=[rr, nsp], writes=[aa])
                P.op("act", lambda: nc.scalar.activation(rr.ap[:], rr.ap[:], AF.Exp, scale=nsp.ap[:, 2 + dr:3 + dr]), reads=[rr, nsp], writes=[rr])
                P.op("dve", lambda: nc.vector.tensor_scalar(rr.ap[:], rr.ap[:], -1.0, 1.0, ALU.mult, ALU.add), reads=[rr], writes=[rr])
                P.op("act", lambda: nc.scalar.activation(rr.ap[:], rr.ap[:], AF.Sqrt), reads=[rr], writes=[rr])
                P.op("dve", lambda: nc.vector.tensor_tensor(bb.ap[:], rr.ap[:], ii.ap[:], ALU.mult), reads=[rr, ii], writes=[bb])
                P.op("pool", lambda: nc.gpsimd.tensor_tensor(bb.ap[:], bb.ap[:], xc.ap[:], ALU.mult), reads=[bb, xc], writes=[bb])
                if dr == 0:
                    P.op("dve", lambda: nc.vector.tensor_tensor_scan(hf.ap[:], aa.ap[:], bb.ap[:], 0.0, ALU.mult, ALU.add),
                         reads=[aa, bb], writes=[hf])
                else:
                    P.op("dve", lambda: nc.vector.tensor_tensor_scan(ii.ap[:, ::-1], aa.ap[:, ::-1], bb.ap[:, ::-1], 0.0, ALU.mult, ALU.add),
                         reads=[aa, bb], writes=[ii])
                    P.op("dve", lambda: nc.vector.tensor_tensor(hf.ap[:], hf.ap[:], ii.ap[:], ALU.add), reads=[hf, ii], writes=[hf])
            gelu_tanh(P, "pool", gg.ap[:], gg.ap[:], aa.ap[:], gg, gg, aa)
            P.op("dve", lambda: nc.vector.tensor_tensor(hf.ap[:], hf.ap[:], gg.ap[:], ALU.mult), reads=[hf, gg], writes=[hf])
            P.dma("sp", d.MIXT[256 + g * 128:256 + (g + 1) * 128, :], hf.ap[:], reads=[hf], writes=[mxb])
    P.barrier()


ATT_DT = BF16


def stage_d(P, C, d, cfg, l):
    nc = P.nc
    T, NT = cfg.T, cfg.NT
    NQG = T // 512
    MX = d.MIXT.rearrange("(b p) t -> p b t", p=128)
    with ExitStack() as st:
        kT = P.sb("d_kT", [64, 2, T], ATT_DT, stack=st)
        vaug = P.sb("d_vaug", [128, NT, 2, 65], ATT_DT, stack=st)
        stg = P.sb("d_stg", [64, 8, 512], stack=st)
        vst = P.sb("d_vst", [128, NT, 128], stack=st)
        FTk = d.FT[1536:1664, :].rearrange("(a dd) t -> dd a t", dd=64)
        for s in range(0, T, 512):
            P.dma("sp", stg.ap[:, 0:2, :], FTk[:, :, s:s + 512], writes=[stg])
            P.op("dve", lambda: nc.vector.tensor_copy(kT.ap[:, :, s:s + 512], stg.ap[:, 0:2, :]), reads=[stg], writes=[kT])
        P.dma("sp", vst.ap[:], d.TM[:, 2176:2304].rearrange("(n p) c -> p n c", p=128), writes=[vst])
        P.op("dve", lambda: nc.vector.memset(vaug.ap[:, :, :, 64:65], 1.0), writes=[vaug])
        P.op("dve", lambda: nc.vector.tensor_copy(vaug.ap[:, :, :, 0:64], vst.ap[:].rearrange("p n (k e) -> p n k e", e=64)),
             reads=[vst], writes=[vaug])
        qTb = P.sb("d_qTb", [64, 8, 512], ATT_DT, stack=st)
        pts = [P.sb("d_pt%d" % i, [128, 512], ATT_DT, stack=st) for i in range(2)]
        oat = P.sb("d_oat", [128, 4, 512], stack=st)
        rs = P.sb("d_rs", [128, 4], stack=st)
        mt = P.sb("d_mt", [128, 4, 512], stack=st)
        pS = [P.ps("d_pS%d" % i, [128, 512], stack=st) for i in range(2)]
        pO = [P.ps("d_pO%d" % i, [128, 512], stack=st) for i in range(4)]
        pT = P.ps("d_pT", [128, 1024], stack=st)
        mxb = P.buf("MIXT")
        FTq = d.FT[1024:1536, :].rearrange("(a dd) t -> dd a t", dd=64)
        for qg in range(NQG):
            P.dma("sp", stg.ap[:], FTq[:, :, qg * 512:(qg + 1) * 512], writes=[stg])
            P.op("dve", lambda: nc.vector.tensor_copy(qTb.ap[:], stg.ap[:]), reads=[stg], writes=[qTb])
            for h in range(8):
                kvh = h // 4
                for jt in range(NT):
                    ps = pS[jt % 2]
                    pt = pts[jt % 2]
                    P.op("pe", lambda: nc.tensor.matmul(ps.ap[:], kT.ap[:, kvh, jt * 128:(jt + 1) * 128], qTb.ap[:, h, :], start=True, stop=True),
                         reads=[kT, qTb], writes=[ps])
                    P.op("act", lambda: nc.scalar.activation(pt.ap[:], ps.ap[:], AF.Exp), reads=[ps], writes=[pt])
                    for b in range(4):
                        P.op("pe", lambda: nc.tensor.matmul(pO[b].ap[:, 0:65], pt.ap[:, b * 128:(b + 1) * 128], vaug.ap[:, jt, kvh, :],
                                                            start=(jt == 0), stop=(jt == NT - 1)), reads=[pt, vaug], writes=[pO[b]])
                for b in range(4):
                    P.op("dve", lambda: nc.vector.reciprocal(rs.ap[:, b:b + 1], pO[b].ap[:, 64:65]), reads=[pO[b]], writes=[rs])
                    P.op("dve", lambda: nc.vector.tensor_scalar(oat.ap[:, b, h * 64:(h + 1) * 64], pO[b].ap[:, 0:64], rs.ap[:, b:b + 1], None, ALU.mult),
                         reads=[pO[b], rs], writes=[oat])
            srcs = [oat.ap[:, b, fb * 128:(fb + 1) * 128] for fb in range(4) for b in range(4)]
            dsts = [mt.ap[:, fb, b * 128:(b + 1) * 128] for fb in range(4) for b in range(4)]
            transpose_blocks(P, C, oat, srcs, pT, mt, dsts)
            P.dma("pool", MX[:, 4:8, qg * 512:(qg + 1) * 512], mt.ap[:], reads=[mt], writes=[mxb])
    P.barrier()


def stage_e(P, C, d, cfg, l, xsrc):
    nc = P.nc
    NT = cfg.NT
    MXv = d.MIXT.rearrange("(c p) t -> p c t", p=128)
    with ExitStack() as st:
        wout = P.sb("e_wout", [128, 8, DM], stack=st)
        wq = P.sb("e_wq", [128, 8, 2048], stack=st)
        keysT = P.sb("e_keysT", [128, 16, 128], stack=st)
        lnw = P.sb("e_lnw", [128, DM], stack=st)
        for c in range(8):
            P.dma("sp", wout.ap[:, c, :], d.w_out[l, c * 128:(c + 1) * 128, :], writes=[wout])
            P.dma("act", wq.ap[:, c, :], d.peer_wq[l, c * 128:(c + 1) * 128, :], writes=[wq])
        P.dma("sp", keysT.ap[:], d.peer_keysT[l].rearrange("q k n -> k q n"), writes=[keysT])
        P.dma("sp", lnw.ap[:], d.ln2_w[l:l + 1, :].to_broadcast([128, DM]), writes=[lnw])
        mixs = [P.sb("e_mix%d" % i, [128, 8, 128], stack=st) for i in range(2)]
        xts = [P.sb("e_x%d" % i, [128, DM], stack=st) for i in range(2)]
        x2 = P.sb("e_x2", [128, DM], stack=st)
        xn = P.sb("e_xn", [128, DM], stack=st)
        junk = P.sb("e_junk", [128, DM], stack=st)
        ssq = P.sb("e_ssq", [128, 1], stack=st)
        xnT = P.sb("e_xnT", [128, 8, 128], stack=st)
        qT = P.sb("e_qT", [128, 16, 128], stack=st)
        sc = P.sb("e_sc", [128, 2048], stack=st)
        tmp = P.sb("e_tmp", [128, 256], stack=st)
        m01 = P.sb("e_m01", [128, 2, 16], stack=st)
        i01 = P.sb("e_i01", [128, 2, 16], U32, stack=st)
        i01f = P.sb("e_i01f", [128, 2, 16], stack=st)
        cand = P.sb("e_cand", [128, 16, 16], stack=st)
        cidx = P.sb("e_cidx", [128, 16, 16], stack=st)
        fv = P.sb("e_fv", [128, 16], stack=st)
        nmx = P.sb("e_nmx", [128, 1], stack=st)
        idxf = P.sb("e_idxf", [128, 128], stack=st)
        ee = P.sb("e_ee", [128, 8, 16], stack=st)
        esum = P.sb("e_esum", [128, 8], stack=st)
        idxi = P.sb("e_idxi", [128, 128], I32, stack=st)
        gw = P.sb("e_gw", [128, 128], stack=st)
        pX = P.ps("e_pX", [128, 1024], stack=st)
        pT = P.ps("e_pT", [128, 1024], stack=st)
        pS = P.ps("e_pS", [128, 2048], stack=st)
        x2b, xnb, idb, gwb = P.buf("X2"), P.buf("XN"), P.buf("IDX"), P.buf("GW")
        for t in range(NT):
            rows = slice(t * 128, (t + 1) * 128)
            mix = mixs[t % 2]
            xt = xts[t % 2]
            P.dma("sp", mix.ap[:], MXv[:, :, t * 128:(t + 1) * 128], writes=[mix])
            P.dma("sp", xt.ap[:], xsrc[rows, :], writes=[xt])
            for nb in range(2):
                for c in range(8):
                    P.op("pe", lambda: nc.tensor.matmul(pX.ap[:, nb * 512:(nb + 1) * 512], mix.ap[:, c, :], wout.ap[:, c, nb * 512:(nb + 1) * 512],
                                                        start=(c == 0), stop=(c == 7)), reads=[mix, wout], writes=[pX])
            P.op("dve", lambda: nc.vector.tensor_tensor(x2.ap[:], pX.ap[:], xt.ap[:], ALU.add), reads=[pX, xt], writes=[x2])
            P.dma("pool", d.X2[rows, :], x2.ap[:], reads=[x2], writes=[x2b])
            rms_rstd(P, x2, x2.ap[:], junk, ssq, DM)
            P.op("dve", lambda: nc.vector.scalar_tensor_tensor(xn.ap[:], x2.ap[:], ssq.ap[:, 0:1], lnw.ap[:], ALU.mult, ALU.mult),
                 reads=[x2, ssq, lnw], writes=[xn])
            P.dma("pool", d.XN[rows, :], xn.ap[:], reads=[xn], writes=[xnb])
            transpose_blocks(P, C, xn, [xn.ap[:, c * 128:(c + 1) * 128] for c in range(8)], pT,
                             xnT, [xnT.ap[:, c, :] for c in range(8)])
            for q4 in range(4):
                for qq in range(4):
                    qc = q4 * 4 + qq
                    for c in range(8):
                        P.op("pe", lambda: nc.tensor.matmul(pX.ap[:, qq * 128:(qq + 1) * 128], wq.ap[:, c, qc * 128:(qc + 1) * 128], xnT.ap[:, c, :],
                                                            start=(c == 0), stop=(c == 7)), reads=[wq, xnT], writes=[pX])
                if q4 % 2 == 0:
                    P.op("act", lambda: nc.scalar.copy(qT.ap[:, q4 * 4:q4 * 4 + 4, :], pX.ap[:, 0:512].rearrange("p (a b) -> p a b", b=128)),
                         reads=[pX], writes=[qT])
                else:
                    P.op("dve", lambda: nc.vector.tensor_copy(qT.ap[:, q4 * 4:q4 * 4 + 4, :], pX.ap[:, 0:512].rearrange("p (a b) -> p a b", b=128)),
                         reads=[pX], writes=[qT])
            for qc in range(16):
                P.op("pe", lambda: nc.tensor.matmul(pS.ap[:, qc * 128:(qc + 1) * 128], qT.ap[:, qc, :], keysT.ap[:, qc, :], start=True, stop=True),
                     reads=[qT, keysT], writes=[pS])
            for nb in range(4):
                P.op("act", lambda: nc.scalar.copy(sc.ap[:, nb * 512:(nb + 1) * 512], pS.ap[:, nb * 512:(nb + 1) * 512]), reads=[pS], writes=[sc])
            V = nc.vector
            for h in range(8):
                for p in range(2):
                    s_ = sc.ap[:, (h * 2 + p) * 128:(h * 2 + p + 1) * 128]
                    P.op("dve", lambda: V.max(m01.ap[:, p, 0:8], s_), reads=[sc], writes=[m01])
                    P.op("dve", lambda: V.match_replace(tmp.ap[:, 0:128], m01.ap[:, p, 0:8], s_, NEG), reads=[sc, m01], writes=[tmp])
                    P.op("dve", lambda: V.max(m01.ap[:, p, 8:16], tmp.ap[:, 0:128]), reads=[tmp], writes=[m01])
                    P.op("dve", lambda: V.max_index(i01.ap[:, p, 0:8], m01.ap[:, p, 0:8], s_), reads=[sc, m01], writes=[i01])
                    P.op("dve", lambda: V.max_index(i01.ap[:, p, 8:16], m01.ap[:, p, 8:16], s_), reads=[sc, m01], writes=[i01])
                P.op("dve", lambda: V.tensor_scalar(i01f.ap[:, 0, :], i01.ap[:, 0, :], 128.0, None, ALU.mult), reads=[i01], writes=[i01f])
                P.op("dve", lambda: V.tensor_copy(i01f.ap[:, 1, :], i01.ap[:, 1, :]), reads=[i01], writes=[i01f])
                P.op("dve", lambda: V.tensor_tensor(cand.ap[:], bc_last(m01.ap[:, 0, :], 16), bc_mid(m01.ap[:, 1, :], 16), ALU.add),
                     reads=[m01], writes=[cand])
                P.op("dve", lambda: V.tensor_tensor(cidx.ap[:], bc_last(i01f.ap[:, 0, :], 16), bc_mid(i01f.ap[:, 1, :], 16), ALU.add),
                     reads=[i01f], writes=[cidx])
                cf = cand.ap[:].rearrange("p a b -> p (a b)")
                xf = cidx.ap[:].rearrange("p a b -> p (a b)")
                P.op("dve", lambda: V.max(fv.ap[:, 0:8], cf), reads=[cand], writes=[fv])
                P.op("dve", lambda: V.match_replace(tmp.ap[:], fv.ap[:, 0:8], cf, NEG), reads=[cand, fv], writes=[tmp])
                P.op("dve", lambda: V.max(fv.ap[:, 8:16], tmp.ap[:]), reads=[tmp], writes=[fv])
                for k in range(16):
                    P.op("dve", lambda: V.scalar_tensor_tensor(tmp.ap[:], cf, fv.ap[:, k:k + 1], xf, ALU.is_equal, ALU.mult,
                                                               accum_out=idxf.ap[:, h * 16 + k:h * 16 + k + 1]),
                         reads=[cand, cidx, fv], writes=[tmp, idxf])
                P.op("dve", lambda: V.tensor_scalar(nmx.ap[:], fv.ap[:, 0:1], -1.0, None, ALU.mult), reads=[fv], writes=[nmx])
                P.op("act", lambda: nc.scalar.activation(ee.ap[:, h, :], fv.ap[:], AF.Exp, bias=nmx.ap[:, 0:1], accum_out=esum.ap[:, h:h + 1]),
                     reads=[fv, nmx], writes=[ee, esum])
            P.op("dve", lambda: V.reciprocal(esum.ap[:], esum.ap[:]), reads=[esum], writes=[esum])
            P.op("dve", lambda: V.tensor_tensor(gw.ap[:].rearrange("p (h k) -> p h k", k=16), ee.ap[:], bc_last(esum.ap[:], 16), ALU.mult),
                 reads=[ee, esum], writes=[gw])
            P.op("dve", lambda: V.tensor_scalar(idxf.ap[:], idxf.ap[:], 0.0, 16383.0, ALU.max, ALU.min), reads=[idxf], writes=[idxf])
            P.op("dve", lambda: V.tensor_copy(idxi.ap[:], idxf.ap[:]), reads=[idxf], writes=[idxi])
            P.dma("pool", d.IDX[rows, :], idxi.ap[:], reads=[idxi], writes=[idb])
            P.dma("pool", d.GW[rows, :], gw.ap[:], reads=[gw], writes=[gwb])
    P.barrier()


def stage_f(P, C, d, cfg, l, last):
    nc = P.nc
    NT = cfg.NT
    NR = 6
    with ExitStack() as st:
        idxs = [P.sb("f_idx%d" % i, [128, 128], I32, stack=st) for i in range(2)]
        gws = [P.sb("f_gw%d" % i, [128, 128], stack=st) for i in range(2)]
        xns = [P.sb("f_xn%d" % i, [128, DM], stack=st) for i in range(2)]
        ys = [P.sb("f_y%d" % i, [128, DM], stack=st) for i in range(2)]
        ring = [P.sb("f_g%d" % i, [128, DM], stack=st) for i in range(NR)]
        junk = P.sb("f_junk", [128, DM], stack=st)
        hid = P.sb("f_hid", [128, 128], stack=st)
        cc = P.sb("f_cc", [128, 128], stack=st)
        tmp = P.sb("f_tmp", [128, 128], stack=st)
        xrb = P.buf("XR")
        U = d.peer_u[l]
        Vt = d.peer_v[l]
        gi = 0
        for t in range(NT):
            rows = slice(t * 128, (t + 1) * 128)
            idx, gw, xn, y = idxs[t % 2], gws[t % 2], xns[t % 2], ys[t % 2]
            P.dma("sp", idx.ap[:], d.IDX[rows, :], writes=[idx])
            P.dma("sp", gw.ap[:], d.GW[rows, :], writes=[gw])
            P.dma("sp", xn.ap[:], d.XN[rows, :], writes=[xn])
            P.dma("sp", y.ap[:], d.X2[rows, :], writes=[y])
            for k in range(128):
                g = ring[gi % NR]
                gi += 1
                P.dma("pool", None, None, reads=[idx], writes=[g],
                      fn=lambda: nc.gpsimd.indirect_dma_start(out=g.ap[:, :], out_offset=None, in_=U,
                                                              in_offset=bass.IndirectOffsetOnAxis(ap=idx.ap[:, k:k + 1], axis=0)))
                P.op("dve", lambda: nc.vector.scalar_tensor_tensor(junk.ap[:], g.ap[:], 1.0, xn.ap[:], ALU.mult, ALU.mult,
                                                                   accum_out=hid.ap[:, k:k + 1]),
                     reads=[g, xn], writes=[junk, hid])
            gelu_tanh(P, "dve", cc.ap[:], hid.ap[:], tmp.ap[:], hid, cc, tmp)
            P.op("dve", lambda: nc.vector.tensor_tensor(cc.ap[:], cc.ap[:], gw.ap[:], ALU.mult), reads=[cc, gw], writes=[cc])
            for k in range(128):
                g = ring[gi % NR]
                gi += 1
                P.dma("pool", None, None, reads=[idx], writes=[g],
                      fn=lambda: nc.gpsimd.indirect_dma_start(out=g.ap[:, :], out_offset=None, in_=Vt,
                                                              in_offset=bass.IndirectOffsetOnAxis(ap=idx.ap[:, k:k + 1], axis=0)))
                P.op("dve", lambda: nc.vector.scalar_tensor_tensor(y.ap[:], g.ap[:], cc.ap[:, k:k + 1], y.ap[:], ALU.mult, ALU.add),
                     reads=[g, cc, y], writes=[y])
            P.dma("sp", d.XR[rows, :], y.ap[:], reads=[y], writes=[xrb])
    P.barrier()


def stage_final(P, C, d, cfg):
    nc = P.nc
    with ExitStack() as st:
        lnw = P.sb("z_lnw", [128, DM], stack=st)
        P.dma("sp", lnw.ap[:], d.lnf_w[0:1, :].to_broadcast([128, DM]), writes=[lnw])
        xts = [P.sb("z_x%d" % i, [128, DM], stack=st) for i in range(2)]
        os_ = [P.sb("z_o%d" % i, [128, DM], stack=st) for i in range(2)]
        junk = P.sb("z_junk", [128, DM], stack=st)
        ssq = P.sb("z_ssq", [128, 1], stack=st)
        ob = P.buf("out")
        for t in range(cfg.NT):
            rows = slice(t * 128, (t + 1) * 128)
            xt, o = xts[t % 2], os_[t % 2]
            P.dma("sp", xt.ap[:], d.XR[rows, :], writes=[xt])
            rms_rstd(P, xt, xt.ap[:], junk, ssq, DM)
            P.op("dve", lambda: nc.vector.scalar_tensor_tensor(o.ap[:], xt.ap[:], ssq.ap[:, 0:1], lnw.ap[:], ALU.mult, ALU.mult),
                 reads=[xt, ssq, lnw], writes=[o])
            P.dma("pool", d.out[rows, :], o.ap[:], reads=[o], writes=[ob])
    P.barrier()


def build_program(cfg):
    nc = bass.Bass("TRN2", target_bir_lowering=False)
    d = declare_dram(nc, cfg)
    with ExitStack() as st:
        P = Prog(nc, st)
        C = load_consts(P, d, cfg)
        for l in range(cfg.L):
            xsrc = d.x if l == 0 else d.XR
            stage_a(P, C, d, cfg, l, xsrc)
            stage_b(P, C, d, cfg, l)
            stage_c(P, C, d, cfg, l)
            stage_d(P, C, d, cfg, l)
            stage_e(P, C, d, cfg, l, xsrc)
            stage_f(P, C, d, cfg, l, l == cfg.L - 1)
        stage_final(P, C, d, cfg)
        P.finish()
    return nc, P


def kernel(**inputs):
    cfg = Cfg(T=4096, L=4)
    nc, P = build_program(cfg)
    in_maps = [make_in_map(inputs, c, cfg) for c in range(8)]
    res = run_bass_kernel_spmd(nc, in_maps, core_ids=list(range(8)))
    out = np.stack([np.asarray(res.results[c]["out"], dtype=np.float32) for c in range(8)], 0)
    return out
```

```python
import numpy as np
from contextlib import ExitStack
import concourse.bass as bass
import concourse.mybir as mybir
from concourse.bass_utils import run_bass_kernel_spmd

F32 = mybir.dt.float32
BF16 = mybir.dt.bfloat16
U32 = mybir.dt.uint32
I32 = mybir.dt.int32
AF = mybir.ActivationFunctionType
ALU = mybir.AluOpType
AX = mybir.AxisListType


class Buf:
    __slots__ = ("name", "w", "r", "ap")

    def __init__(self, name, ap=None):
        self.name = name
        self.w = None
        self.r = {}
        self.ap = ap


class Prog:
    NDMA = 8
    CAP = 30000

    def __init__(self, nc, stack):
        self.nc = nc
        self.stack = stack
        self.E = {"pe": nc.tensor, "act": nc.scalar, "dve": nc.vector, "pool": nc.gpsimd, "sp": nc.sync}
        self.nsem = 0
        self.sem = {}
        self.cnt = {}
        self.last = {}
        for k in ("pe", "act", "dve", "pool"):
            self.sem[k] = self._newsem("s_" + k)
            self.cnt[k] = 0
            self.last[k] = None
        self.dslot = {}
        self.dcnt = {}
        for q in ("sp", "pool", "act"):
            self.dslot[q] = [{"sem": self._newsem("d_%s%d" % (q, i)), "val": 0, "last": None} for i in range(self.NDMA)]
            self.dcnt[q] = 0
        self.seen = {k: {} for k in self.E}
        self.nwait = 0
        self.nins = 0
        self.uid = 0

    def _newsem(self, name):
        self.nsem += 1
        return self.stack.enter_context(self.nc.semaphore("%s_%d" % (name, self.nsem)))

    def _uid(self):
        self.uid += 1
        return self.uid

    def sb(self, name, shape, dtype=F32, stack=None):
        t = (stack or self.stack).enter_context(self.nc.sbuf_tensor("sb%d_%s" % (self._uid(), name), list(shape), dtype))
        return Buf(name, t)

    def ps(self, name, shape, dtype=F32, stack=None):
        t = (stack or self.stack).enter_context(self.nc.psum_tensor("ps%d_%s" % (self._uid(), name), list(shape), dtype))
        return Buf(name, t)

    def buf(self, name):
        return Buf(name)

    def _wait(self, e, ev):
        sem, val = ev
        if self.seen[e].get(sem.num, 0) >= val:
            return
        self.E[e].wait_ge(sem, val)
        self.seen[e][sem.num] = val
        self.nwait += 1

    def _deps(self, e, reads, writes, skip_sem=None):
        for b in reads:
            if b.w is not None and (skip_sem is None or b.w[0].num != skip_sem):
                self._wait(e, b.w)
        for b in writes:
            if b.w is not None and (skip_sem is None or b.w[0].num != skip_sem):
                self._wait(e, b.w)
            for ev in b.r.values():
                if skip_sem is None or ev[0].num != skip_sem:
                    self._wait(e, ev)

    def _mark(self, ev, reads, writes):
        for b in reads:
            b.r[ev[0].num] = ev
        for b in writes:
            b.w = ev
            b.r = {}

    def op(self, e, fn, reads=(), writes=(), same_engine_sync=True):
        if self.cnt[e] >= self.CAP:
            self.sem[e] = self._newsem("s_" + e)
            self.cnt[e] = 0
        skip = None if (same_engine_sync and e != "pe") else self.sem[e].num
        self._deps(e, reads, writes, skip)
        ins = fn()
        self.cnt[e] += 1
        ins.then_inc(self.sem[e], 1)
        ev = (self.sem[e], self.cnt[e])
        self.last[e] = ev
        self._mark(ev, reads, writes)
        self.nins += 1
        return ins

    def dma(self, q, out, in_, reads=(), writes=(), fn=None):
        slot = self.dslot[q][self.dcnt[q] % self.NDMA]
        if slot["last"] is not None:
            self._wait(q, slot["last"])
        if slot["val"] + 16 > self.CAP:
            slot["sem"] = self._newsem("d_" + q)
            slot["val"] = 0
        self._deps(q, reads, writes)
        if fn is None:
            ins = self.E[q].dma_start(out=out, in_=in_)
        else:
            ins = fn()
        ins.then_inc(slot["sem"], 16)
        slot["val"] += 16
        ev = (slot["sem"], slot["val"])
        slot["last"] = ev
        self.dcnt[q] += 1
        self._mark(ev, reads, writes)
        self.nins += 1
        return ins

    def barrier(self):
        evs = [ev for ev in self.last.values() if ev is not None]
        for q in self.dslot:
            evs += [sl["last"] for sl in self.dslot[q] if sl["last"] is not None]
        for e in self.E:
            for ev in evs:
                self._wait(e, ev)

    def finish(self):
        self.barrier()


DM = 1024
INW = 2304
EPS = 1e-6
NEG = -1.0e30


class Cfg:
    def __init__(self, T=4096, L=4, debug=False):
        self.T = T
        self.L = L
        self.NT = T // 128
        self.debug = debug


def bc_mid(ap, n):
    a = ap.ap
    return bass.AP(ap.tensor, ap.offset, [list(a[0]), [0, n]] + [list(x) for x in a[1:]])


def bc_last(ap, n):
    a = ap.ap
    return bass.AP(ap.tensor, ap.offset, [list(x) for x in a] + [[0, n]])


def host_constants(T):
    f32 = np.float32
    t = np.arange(T, dtype=f32)
    ret_inv = (1.0 / (10000.0 ** np.linspace(0.0, 1.0, 32, dtype=f32))).astype(f32)
    ang = t[:, None] * ret_inv[None, :]
    rc = np.concatenate([np.cos(ang), np.cos(ang)], -1).astype(f32)
    rs = np.concatenate([-np.sin(ang), np.sin(ang)], -1).astype(f32)
    rows = np.floor(t / 64.0).astype(f32)
    cols = (t - rows * 64.0).astype(f32)
    ax_inv = (10000.0 ** (-np.arange(16, dtype=f32) / 16.0)).astype(f32)
    ar = rows[:, None] * ax_inv[None, :]
    ac = cols[:, None] * ax_inv[None, :]
    acos = np.concatenate([np.cos(ar), np.cos(ar), np.cos(ac), np.cos(ac)], -1).astype(f32)
    asin = np.concatenate([-np.sin(ar), np.sin(ar), -np.sin(ac), np.sin(ac)], -1).astype(f32)
    tab = np.stack([rc, rs, rc * f32(0.125), rs * f32(0.125), acos, asin], 1).astype(f32)
    j = np.arange(128, dtype=f32)[:, None]
    i = np.arange(128, dtype=f32)[None, :]
    cm = np.stack([np.eye(128, dtype=f32), np.maximum(i - j, 0) + 0 * j, np.maximum(j - i, 0) + 0 * i,
                   (i >= j).astype(f32), (j > i).astype(f32)], 1).astype(f32)
    rw = np.stack([np.broadcast_to(i + 1.0, (64, 128)), np.broadcast_to(128.0 - i, (64, 128))], 1).astype(f32)
    cw = np.concatenate([127.0 - j, j], 1).astype(f32)
    return {"c_tab": np.ascontiguousarray(tab), "c_mat": np.ascontiguousarray(cm),
            "c_rw": np.ascontiguousarray(rw), "c_cw": np.ascontiguousarray(cw)}


class D:
    pass


def declare_dram(nc, cfg):
    T, L = cfg.T, cfg.L
    d = D()

    def inp(name, shape, dt=F32):
        return nc.dram_tensor(name, list(shape), dt, kind="ExternalInput").ap()

    def scr(name, shape, dt=F32):
        kind = "ExternalOutput" if cfg.debug else "Internal"
        return nc.dram_tensor(name, list(shape), dt, kind=kind).ap()

    d.x = inp("x", [T, DM])
    d.ln1_w = inp("ln1_w", [L, DM])
    d.w_in = inp("w_in", [L, DM, INW])
    d.ret_log_decay = inp("ret_log_decay", [L, 8])
    d.ret_gn_w = inp("ret_gn_w", [L, 256])
    d.lru_conv_w = inp("lru_conv_w", [L, 4, 256])
    d.lru_conv_b = inp("lru_conv_b", [L, 256])
    d.lru_bd = inp("lru_bd", [L, 2, 2, 2, 128, 128])
    d.lru_gate_b = inp("lru_gate_b", [L, 2, 2, 256])
    d.lru_lambda = inp("lru_lambda", [L, 2, 256])
    d.attn_q_norm = inp("attn_q_norm", [L, 64])
    d.attn_k_norm = inp("attn_k_norm", [L, 64])
    d.w_out = inp("w_out", [L, DM, DM])
    d.ln2_w = inp("ln2_w", [L, DM])
    d.peer_wq = inp("peer_wq", [L, DM, 2048])
    d.peer_keysT = inp("peer_keysT", [L, 16, 128, 128])
    d.peer_u = [inp("peer_u%d" % i, [16384, DM]) for i in range(L)]
    d.peer_v = [inp("peer_v%d" % i, [16384, DM]) for i in range(L)]
    d.lnf_w = inp("lnf_w", [1, DM])
    d.c_tab = inp("c_tab", [T, 6, 64])
    d.c_mat = inp("c_mat", [128, 5, 128])
    d.c_rw = inp("c_rw", [64, 2, 128])
    d.c_cw = inp("c_cw", [128, 2])
    d.out = nc.dram_tensor("out", [T, DM], F32, kind="ExternalOutput").ap()
    d.XR = scr("XR", [T, DM])
    d.X2 = scr("X2", [T, DM])
    d.XN = scr("XN", [T, DM])
    d.TM = scr("TM", [T, INW])
    d.FT = scr("FT", [1664, T])
    d.MIXT = scr("MIXT", [DM, T])
    d.IDX = scr("IDX", [T, 128], I32)
    d.GW = scr("GW", [T, 128])
    d.UVB = [scr("UVB%d" % i, [16384, 2048], BF16) for i in range(L)]
    return d


def load_consts(P, d, cfg):
    nc = P.nc
    C = D()
    C.mat = P.sb("c_mat", [128, 5, 128])
    P.dma("sp", C.mat.ap[:], d.c_mat, writes=[C.mat])
    C.ident = C.mat.ap[:, 0, :]
    C.rw = P.sb("c_rw", [64, 2, 128])
    P.dma("sp", C.rw.ap[:], d.c_rw, writes=[C.rw])
    C.cw = P.sb("c_cw", [128, 2])
    P.dma("sp", C.cw.ap[:], d.c_cw, writes=[C.cw])
    return C


def transpose_blocks(P, C, src_buf, src_aps, pT, dst_buf, dst_aps, evac_engs=("act", "dve")):
    nc = P.nc
    n = len(src_aps)
    for g0 in range(0, n, 8):
        g1 = min(n, g0 + 8)
        for k in range(g0, g1):
            s = (k - g0) * 128
            P.op("pe", lambda: nc.tensor.transpose(pT.ap[:, s:s + 128], src_aps[k], C.ident),
                 reads=[src_buf, C.mat], writes=[pT])
        for k in range(g0, g1):
            s = (k - g0) * 128
            e = evac_engs[k % len(evac_engs)]
            if e == "act":
                P.op("act", lambda: nc.scalar.copy(dst_aps[k], pT.ap[:, s:s + 128]), reads=[pT], writes=[dst_buf])
            else:
                P.op(e, lambda: P.E[e].tensor_copy(dst_aps[k], pT.ap[:, s:s + 128]), reads=[pT], writes=[dst_buf])


def rope(P, eng, out_ap, x_ap, cos_ap, ss_ap, H, R, tA, tB, rbufs, wbufs):
    nc = P.nc
    E = P.E[eng]
    hs = 32 // R
    W = H * 64
    x3 = x_ap.rearrange("p (h d) -> p h d", d=64)
    a3 = tA.ap[:, 0:W].rearrange("p (h d) -> p h d", d=64)
    P.op(eng, lambda: E.tensor_tensor(a3, x3, bc_mid(cos_ap, H), ALU.mult), reads=rbufs, writes=[tA])
    x5 = x_ap.rearrange("p (h r two d) -> p h r two d", r=R, two=2, d=hs)
    b5 = tB.ap[:, 0:W].rearrange("p (h r two d) -> p h r two d", r=R, two=2, d=hs)
    s4 = ss_ap.rearrange("p (r two d) -> p r two d", r=R, two=2, d=hs)
    for two in range(2):
        sv = s4[:, :, two, :]
        if R == 1:
            o_ = b5[:, :, 0, two, :]
            i_ = x5[:, :, 0, 1 - two, :]
            s_ = bc_mid(sv[:, 0, :], H)
        else:
            o_ = b5[:, :, :, two, :]
            i_ = x5[:, :, :, 1 - two, :]
            s_ = bc_mid(sv, H)
        P.op(eng, lambda: E.tensor_tensor(o_, i_, s_, ALU.mult), reads=rbufs, writes=[tB])
    P.op(eng, lambda: E.tensor_tensor(out_ap, tA.ap[:, 0:W], tB.ap[:, 0:W], ALU.add), reads=[tA, tB], writes=wbufs)


def rms_rstd(P, x_buf, x_ap, junk, ssq, n):
    nc = P.nc
    P.op("act", lambda: nc.scalar.activation(junk.ap[:, 0:n], x_ap, AF.Square, accum_out=ssq.ap[:, 0:1]),
         reads=[x_buf], writes=[junk, ssq])
    P.op("dve", lambda: nc.vector.tensor_scalar(ssq.ap[:, 0:1], ssq.ap[:, 0:1], 1.0 / n, EPS, ALU.mult, ALU.add),
         reads=[ssq], writes=[ssq])
    P.op("act", lambda: nc.scalar.activation(ssq.ap[:, 0:1], ssq.ap[:, 0:1], AF.Sqrt), reads=[ssq], writes=[ssq])
    P.op("dve", lambda: nc.vector.reciprocal(ssq.ap[:, 0:1], ssq.ap[:, 0:1]), reads=[ssq], writes=[ssq])


def stage_a(P, C, d, cfg, l, xsrc):
    nc = P.nc
    NT = cfg.NT
    with ExitStack() as st:
        win = P.sb("a_win", [128, 8, INW], stack=st)
        for c in range(8):
            P.dma("sp" if c % 2 == 0 else "act", win.ap[:, c, :], d.w_in[l, c * 128:(c + 1) * 128, :], writes=[win])
        lnw = P.sb("a_lnw", [128, DM], stack=st)
        P.dma("sp", lnw.ap[:], d.ln1_w[l:l + 1, :].to_broadcast([128, DM]), writes=[lnw])
        qkw = P.sb("a_qkw", [128, 2, 64], stack=st)
        P.dma("sp", qkw.ap[:, 0, :], d.attn_q_norm[l:l + 1, :].to_broadcast([128, 64]), writes=[qkw])
        P.dma("sp", qkw.ap[:, 1, :], d.attn_k_norm[l:l + 1, :].to_broadcast([128, 64]), writes=[qkw])
        P.op("dve", lambda: nc.vector.tensor_scalar(qkw.ap[:, 0, :], qkw.ap[:, 0, :], 0.125, None, ALU.mult),
             reads=[qkw], writes=[qkw])
        xts = [P.sb("a_x%d" % i, [128, DM], stack=st) for i in range(2)]
        tabs = [P.sb("a_tab%d" % i, [128, 6, 64], stack=st) for i in range(2)]
        h = P.sb("a_h", [128, DM], stack=st)
        junk = P.sb("a_junk", [128, DM], stack=st)
        ssq = P.sb("a_ssq", [128, 1], stack=st)
        hT = P.sb("a_hT", [128, 8, 128], stack=st)
        preps = [P.sb("a_prep%d" % i, [128, INW], stack=st) for i in range(2)]
        tA = P.sb("a_tA", [128, 512], stack=st)
        tB = P.sb("a_tB", [128, 512], stack=st)
        tC = P.sb("a_tC", [128, 640], stack=st)
        s10 = P.sb("a_s10", [128, 10], stack=st)
        tA2 = P.sb("a_tA2", [128, 512], stack=st)
        tB2 = P.sb("a_tB2", [128, 512], stack=st)
        ftts = [P.sb("a_ftt%d" % i, [128, 13, 128], stack=st) for i in range(2)]
        pT = P.ps("a_pT", [128, 1024], stack=st)
        pP = P.ps("a_pP", [128, 2560], stack=st)
        tmb = P.buf("TM")
        ftb = P.buf("FT")
        FTv = d.FT.rearrange("(b p) t -> p b t", p=128)

        for t in range(NT):
            xt = xts[t % 2]
            tab = tabs[t % 2]
            prep = preps[t % 2]
            ftt = ftts[t % 2]
            rows = slice(t * 128, (t + 1) * 128)
            P.dma("sp", xt.ap[:], xsrc[rows, :], writes=[xt])
            P.dma("sp", tab.ap[:], d.c_tab[rows, :, :], writes=[tab])
            rms_rstd(P, xt, xt.ap[:], junk, ssq, DM)
            P.op("dve", lambda: nc.vector.scalar_tensor_tensor(h.ap[:], xt.ap[:], ssq.ap[:, 0:1], lnw.ap[:], ALU.mult, ALU.mult),
                 reads=[xt, ssq, lnw], writes=[h])
            transpose_blocks(P, C, h, [h.ap[:, c * 128:(c + 1) * 128] for c in range(8)], pT,
                             hT, [hT.ap[:, c, :] for c in range(8)])
            for nb in range(5):
                n0, n1 = nb * 512, min(INW, nb * 512 + 512)
                for c in range(8):
                    P.op("pe", lambda: nc.tensor.matmul(pP.ap[:, n0:n1], hT.ap[:, c, :], win.ap[:, c, n0:n1],
                                                        start=(c == 0), stop=(c == 7)), reads=[hT, win], writes=[pP])
            for nb in range(5):
                n0, n1 = nb * 512, min(INW, nb * 512 + 512)
                if nb % 2 == 0:
                    P.op("act", lambda: nc.scalar.copy(prep.ap[:, n0:n1], pP.ap[:, n0:n1]), reads=[pP], writes=[prep])
                else:
                    P.op("dve", lambda: nc.vector.tensor_copy(prep.ap[:, n0:n1], pP.ap[:, n0:n1]), reads=[pP], writes=[prep])
            rope(P, "pool", prep.ap[:, 0:256], prep.ap[:, 0:256], tab.ap[:, 0, :], tab.ap[:, 1, :], 4, 1, tA, tB,
                 [prep, tab], [prep])
            rope(P, "pool", prep.ap[:, 256:512], prep.ap[:, 256:512], tab.ap[:, 2, :], tab.ap[:, 3, :], 4, 1, tA, tB,
                 [prep, tab], [prep])
            qk = prep.ap[:, 1536:2176]
            P.op("dve", lambda: nc.vector.tensor_tensor(tC.ap[:], qk, qk, ALU.mult), reads=[prep], writes=[tC])
            P.op("dve", lambda: nc.vector.tensor_reduce(s10.ap[:], tC.ap[:].rearrange("p (h d) -> p h d", d=64), AX.X, ALU.add),
                 reads=[tC], writes=[s10])
            P.op("dve", lambda: nc.vector.tensor_scalar(s10.ap[:], s10.ap[:], 1.0 / 64, EPS, ALU.mult, ALU.add),
                 reads=[s10], writes=[s10])
            P.op("act", lambda: nc.scalar.activation(s10.ap[:], s10.ap[:], AF.Sqrt), reads=[s10], writes=[s10])
            P.op("dve", lambda: nc.vector.reciprocal(s10.ap[:], s10.ap[:]), reads=[s10], writes=[s10])
            P.op("dve", lambda: nc.vector.tensor_tensor(tC.ap[:].rearrange("p (h d) -> p h d", d=64),
                                                        qk.rearrange("p (h d) -> p h d", d=64),
                                                        bc_last(s10.ap[:], 64), ALU.mult), reads=[prep, s10], writes=[tC])
            P.op("dve", lambda: nc.vector.tensor_tensor(tC.ap[:, 0:512].rearrange("p (h d) -> p h d", d=64),
                                                        tC.ap[:, 0:512].rearrange("p (h d) -> p h d", d=64),
                                                        bc_mid(qkw.ap[:, 0, :], 8), ALU.mult), reads=[tC, qkw], writes=[tC])
            P.op("dve", lambda: nc.vector.tensor_tensor(tC.ap[:, 512:640].rearrange("p (h d) -> p h d", d=64),
                                                        tC.ap[:, 512:640].rearrange("p (h d) -> p h d", d=64),
                                                        bc_mid(qkw.ap[:, 1, :], 2), ALU.mult), reads=[tC, qkw], writes=[tC])
            rope(P, "dve", prep.ap[:, 1536:2048], tC.ap[:, 0:512], tab.ap[:, 4, :], tab.ap[:, 5, :], 8, 2, tA2, tB2,
                 [tC, tab], [prep])
            rope(P, "dve", prep.ap[:, 2048:2176], tC.ap[:, 512:640], tab.ap[:, 4, :], tab.ap[:, 5, :], 2, 2, tA2, tB2,
                 [tC, tab], [prep])
            P.dma("pool", d.TM[rows, :], prep.ap[:], reads=[prep], writes=[tmb])
            cols = [k * 128 for k in range(4)] + [1024 + k * 128 for k in range(9)]
            transpose_blocks(P, C, prep, [prep.ap[:, c0:c0 + 128] for c0 in cols], pT,
                             ftt, [ftt.ap[:, k, :] for k in range(13)])
            P.dma("pool", FTv[:, :, t * 128:(t + 1) * 128], ftt.ap[:], reads=[ftt], writes=[ftb])
    P.barrier()
    return


def make_in_map(inputs, core, cfg):
    T, L = cfg.T, cfg.L
    f = lambda a: np.ascontiguousarray(np.asarray(a, dtype=np.float32))
    m = {}
    m["x"] = f(inputs["x"][core, :T])
    for k in ("ln1_w", "w_in", "ret_gn_w", "lru_conv_w", "lru_conv_b", "lru_gate_b", "lru_lambda",
              "attn_q_norm", "attn_k_norm", "w_out", "ln2_w", "peer_wq"):
        m[k] = f(inputs[k][:L])
    for i in range(L):
        m["peer_u%d" % i] = f(inputs["peer_u"][i])
        m["peer_v%d" % i] = f(inputs["peer_v"][i])
    m["ret_log_decay"] = f(np.asarray(inputs["ret_log_decay"])[:L].reshape(L, 8))
    gw = np.asarray(inputs["lru_gate_w"], dtype=np.float32)[:L]
    bd = np.zeros((L, 2, 2, 2, 128, 128), np.float32)
    for g in range(2):
        for b in range(2):
            bd[:, :, :, g, b * 64:(b + 1) * 64, b * 64:(b + 1) * 64] = gw[:, :, :, g * 2 + b]
    m["lru_bd"] = bd
    keys = np.asarray(inputs["peer_keys"], dtype=np.float32)[:L]
    m["peer_keysT"] = np.ascontiguousarray(keys.reshape(L, 16, 128, 128).transpose(0, 1, 3, 2))
    m["lnf_w"] = f(np.asarray(inputs["lnf_w"]).reshape(1, DM))
    m.update(host_constants(T))
    return m


def gelu_tanh(P, eng, out_ap, x_ap, tmp_ap, xb, ob, tb):
    nc = P.nc
    E = P.E[eng]
    P.op(eng, lambda: E.tensor_tensor(tmp_ap, x_ap, x_ap, ALU.mult), reads=[xb], writes=[tb])
    P.op(eng, lambda: E.tensor_scalar(tmp_ap, tmp_ap, 0.044715, 1.0, ALU.mult, ALU.add), reads=[tb], writes=[tb])
    P.op(eng, lambda: E.tensor_tensor(tmp_ap, tmp_ap, x_ap, ALU.mult), reads=[tb, xb], writes=[tb])
    P.op("act", lambda: nc.scalar.activation(tmp_ap, tmp_ap, AF.Sigmoid, scale=1.5957691216057308), reads=[tb], writes=[tb])
    P.op(eng, lambda: E.tensor_tensor(out_ap, x_ap, tmp_ap, ALU.mult), reads=[tb, xb], writes=[ob])


def stage_b(P, C, d, cfg, l):
    nc = P.nc
    NT = cfg.NT
    MX = d.MIXT.rearrange("(b p) t -> p b t", p=128)
    with ExitStack() as st:
        lg = P.sb("b_lg", [128, 8], stack=st)
        P.dma("sp", lg.ap[:], d.ret_log_decay[l:l + 1, :].to_broadcast([128, 8]), writes=[lg])
        gnw = P.sb("b_gnw", [128, 256], stack=st)
        P.dma("sp", gnw.ap[:], d.ret_gn_w[l:l + 1, :].to_broadcast([128, 256]), writes=[gnw])
        dm = P.sb("b_dm", [128, 4, 128], stack=st)
        t1 = P.sb("b_t1", [128, 128], stack=st)
        t2 = P.sb("b_t2", [128, 128], stack=st)
        qw = P.sb("b_qw", [64, 8, 128], stack=st)
        kw = P.sb("b_kw", [128, 8], stack=st)
        dec = P.sb("b_dec", [64, 8], stack=st)
        for h in range(4):
            P.op("act", lambda: nc.scalar.activation(t1.ap[:], C.mat.ap[:, 1, :], AF.Exp, scale=lg.ap[:, h:h + 1]),
                 reads=[C.mat, lg], writes=[t1])
            P.op("dve", lambda: nc.vector.tensor_tensor(t1.ap[:], t1.ap[:], C.mat.ap[:, 3, :], ALU.mult), reads=[t1, C.mat], writes=[t1])
            P.op("act", lambda: nc.scalar.activation(t2.ap[:], C.mat.ap[:, 2, :], AF.Exp, scale=lg.ap[:, 4 + h:5 + h]),
                 reads=[C.mat, lg], writes=[t2])
            P.op("dve", lambda: nc.vector.tensor_tensor(t2.ap[:], t2.ap[:], C.mat.ap[:, 4, :], ALU.mult), reads=[t2, C.mat], writes=[t2])
            P.op("dve", lambda: nc.vector.tensor_tensor(dm.ap[:, h, :], t1.ap[:], t2.ap[:], ALU.add), reads=[t1, t2], writes=[dm])
            for dr in range(2):
                k = dr * 4 + h
                P.op("act", lambda: nc.scalar.activation(qw.ap[:, k, :], C.rw.ap[:, dr, :], AF.Exp, scale=lg.ap[0:64, k:k + 1]),
                     reads=[C.rw, lg], writes=[qw])
                P.op("act", lambda: nc.scalar.activation(kw.ap[:, k:k + 1], C.cw.ap[:, dr:dr + 1], AF.Exp, scale=lg.ap[:, k:k + 1]),
                     reads=[C.cw, lg], writes=[kw])
        P.op("act", lambda: nc.scalar.activation(dec.ap[:], lg.ap[0:64, :], AF.Exp, scale=128.0), reads=[lg], writes=[dec])

        KV = P.sb("b_KV", [64, NT, 8, 64], stack=st)
        ST = P.sb("b_ST", [64, NT, 8, 64], stack=st)
        kvs = [P.sb("b_kv%d" % i, [128, 512], stack=st) for i in range(2)]
        kwk = P.sb("b_kwk", [128, 8, 64], stack=st)
        pKV = P.ps("b_pKV", [64, 8, 64], stack=st)
        for c in range(NT):
            rows = slice(c * 128, (c + 1) * 128)
            kv = kvs[c % 2]
            P.dma("sp", kv.ap[:], d.TM[rows, 256:768], writes=[kv])
            for k in range(8):
                h = k % 4
                e = "dve" if k % 2 == 0 else "pool"
                P.op(e, lambda: P.E[e].tensor_scalar(kwk.ap[:, k, :], kv.ap[:, h * 64:(h + 1) * 64], kw.ap[:, k:k + 1], None, ALU.mult),
                     reads=[kv, kw], writes=[kwk])
            for k in range(8):
                h = k % 4
                P.op("pe", lambda: nc.tensor.matmul(pKV.ap[:, k, :], kwk.ap[:, k, :], kv.ap[:, 256 + h * 64:256 + (h + 1) * 64],
                                                    start=True, stop=True), reads=[kwk, kv], writes=[pKV])
            P.op("act", lambda: nc.scalar.copy(KV.ap[:, c, :, :], pKV.ap[:]), reads=[pKV], writes=[KV])
        P.op("dve", lambda: nc.vector.memset(ST.ap[:, 0, 0:4, :], 0.0), writes=[ST])
        P.op("dve", lambda: nc.vector.memset(ST.ap[:, NT - 1, 4:8, :], 0.0), writes=[ST])
        for c in range(1, NT):
            for h in range(4):
                P.op("dve", lambda: nc.vector.scalar_tensor_tensor(ST.ap[:, c, h, :], ST.ap[:, c - 1, h, :], dec.ap[:, h:h + 1],
                                                                   KV.ap[:, c - 1, h, :], ALU.mult, ALU.add),
                     reads=[ST, KV, dec], writes=[ST])
            cb = NT - 1 - c
            for h in range(4, 8):
                P.op("dve", lambda: nc.vector.scalar_tensor_tensor(ST.ap[:, cb, h, :], ST.ap[:, cb + 1, h, :], dec.ap[:, h:h + 1],
                                                                   KV.ap[:, cb + 1, h, :], ALU.mult, ALU.add),
                     reads=[ST, KV, dec], writes=[ST])
        qks = [P.sb("b_qk%d" % i, [64, 8, 128], stack=st) for i in range(2)]
        vgs = [P.sb("b_vg%d" % i, [128, 512], stack=st) for i in range(2)]
        qwq = P.sb("b_qwq", [64, 8, 128], stack=st)
        SM = P.sb("b_SM", [128, 4, 128], stack=st)
        o = P.sb("b_o", [128, 256], stack=st)
        sq = P.sb("b_sq", [128, 256], stack=st)
        s4 = P.sb("b_s4", [128, 4], stack=st)
        sg = P.sb("b_sg", [128, 256], stack=st)
        mts = [P.sb("b_mt%d" % i, [128, 2, 128], stack=st) for i in range(2)]
        pS = P.ps("b_pS", [128, 4, 128], stack=st)
        pO = P.ps("b_pO", [128, 4, 64], stack=st)
        pT = P.ps("b_pT", [128, 1024], stack=st)
        mxb = P.buf("MIXT")
        FTq = d.FT[0:512, :].rearrange("(a dd) t -> dd a t", dd=64)
        for c in range(NT):
            rows = slice(c * 128, (c + 1) * 128)
            qk = qks[c % 2]
            vg = vgs[c % 2]
            mt = mts[c % 2]
            P.dma("sp", qk.ap[:], FTq[:, :, c * 128:(c + 1) * 128], writes=[qk])
            P.dma("sp", vg.ap[:], d.TM[rows, 512:1024], writes=[vg])
            for dr in range(2):
                P.op("pool", lambda: nc.gpsimd.tensor_tensor(qwq.ap[:, dr * 4:dr * 4 + 4, :], qk.ap[:, 0:4, :], qw.ap[:, dr * 4:dr * 4 + 4, :], ALU.mult),
                     reads=[qk, qw], writes=[qwq])
            for h in range(4):
                P.op("pe", lambda: nc.tensor.matmul(pS.ap[:, h, :], qk.ap[:, 4 + h, :], qk.ap[:, h, :], start=True, stop=True),
                     reads=[qk], writes=[pS])
            P.op("dve", lambda: nc.vector.tensor_tensor(SM.ap[:], pS.ap[:], dm.ap[:], ALU.mult), reads=[pS, dm], writes=[SM])
            for h in range(4):
                P.op("pe", lambda: nc.tensor.matmul(pO.ap[:, h, :], SM.ap[:, h, :], vg.ap[:, h * 64:(h + 1) * 64], start=True, stop=False),
                     reads=[SM, vg], writes=[pO])
                P.op("pe", lambda: nc.tensor.matmul(pO.ap[:, h, :], qwq.ap[:, h, :], ST.ap[:, c, h, :], start=False, stop=False),
                     reads=[qwq, ST], writes=[pO])
                P.op("pe", lambda: nc.tensor.matmul(pO.ap[:, h, :], qwq.ap[:, 4 + h, :], ST.ap[:, c, 4 + h, :], start=False, stop=True),
                     reads=[qwq, ST], writes=[pO])
            P.op("act", lambda: nc.scalar.copy(o.ap[:], pO.ap[:].rearrange("p h e -> p (h e)")), reads=[pO], writes=[o])
            P.op("dve", lambda: nc.vector.tensor_tensor(sq.ap[:], o.ap[:], o.ap[:], ALU.mult), reads=[o], writes=[sq])
            P.op("dve", lambda: nc.vector.tensor_reduce(s4.ap[:], sq.ap[:].rearrange("p (h e) -> p h e", e=64), AX.X, ALU.add),
                 reads=[sq], writes=[s4])
            P.op("dve", lambda: nc.vector.tensor_scalar(s4.ap[:], s4.ap[:], 1.0 / 64, EPS, ALU.mult, ALU.add), reads=[s4], writes=[s4])
            P.op("act", lambda: nc.scalar.activation(s4.ap[:], s4.ap[:], AF.Sqrt), reads=[s4], writes=[s4])
            P.op("dve", lambda: nc.vector.reciprocal(s4.ap[:], s4.ap[:]), reads=[s4], writes=[s4])
            P.op("dve", lambda: nc.vector.tensor_tensor(o.ap[:].rearrange("p (h e) -> p h e", e=64),
                                                        o.ap[:].rearrange("p (h e) -> p h e", e=64), bc_last(s4.ap[:], 64), ALU.mult),
                 reads=[o, s4], writes=[o])
            P.op("dve", lambda: nc.vector.tensor_tensor(o.ap[:], o.ap[:], gnw.ap[:], ALU.mult), reads=[o, gnw], writes=[o])
            P.op("act", lambda: nc.scalar.activation(sg.ap[:], vg.ap[:, 256:512], AF.Sigmoid), reads=[vg], writes=[sg])
            P.op("pool", lambda: nc.gpsimd.tensor_tensor(sg.ap[:], sg.ap[:], vg.ap[:, 256:512], ALU.mult), reads=[sg, vg], writes=[sg])
            P.op("dve", lambda: nc.vector.tensor_tensor(o.ap[:], o.ap[:], sg.ap[:], ALU.mult), reads=[o, sg], writes=[o])
            transpose_blocks(P, C, o, [o.ap[:, 0:128], o.ap[:, 128:256]], pT, mt, [mt.ap[:, 0, :], mt.ap[:, 1, :]])
            P.dma("pool", MX[:, 0:2, c * 128:(c + 1) * 128], mt.ap[:], reads=[mt], writes=[mxb])
    P.barrier()


def stage_c(P, C, d, cfg, l):
    nc = P.nc
    T = cfg.T
    with ExitStack() as st:
        xp = P.sb("c_xp", [128, T + 3], stack=st)
        xc = P.sb("c_xc", [128, T], stack=st)
        rr = P.sb("c_rr", [128, T], stack=st)
        ii = P.sb("c_ii", [128, T], stack=st)
        aa = P.sb("c_aa", [128, T], stack=st)
        bb = P.sb("c_bb", [128, T], stack=st)
        hf = P.sb("c_hf", [128, T], stack=st)
        gg = P.sb("c_gg", [128, T], stack=st)
        cwt = P.sb("c_cwt", [128, 5], stack=st)
        gb = P.sb("c_gb", [128, 4], stack=st)
        lam = P.sb("c_lam", [128, 2], stack=st)
        nsp = P.sb("c_nsp", [128, 4], stack=st)
        bd = P.sb("c_bd", [128, 4, 128], stack=st)
        pG = [P.ps("c_pG%d" % i, [128, 512], stack=st) for i in range(2)]
        mxb = P.buf("MIXT")
        col = lambda ap1d: ap1d.rearrange("(c o) -> c o", o=1)
        for g in range(2):
            ch = slice(g * 128, (g + 1) * 128)
            for j in range(4):
                P.dma("sp", cwt.ap[:, j:j + 1], col(d.lru_conv_w[l, j, ch]), writes=[cwt])
            P.dma("sp", cwt.ap[:, 4:5], col(d.lru_conv_b[l, ch]), writes=[cwt])
            for dr in range(2):
                P.dma("sp", lam.ap[:, dr:dr + 1], col(d.lru_lambda[l, dr, ch]), writes=[lam])
                for gt in range(2):
                    P.dma("sp", gb.ap[:, dr * 2 + gt:dr * 2 + gt + 1], col(d.lru_gate_b[l, dr, gt, ch]), writes=[gb])
                    P.dma("sp", bd.ap[:, dr * 2 + gt, :], d.lru_bd[l, dr, gt, g, :, :], writes=[bd])
            P.op("dve", lambda: nc.vector.memset(xp.ap[:, 0:2], 0.0), writes=[xp])
            P.op("dve", lambda: nc.vector.memset(xp.ap[:, T + 2:T + 3], 0.0), writes=[xp])
            P.dma("sp", xp.ap[:, 2:T + 2], d.FT[512 + g * 128:512 + (g + 1) * 128, :], writes=[xp])
            P.dma("act", gg.ap[:], d.FT[768 + g * 128:768 + (g + 1) * 128, :], writes=[gg])
            P.op("act", lambda: nc.scalar.activation(nsp.ap[:, 0:2], lam.ap[:], AF.Exp, scale=-1.0), reads=[lam], writes=[nsp])
            P.op("act", lambda: nc.scalar.activation(nsp.ap[:, 0:2], nsp.ap[:, 0:2], AF.Ln, bias=1.0), reads=[nsp], writes=[nsp])
            P.op("dve", lambda: nc.vector.tensor_scalar(nsp.ap[:, 2:4], nsp.ap[:, 0:2], -16.0, None, ALU.mult), reads=[nsp], writes=[nsp])
            P.op("dve", lambda: nc.vector.tensor_scalar(nsp.ap[:, 0:2], nsp.ap[:, 0:2], -8.0, None, ALU.mult), reads=[nsp], writes=[nsp])
            P.op("dve", lambda: nc.vector.tensor_scalar(xc.ap[:], xp.ap[:, 0:T], cwt.ap[:, 0:1], cwt.ap[:, 4:5], ALU.mult, ALU.add),
                 reads=[xp, cwt], writes=[xc])
            for j in range(1, 4):
                P.op("dve", lambda: nc.vector.scalar_tensor_tensor(xc.ap[:], xp.ap[:, j:j + T], cwt.ap[:, j:j + 1], xc.ap[:], ALU.mult, ALU.add),
                     reads=[xp, cwt, xc], writes=[xc])
            for dr in range(2):
                for gt in range(2):
                    dst = rr if gt == 0 else ii
                    for s in range(0, T, 512):
                        pg = pG[(s // 512) % 2]
                        P.op("pe", lambda: nc.tensor.matmul(pg.ap[:], bd.ap[:, dr * 2 + gt, :], xc.ap[:, s:s + 512], start=True, stop=True),
                             reads=[bd, xc], writes=[pg])
                        P.op("act", lambda: nc.scalar.activation(dst.ap[:, s:s + 512], pg.ap[:], AF.Sigmoid,
                                                                 bias=gb.ap[:, dr * 2 + gt:dr * 2 + gt + 1]),
                             reads=[pg, gb], writes=[dst])
                P.op("act", lambda: nc.scalar.activation(aa.ap[:], rr.ap[:], AF.Exp, scale=nsp.ap[:, dr:dr + 1]), reads=[rr, nsp], writes=[aa])
                P.op("act", lambda: nc.scalar.activation(rr.ap[:], rr.ap[:], AF.Exp, scale=nsp.ap[:, 2 + dr:3 + dr]), reads=[rr, nsp], writes=[rr])
                P.op("dve", lambda: nc.vector.tensor_scalar(rr.ap[:], rr.ap[:], -1.0, 1.0, ALU.mult, ALU.add), reads=[rr], writes=[rr])
                P.op("act", lambda: nc.scalar.activation(rr.ap[:], rr.ap[:], AF.Sqrt), reads=[rr], writes=[rr])
                P.op("dve", lambda: nc.vector.tensor_tensor(bb.ap[:], rr.ap[:], ii.ap[:], ALU.mult), reads=[rr, ii], writes=[bb])
                P.op("pool", lambda: nc.gpsimd.tensor_tensor(bb.ap[:], bb.ap[:], xc.ap[:], ALU.mult), reads=[bb, xc], writes=[bb])
                if dr == 0:
                    P.op("dve", lambda: nc.vector.tensor_tensor_scan(hf.ap[:], aa.ap[:], bb.ap[:], 0.0, ALU.mult, ALU.add),
                         reads=[aa, bb], writes=[hf])
                else:
                    P.op("dve", lambda: nc.vector.tensor_tensor_scan(ii.ap[:, ::-1], aa.ap[:, ::-1], bb.ap[:, ::-1], 0.0, ALU.mult, ALU.add),
                         reads=[aa, bb], writes=[ii])
                    P.op("dve", lambda: nc.vector.tensor_tensor(hf.ap[:], hf.ap[:], ii.ap[:], ALU.add), reads=[hf, ii], writes=[hf])
            gelu_tanh(P, "pool", gg.ap[:], gg.ap[:], aa.ap[:], gg, gg, aa)
            P.op("dve", lambda: nc.vector.tensor_tensor(hf.ap[:], hf.ap[:], gg.ap[:], ALU.mult), reads=[hf, gg], writes=[hf])
            P.dma("sp", d.MIXT[256 + g * 128:256 + (g + 1) * 128, :], hf.ap[:], reads=[hf], writes=[mxb])
    P.barrier()


ATT_DT = BF16


def stage_d(P, C, d, cfg, l):
    nc = P.nc
    T, NT = cfg.T, cfg.NT
    NQG = T // 512
    MX = d.MIXT.rearrange("(b p) t -> p b t", p=128)
    with ExitStack() as st:
        kT = P.sb("d_kT", [64, 2, T], ATT_DT, stack=st)
        vaug = P.sb("d_vaug", [128, NT, 2, 65], ATT_DT, stack=st)
        stg = P.sb("d_stg", [64, 8, 512], stack=st)
        vst = P.sb("d_vst", [128, NT, 128], stack=st)
        FTk = d.FT[1536:1664, :].rearrange("(a dd) t -> dd a t", dd=64)
        for s in range(0, T, 512):
            P.dma("sp", stg.ap[:, 0:2, :], FTk[:, :, s:s + 512], writes=[stg])
            P.op("dve", lambda: nc.vector.tensor_copy(kT.ap[:, :, s:s + 512], stg.ap[:, 0:2, :]), reads=[stg], writes=[kT])
        P.dma("sp", vst.ap[:], d.TM[:, 2176:2304].rearrange("(n p) c -> p n c", p=128), writes=[vst])
        P.op("dve", lambda: nc.vector.memset(vaug.ap[:, :, :, 64:65], 1.0), writes=[vaug])
        P.op("dve", lambda: nc.vector.tensor_copy(vaug.ap[:, :, :, 0:64], vst.ap[:].rearrange("p n (k e) -> p n k e", e=64)),
             reads=[vst], writes=[vaug])
        qTb = P.sb("d_qTb", [64, 8, 512], ATT_DT, stack=st)
        pts = [P.sb("d_pt%d" % i, [128, 512], ATT_DT, stack=st) for i in range(2)]
        oat = P.sb("d_oat", [128, 4, 512], stack=st)
        rs = P.sb("d_rs", [128, 4], stack=st)
        mt = P.sb("d_mt", [128, 4, 512], stack=st)
        pS = [P.ps("d_pS%d" % i, [128, 512], stack=st) for i in range(2)]
        pO = [P.ps("d_pO%d" % i, [128, 512], stack=st) for i in range(4)]
        pT = P.ps("d_pT", [128, 1024], stack=st)
        mxb = P.buf("MIXT")
        FTq = d.FT[1024:1536, :].rearrange("(a dd) t -> dd a t", dd=64)
        for qg in range(NQG):
            P.dma("sp", stg.ap[:], FTq[:, :, qg * 512:(qg + 1) * 512], writes=[stg])
            P.op("dve", lambda: nc.vector.tensor_copy(qTb.ap[:], stg.ap[:]), reads=[stg], writes=[qTb])
            for h in range(8):
                kvh = h // 4
                for jt in range(NT):
                    ps = pS[jt % 2]
                    pt = pts[jt % 2]
                    P.op("pe", lambda: nc.tensor.matmul(ps.ap[:], kT.ap[:, kvh, jt * 128:(jt + 1) * 128], qTb.ap[:, h, :], start=True, stop=True),
                         reads=[kT, qTb], writes=[ps])
                    P.op("act", lambda: nc.scalar.activation(pt.ap[:], ps.ap[:], AF.Exp), reads=[ps], writes=[pt])
                    for b in range(4):
                        P.op("pe", lambda: nc.tensor.matmul(pO[b].ap[:, 0:65], pt.ap[:, b * 128:(b + 1) * 128], vaug.ap[:, jt, kvh, :],
                                                            start=(jt == 0), stop=(jt == NT - 1)), reads=[pt, vaug], writes=[pO[b]])
                for b in range(4):
                    P.op("dve", lambda: nc.vector.reciprocal(rs.ap[:, b:b + 1], pO[b].ap[:, 64:65]), reads=[pO[b]], writes=[rs])
                    P.op("dve", lambda: nc.vector.tensor_scalar(oat.ap[:, b, h * 64:(h + 1) * 64], pO[b].ap[:, 0:64], rs.ap[:, b:b + 1], None, ALU.mult),
                         reads=[pO[b], rs], writes=[oat])
            srcs = [oat.ap[:, b, fb * 128:(fb + 1) * 128] for fb in range(4) for b in range(4)]
            dsts = [mt.ap[:, fb, b * 128:(b + 1) * 128] for fb in range(4) for b in range(4)]
            transpose_blocks(P, C, oat, srcs, pT, mt, dsts)
            P.dma("pool", MX[:, 4:8, qg * 512:(qg + 1) * 512], mt.ap[:], reads=[mt], writes=[mxb])
    P.barrier()


def stage_e(P, C, d, cfg, l, xsrc):
    nc = P.nc
    NT = cfg.NT
    MXv = d.MIXT.rearrange("(c p) t -> p c t", p=128)
    with ExitStack() as st:
        wout = P.sb("e_wout", [128, 8, DM], stack=st)
        wq = P.sb("e_wq", [128, 8, 2048], stack=st)
        keysT = P.sb("e_keysT", [128, 16, 128], stack=st)
        lnw = P.sb("e_lnw", [128, DM], stack=st)
        for c in range(8):
            P.dma("sp", wout.ap[:, c, :], d.w_out[l, c * 128:(c + 1) * 128, :], writes=[wout])
            P.dma("act", wq.ap[:, c, :], d.peer_wq[l, c * 128:(c + 1) * 128, :], writes=[wq])
        P.dma("sp", keysT.ap[:], d.peer_keysT[l].rearrange("q k n -> k q n"), writes=[keysT])
        P.dma("sp", lnw.ap[:], d.ln2_w[l:l + 1, :].to_broadcast([128, DM]), writes=[lnw])
        mixs = [P.sb("e_mix%d" % i, [128, 8, 128], stack=st) for i in range(2)]
        xts = [P.sb("e_x%d" % i, [128, DM], stack=st) for i in range(2)]
        x2 = P.sb("e_x2", [128, DM], stack=st)
        xn = P.sb("e_xn", [128, DM], stack=st)
        junk = P.sb("e_junk", [128, DM], stack=st)
        ssq = P.sb("e_ssq", [128, 1], stack=st)
        xnT = P.sb("e_xnT", [128, 8, 128], stack=st)
        qT = P.sb("e_qT", [128, 16, 128], stack=st)
        sc = P.sb("e_sc", [128, 2048], stack=st)
        tmp = P.sb("e_tmp", [128, 256], stack=st)
        m01 = P.sb("e_m01", [128, 2, 16], stack=st)
        i01 = P.sb("e_i01", [128, 2, 16], U32, stack=st)
        i01f = P.sb("e_i01f", [128, 2, 16], stack=st)
        cand = P.sb("e_cand", [128, 16, 16], stack=st)
        cidx = P.sb("e_cidx", [128, 16, 16], stack=st)
        fv = P.sb("e_fv", [128, 16], stack=st)
        nmx = P.sb("e_nmx", [128, 1], stack=st)
        idxf = P.sb("e_idxf", [128, 128], stack=st)
        ee = P.sb("e_ee", [128, 8, 16], stack=st)
        esum = P.sb("e_esum", [128, 8], stack=st)
        idxi = P.sb("e_idxi", [128, 128], I32, stack=st)
        gw = P.sb("e_gw", [128, 128], stack=st)
        pX = P.ps("e_pX", [128, 1024], stack=st)
        pT = P.ps("e_pT", [128, 1024], stack=st)
        pS = P.ps("e_pS", [128, 2048], stack=st)
        x2b, xnb, idb, gwb = P.buf("X2"), P.buf("XN"), P.buf("IDX"), P.buf("GW")
        for t in range(NT):
            rows = slice(t * 128, (t + 1) * 128)
            mix = mixs[t % 2]
            xt = xts[t % 2]
            P.dma("sp", mix.ap[:], MXv[:, :, t * 128:(t + 1) * 128], writes=[mix])
            P.dma("sp", xt.ap[:], xsrc[rows, :], writes=[xt])
            for nb in range(2):
                for c in range(8):
                    P.op("pe", lambda: nc.tensor.matmul(pX.ap[:, nb * 512:(nb + 1) * 512], mix.ap[:, c, :], wout.ap[:, c, nb * 512:(nb + 1) * 512],
                                                        start=(c == 0), stop=(c == 7)), reads=[mix, wout], writes=[pX])
            P.op("dve", lambda: nc.vector.tensor_tensor(x2.ap[:], pX.ap[:], xt.ap[:], ALU.add), reads=[pX, xt], writes=[x2])
            P.dma("pool", d.X2[rows, :], x2.ap[:], reads=[x2], writes=[x2b])
            rms_rstd(P, x2, x2.ap[:], junk, ssq, DM)
            P.op("dve", lambda: nc.vector.scalar_tensor_tensor(xn.ap[:], x2.ap[:], ssq.ap[:, 0:1], lnw.ap[:], ALU.mult, ALU.mult),
                 reads=[x2, ssq, lnw], writes=[xn])
            P.dma("pool", d.XN[rows, :], xn.ap[:], reads=[xn], writes=[xnb])
            transpose_blocks(P, C, xn, [xn.ap[:, c * 128:(c + 1) * 128] for c in range(8)], pT,
                             xnT, [xnT.ap[:, c, :] for c in range(8)])
            for q4 in range(4):
                for qq in range(4):
                    qc = q4 * 4 + qq
                    for c in range(8):
                        P.op("pe", lambda: nc.tensor.matmul(pX.ap[:, qq * 128:(qq + 1) * 128], wq.ap[:, c, qc * 128:(qc + 1) * 128], xnT.ap[:, c, :],
                                                            start=(c == 0), stop=(c == 7)), reads=[wq, xnT], writes=[pX])
                if q4 % 2 == 0:
                    P.op("act", lambda: nc.scalar.copy(qT.ap[:, q4 * 4:q4 * 4 + 4, :], pX.ap[:, 0:512].rearrange("p (a b) -> p a b", b=128)),
                         reads=[pX], writes=[qT])
                else:
                    P.op("dve", lambda: nc.vector.tensor_copy(qT.ap[:, q4 * 4:q4 * 4 + 4, :], pX.ap[:, 0:512].rearrange("p (a b) -> p a b", b=128)),
                         reads=[pX], writes=[qT])
            for qc in range(16):
                P.op("pe", lambda: nc.tensor.matmul(pS.ap[:, qc * 128:(qc + 1) * 128], qT.ap[:, qc, :], keysT.ap[:, qc, :], start=True, stop=True),
                     reads=[qT, keysT], writes=[pS])
            for nb in range(4):
                P.op("act", lambda: nc.scalar.copy(sc.ap[:, nb * 512:(nb + 1) * 512], pS.ap[:, nb * 512:(nb + 1) * 512]), reads=[pS], writes=[sc])
            V = nc.vector
            for h in range(8):
                for p in range(2):
                    s_ = sc.ap[:, (h * 2 + p) * 128:(h * 2 + p + 1) * 128]
                    P.op("dve", lambda: V.max(m01.ap[:, p, 0:8], s_), reads=[sc], writes=[m01])
                    P.op("dve", lambda: V.match_replace(tmp.ap[:, 0:128], m01.ap[:, p, 0:8], s_, NEG), reads=[sc, m01], writes=[tmp])
                    P.op("dve", lambda: V.max(m01.ap[:, p, 8:16], tmp.ap[:, 0:128]), reads=[tmp], writes=[m01])
                    P.op("dve", lambda: V.max_index(i01.ap[:, p, 0:8], m01.ap[:, p, 0:8], s_), reads=[sc, m01], writes=[i01])
                    P.op("dve", lambda: V.max_index(i01.ap[:, p, 8:16], m01.ap[:, p, 8:16], s_), reads=[sc, m01], writes=[i01])
                P.op("dve", lambda: V.tensor_scalar(i01f.ap[:, 0, :], i01.ap[:, 0, :], 128.0, None, ALU.mult), reads=[i01], writes=[i01f])
                P.op("dve", lambda: V.tensor_copy(i01f.ap[:, 1, :], i01.ap[:, 1, :]), reads=[i01], writes=[i01f])
                P.op("dve", lambda: V.tensor_tensor(cand.ap[:], bc_last(m01.ap[:, 0, :], 16), bc_mid(m01.ap[:, 1, :], 16), ALU.add),
                     reads=[m01], writes=[cand])
                P.op("dve", lambda: V.tensor_tensor(cidx.ap[:], bc_last(i01f.ap[:, 0, :], 16), bc_mid(i01f.ap[:, 1, :], 16), ALU.add),
                     reads=[i01f], writes=[cidx])
                cf = cand.ap[:].rearrange("p a b -> p (a b)")
                xf = cidx.ap[:].rearrange("p a b -> p (a b)")
                P.op("dve", lambda: V.max(fv.ap[:, 0:8], cf), reads=[cand], writes=[fv])
                P.op("dve", lambda: V.match_replace(tmp.ap[:], fv.ap[:, 0:8], cf, NEG), reads=[cand, fv], writes=[tmp])
                P.op("dve", lambda: V.max(fv.ap[:, 8:16], tmp.ap[:]), reads=[tmp], writes=[fv])
                for k in range(16):
                    P.op("dve", lambda: V.scalar_tensor_tensor(tmp.ap[:], cf, fv.ap[:, k:k + 1], xf, ALU.is_equal, ALU.mult,
                                                               accum_out=idxf.ap[:, h * 16 + k:h * 16 + k + 1]),
                         reads=[cand, cidx, fv], writes=[tmp, idxf])
                P.op("dve", lambda: V.tensor_scalar(nmx.ap[:], fv.ap[:, 0:1], -1.0, None, ALU.mult), reads=[fv], writes=[nmx])
                P.op("act", lambda: nc.scalar.activation(ee.ap[:, h, :], fv.ap[:], AF.Exp, bias=nmx.ap[:, 0:1], accum_out=esum.ap[:, h:h + 1]),
                     reads=[fv, nmx], writes=[ee, esum])
            P.op("dve", lambda: V.reciprocal(esum.ap[:], esum.ap[:]), reads=[esum], writes=[esum])
            P.op("dve", lambda: V.tensor_tensor(gw.ap[:].rearrange("p (h k) -> p h k", k=16), ee.ap[:], bc_last(esum.ap[:], 16), ALU.mult),
                 reads=[ee, esum], writes=[gw])
            P.op("dve", lambda: V.tensor_scalar(idxf.ap[:], idxf.ap[:], 0.0, 16383.0, ALU.max, ALU.min), reads=[idxf], writes=[idxf])
            P.op("dve", lambda: V.tensor_copy(idxi.ap[:], idxf.ap[:]), reads=[idxf], writes=[idxi])
            P.dma("pool", d.IDX[rows, :], idxi.ap[:], reads=[idxi], writes=[idb])
            P.dma("pool", d.GW[rows, :], gw.ap[:], reads=[gw], writes=[gwb])
    P.barrier()


def stage_f(P, C, d, cfg, l, last):
    nc = P.nc
    NT = cfg.NT
    NR = 6
    with ExitStack() as st:
        idxs = [P.sb("f_idx%d" % i, [128, 128], I32, stack=st) for i in range(2)]
        gws = [P.sb("f_gw%d" % i, [128, 128], stack=st) for i in range(2)]
        xns = [P.sb("f_xn%d" % i, [128, DM], stack=st) for i in range(2)]
        ys = [P.sb("f_y%d" % i, [128, DM], stack=st) for i in range(2)]
        ring = [P.sb("f_g%d" % i, [128, DM], stack=st) for i in range(NR)]
        junk = P.sb("f_junk", [128, DM], stack=st)
        hid = P.sb("f_hid", [128, 128], stack=st)
        cc = P.sb("f_cc", [128, 128], stack=st)
        tmp = P.sb("f_tmp", [128, 128], stack=st)
        xrb = P.buf("XR")
        U = d.peer_u[l]
        Vt = d.peer_v[l]
        gi = 0
        for t in range(NT):
            rows = slice(t * 128, (t + 1) * 128)
            idx, gw, xn, y = idxs[t % 2], gws[t % 2], xns[t % 2], ys[t % 2]
            P.dma("sp", idx.ap[:], d.IDX[rows, :], writes=[idx])
            P.dma("sp", gw.ap[:], d.GW[rows, :], writes=[gw])
            P.dma("sp", xn.ap[:], d.XN[rows, :], writes=[xn])
            P.dma("sp", y.ap[:], d.X2[rows, :], writes=[y])
            for k in range(128):
                g = ring[gi % NR]
                gi += 1
                P.dma("pool", None, None, reads=[idx], writes=[g],
                      fn=lambda: nc.gpsimd.indirect_dma_start(out=g.ap[:, :], out_offset=None, in_=U,
                                                              in_offset=bass.IndirectOffsetOnAxis(ap=idx.ap[:, k:k + 1], axis=0)))
                P.op("dve", lambda: nc.vector.scalar_tensor_tensor(junk.ap[:], g.ap[:], 1.0, xn.ap[:], ALU.mult, ALU.mult,
                                                                   accum_out=hid.ap[:, k:k + 1]),
                     reads=[g, xn], writes=[junk, hid])
            gelu_tanh(P, "dve", cc.ap[:], hid.ap[:], tmp.ap[:], hid, cc, tmp)
            P.op("dve", lambda: nc.vector.tensor_tensor(cc.ap[:], cc.ap[:], gw.ap[:], ALU.mult), reads=[cc, gw], writes=[cc])
            for k in range(128):
                g = ring[gi % NR]
                gi += 1
                P.dma("pool", None, None, reads=[idx], writes=[g],
                      fn=lambda: nc.gpsimd.indirect_dma_start(out=g.ap[:, :], out_offset=None, in_=Vt,
                                                              in_offset=bass.IndirectOffsetOnAxis(ap=idx.ap[:, k:k + 1], axis=0)))
                P.op("dve", lambda: nc.vector.scalar_tensor_tensor(y.ap[:], g.ap[:], cc.ap[:, k:k + 1], y.ap[:], ALU.mult, ALU.add),
                     reads=[g, cc, y], writes=[y])
            P.dma("sp", d.XR[rows, :], y.ap[:], reads=[y], writes=[xrb])
    P.barrier()


def stage_p(P, C, d, cfg, l):
    nc = P.nc
    J = 4
    NB = 16384 // (128 * J)
    Uv = d.peer_u[l].rearrange("(n p j) c -> n p j c", p=128, j=J)
    Vv = d.peer_v[l].rearrange("(n p j) c -> n p j c", p=128, j=J)
    Ov = d.UVB[l].rearrange("(n p j) c -> n p j c", p=128, j=J)
    with ExitStack() as st:
        us = [P.sb("p_u%d" % i, [128, J, DM], stack=st) for i in range(2)]
        vs = [P.sb("p_v%d" % i, [128, J, DM], stack=st) for i in range(2)]
        os_ = [P.sb("p_o%d" % i, [128, J, 2048], BF16, stack=st) for i in range(2)]
        ob = P.buf("UVB")
        for n in range(NB):
            u, v, o = us[n % 2], vs[n % 2], os_[n % 2]
            P.dma("sp", u.ap[:], Uv[n], writes=[u])
            P.dma("act", v.ap[:], Vv[n], writes=[v])
            P.op("act", lambda: nc.scalar.copy(o.ap[:, :, 0:1024], u.ap[:]), reads=[u], writes=[o])
            if n % 2 == 0:
                P.op("dve", lambda: nc.vector.tensor_copy(o.ap[:, :, 1024:2048], v.ap[:]), reads=[v], writes=[o])
            else:
                P.op("pool", lambda: nc.gpsimd.tensor_copy(o.ap[:, :, 1024:2048], v.ap[:]), reads=[v], writes=[o])
            P.dma("sp", Ov[n], o.ap[:], reads=[o], writes=[ob])
    P.barrier()


def stage_f2(P, C, d, cfg, l):
    nc = P.nc
    NT = cfg.NT
    NR = 20
    G = 8
    with ExitStack() as st:
        idxs = [P.sb("f_idx%d" % i, [128, 128], I32, stack=st) for i in range(2)]
        gws = [P.sb("f_gw%d" % i, [128, 128], stack=st) for i in range(2)]
        xns = [P.sb("f_xn%d" % i, [128, DM], stack=st) for i in range(2)]
        x2s = [P.sb("f_x2%d" % i, [128, DM], stack=st) for i in range(2)]
        ys = [P.sb("f_y%d" % i, [128, DM], stack=st) for i in range(2)]
        ring = [P.sb("f_g%d" % i, [128, 2048], BF16, stack=st) for i in range(NR)]
        dgs = [P.sb("f_dg%d" % i, [128, 128], BF16, stack=st) for i in range(NR)]
        junk = P.sb("f_junk", [128, DM], BF16, stack=st)
        hid = P.sb("f_hid", [128, 128], stack=st)
        cc = P.sb("f_cc", [128, 128], stack=st)
        tmp = P.sb("f_tmp", [128, 128], stack=st)
        pys = [P.ps("f_py%d" % i, [128, DM], stack=st) for i in range(2)]
        xrb = P.buf("XR")
        UV = d.UVB[l]
        gi = 0
        for t in range(NT):
            rows = slice(t * 128, (t + 1) * 128)
            idx, gw, xn, x2, y, py = idxs[t % 2], gws[t % 2], xns[t % 2], x2s[t % 2], ys[t % 2], pys[t % 2]
            P.dma("sp", idx.ap[:], d.IDX[rows, :], writes=[idx])
            P.dma("sp", gw.ap[:], d.GW[rows, :], writes=[gw])
            P.dma("sp", xn.ap[:], d.XN[rows, :], writes=[xn])
            P.dma("sp", x2.ap[:], d.X2[rows, :], writes=[x2])
            for grp in range(128 // G):
                k0 = grp * G
                slots = []
                for k in range(k0, k0 + G):
                    j = gi % NR
                    gi += 1
                    g = ring[j]
                    slots.append(j)
                    P.dma("pool", None, None, reads=[idx], writes=[g],
                          fn=lambda: nc.gpsimd.indirect_dma_start(out=g.ap[:, :], out_offset=None, in_=UV,
                                                                  in_offset=bass.IndirectOffsetOnAxis(ap=idx.ap[:, k:k + 1], axis=0)))
                    P.op("dve", lambda: nc.vector.scalar_tensor_tensor(junk.ap[:], g.ap[:, 0:1024], 1.0, xn.ap[:], ALU.mult, ALU.mult,
                                                                       accum_out=hid.ap[:, k:k + 1]),
                         reads=[g, xn], writes=[junk, hid])
                hs, cs, ts, gs = hid.ap[:, k0:k0 + G], cc.ap[:, k0:k0 + G], tmp.ap[:, k0:k0 + G], gw.ap[:, k0:k0 + G]
                V = nc.vector
                P.op("dve", lambda: V.tensor_tensor(ts, hs, hs, ALU.mult), reads=[hid], writes=[tmp])
                P.op("dve", lambda: V.tensor_scalar(ts, ts, 0.044715, 1.0, ALU.mult, ALU.add), reads=[tmp], writes=[tmp])
                P.op("dve", lambda: V.tensor_tensor(ts, ts, hs, ALU.mult), reads=[tmp, hid], writes=[tmp])
                P.op("act", lambda: nc.scalar.activation(ts, ts, AF.Sigmoid, scale=1.5957691216057308), reads=[tmp], writes=[tmp])
                P.op("dve", lambda: V.tensor_tensor(ts, ts, hs, ALU.mult), reads=[tmp, hid], writes=[tmp])
                P.op("dve", lambda: V.tensor_tensor(cs, ts, gs, ALU.mult), reads=[tmp, gw], writes=[cc])
                for i, k in enumerate(range(k0, k0 + G)):
                    j = slots[i]
                    g, dg = ring[j], dgs[j]
                    P.op("act", lambda: nc.scalar.activation(dg.ap[:], C.ident, AF.Copy, scale=cc.ap[:, k:k + 1]),
                         reads=[C.mat, cc], writes=[dg])
                    for nb in range(2):
                        P.op("pe", lambda: nc.tensor.matmul(py.ap[:, nb * 512:(nb + 1) * 512], dg.ap[:], g.ap[:, 1024 + nb * 512:1024 + (nb + 1) * 512],
                                                            start=(k == 0), stop=(k == 127)), reads=[dg, g], writes=[py])
            P.op("dve", lambda: nc.vector.tensor_tensor(y.ap[:], py.ap[:], x2.ap[:], ALU.add), reads=[py, x2], writes=[y])
            P.dma("sp", d.XR[rows, :], y.ap[:], reads=[y], writes=[xrb])
    P.barrier()


def stage_final(P, C, d, cfg):
    nc = P.nc
    with ExitStack() as st:
        lnw = P.sb("z_lnw", [128, DM], stack=st)
        P.dma("sp", lnw.ap[:], d.lnf_w[0:1, :].to_broadcast([128, DM]), writes=[lnw])
        xts = [P.sb("z_x%d" % i, [128, DM], stack=st) for i in range(2)]
        os_ = [P.sb("z_o%d" % i, [128, DM], stack=st) for i in range(2)]
        junk = P.sb("z_junk", [128, DM], stack=st)
        ssq = P.sb("z_ssq", [128, 1], stack=st)
        ob = P.buf("out")
        for t in range(cfg.NT):
            rows = slice(t * 128, (t + 1) * 128)
            xt, o = xts[t % 2], os_[t % 2]
            P.dma("sp", xt.ap[:], d.XR[rows, :], writes=[xt])
            rms_rstd(P, xt, xt.ap[:], junk, ssq, DM)
            P.op("dve", lambda: nc.vector.scalar_tensor_tensor(o.ap[:], xt.ap[:], ssq.ap[:, 0:1], lnw.ap[:], ALU.mult, ALU.mult),
                 reads=[xt, ssq, lnw], writes=[o])
            P.dma("pool", d.out[rows, :], o.ap[:], reads=[o], writes=[ob])
    P.barrier()


def build_program(cfg):
    nc = bass.Bass("TRN2", target_bir_lowering=False)
    d = declare_dram(nc, cfg)
    with ExitStack() as st:
        P = Prog(nc, st)
        C = load_consts(P, d, cfg)
        for l in range(cfg.L):
            stage_p(P, C, d, cfg, l)
        for l in range(cfg.L):
            xsrc = d.x if l == 0 else d.XR
            stage_a(P, C, d, cfg, l, xsrc)
            stage_b(P, C, d, cfg, l)
            stage_c(P, C, d, cfg, l)
            stage_d(P, C, d, cfg, l)
            stage_e(P, C, d, cfg, l, xsrc)
            stage_f2(P, C, d, cfg, l)
        stage_final(P, C, d, cfg)
        P.finish()
    return nc, P


def kernel(**inputs):
    cfg = Cfg(T=4096, L=4)
    nc, P = build_program(cfg)
    in_maps = [make_in_map(inputs, c, cfg) for c in range(8)]
    res = run_bass_kernel_spmd(nc, in_maps, core_ids=list(range(8)))
    out = np.stack([np.asarray(res.results[c]["out"], dtype=np.float32) for c in range(8)], 0)
    return out
```

```python
import numpy as np
from contextlib import ExitStack
import concourse.bass as bass
import concourse.mybir as mybir
from concourse.bass_utils import run_bass_kernel_spmd

F32 = mybir.dt.float32
BF16 = mybir.dt.bfloat16
U32 = mybir.dt.uint32
I32 = mybir.dt.int32
AF = mybir.ActivationFunctionType
ALU = mybir.AluOpType
AX = mybir.AxisListType


class Buf:
    __slots__ = ("name", "w", "r", "ap")

    def __init__(self, name, ap=None):
        self.name = name
        self.w = None
        self.r = {}
        self.ap = ap


class Prog:
    NDMA = 8
    CAP = 30000

    def __init__(self, nc, stack):
        self.nc = nc
        self.stack = stack
        self.E = {"pe": nc.tensor, "act": nc.scalar, "dve": nc.vector, "pool": nc.gpsimd, "sp": nc.sync}
        self.nsem = 0
        self.sem = {}
        self.cnt = {}
        self.last = {}
        for k in ("pe", "act", "dve", "pool"):
            self.sem[k] = self._newsem("s_" + k)
            self.cnt[k] = 0
            self.last[k] = None
        self.dslot = {}
        self.dcnt = {}
        for q in ("sp", "pool", "act"):
            self.dslot[q] = [{"sem": self._newsem("d_%s%d" % (q, i)), "val": 0, "last": None} for i in range(self.NDMA)]
            self.dcnt[q] = 0
        self.seen = {k: {} for k in self.E}
        self.nwait = 0
        self.nins = 0
        self.uid = 0

    def _newsem(self, name):
        self.nsem += 1
        return self.stack.enter_context(self.nc.semaphore("%s_%d" % (name, self.nsem)))

    def _uid(self):
        self.uid += 1
        return self.uid

    def sb(self, name, shape, dtype=F32, stack=None):
        t = (stack or self.stack).enter_context(self.nc.sbuf_tensor("sb%d_%s" % (self._uid(), name), list(shape), dtype))
        return Buf(name, t)

    def ps(self, name, shape, dtype=F32, stack=None):
        t = (stack or self.stack).enter_context(self.nc.psum_tensor("ps%d_%s" % (self._uid(), name), list(shape), dtype))
        return Buf(name, t)

    def buf(self, name):
        return Buf(name)

    def _wait(self, e, ev):
        sem, val = ev
        if self.seen[e].get(sem.num, 0) >= val:
            return
        self.E[e].wait_ge(sem, val)
        self.seen[e][sem.num] = val
        self.nwait += 1

    def _deps(self, e, reads, writes, skip_sem=None):
        for b in reads:
            if b.w is not None and (skip_sem is None or b.w[0].num != skip_sem):
                self._wait(e, b.w)
        for b in writes:
            if b.w is not None and (skip_sem is None or b.w[0].num != skip_sem):
                self._wait(e, b.w)
            for ev in b.r.values():
                if skip_sem is None or ev[0].num != skip_sem:
                    self._wait(e, ev)

    def _mark(self, ev, reads, writes):
        for b in reads:
            b.r[ev[0].num] = ev
        for b in writes:
            b.w = ev
            b.r = {}

    def op(self, e, fn, reads=(), writes=(), same_engine_sync=True):
        if self.cnt[e] >= self.CAP:
            self.sem[e] = self._newsem("s_" + e)
            self.cnt[e] = 0
        skip = None if (same_engine_sync and e != "pe") else self.sem[e].num
        self._deps(e, reads, writes, skip)
        ins = fn()
        self.cnt[e] += 1
        ins.then_inc(self.sem[e], 1)
        ev = (self.sem[e], self.cnt[e])
        self.last[e] = ev
        self._mark(ev, reads, writes)
        self.nins += 1
        return ins

    def dma(self, q, out, in_, reads=(), writes=(), fn=None):
        slot = self.dslot[q][self.dcnt[q] % self.NDMA]
        if slot["last"] is not None:
            self._wait(q, slot["last"])
        if slot["val"] + 16 > self.CAP:
            slot["sem"] = self._newsem("d_" + q)
            slot["val"] = 0
        self._deps(q, reads, writes)
        if fn is None:
            ins = self.E[q].dma_start(out=out, in_=in_)
        else:
            ins = fn()
        ins.then_inc(slot["sem"], 16)
        slot["val"] += 16
        ev = (slot["sem"], slot["val"])
        slot["last"] = ev
        self.dcnt[q] += 1
        self._mark(ev, reads, writes)
        self.nins += 1
        return ins

    def barrier(self):
        evs = [ev for ev in self.last.values() if ev is not None]
        for q in self.dslot:
            evs += [sl["last"] for sl in self.dslot[q] if sl["last"] is not None]
        for e in self.E:
            for ev in evs:
                self._wait(e, ev)

    def finish(self):
        self.barrier()


DM = 1024
INW = 2304
EPS = 1e-6
NEG = -1.0e30


class Cfg:
    def __init__(self, T=4096, L=4, debug=False):
        self.T = T
        self.L = L
        self.NT = T // 128
        self.debug = debug


def bc_mid(ap, n):
    a = ap.ap
    return bass.AP(ap.tensor, ap.offset, [list(a[0]), [0, n]] + [list(x) for x in a[1:]])


def bc_last(ap, n):
    a = ap.ap
    return bass.AP(ap.tensor, ap.offset, [list(x) for x in a] + [[0, n]])


def host_constants(T):
    f32 = np.float32
    t = np.arange(T, dtype=f32)
    ret_inv = (1.0 / (10000.0 ** np.linspace(0.0, 1.0, 32, dtype=f32))).astype(f32)
    ang = t[:, None] * ret_inv[None, :]
    rc = np.concatenate([np.cos(ang), np.cos(ang)], -1).astype(f32)
    rs = np.concatenate([-np.sin(ang), np.sin(ang)], -1).astype(f32)
    rows = np.floor(t / 64.0).astype(f32)
    cols = (t - rows * 64.0).astype(f32)
    ax_inv = (10000.0 ** (-np.arange(16, dtype=f32) / 16.0)).astype(f32)
    ar = rows[:, None] * ax_inv[None, :]
    ac = cols[:, None] * ax_inv[None, :]
    acos = np.concatenate([np.cos(ar), np.cos(ar), np.cos(ac), np.cos(ac)], -1).astype(f32)
    asin = np.concatenate([-np.sin(ar), np.sin(ar), -np.sin(ac), np.sin(ac)], -1).astype(f32)
    tab = np.stack([rc, rs, rc * f32(0.125), rs * f32(0.125), acos, asin], 1).astype(f32)
    j = np.arange(128, dtype=f32)[:, None]
    i = np.arange(128, dtype=f32)[None, :]
    cm = np.stack([np.eye(128, dtype=f32), np.maximum(i - j, 0) + 0 * j, np.maximum(j - i, 0) + 0 * i,
                   (i >= j).astype(f32), (j > i).astype(f32)], 1).astype(f32)
    rw = np.stack([np.broadcast_to(i + 1.0, (64, 128)), np.broadcast_to(128.0 - i, (64, 128))], 1).astype(f32)
    cw = np.concatenate([127.0 - j, j], 1).astype(f32)
    return {"c_tab": np.ascontiguousarray(tab), "c_mat": np.ascontiguousarray(cm),
            "c_rw": np.ascontiguousarray(rw), "c_cw": np.ascontiguousarray(cw)}


class D:
    pass


def declare_dram(nc, cfg):
    T, L = cfg.T, cfg.L
    d = D()

    def inp(name, shape, dt=F32):
        return nc.dram_tensor(name, list(shape), dt, kind="ExternalInput").ap()

    def scr(name, shape, dt=F32):
        kind = "ExternalOutput" if cfg.debug else "Internal"
        return nc.dram_tensor(name, list(shape), dt, kind=kind).ap()

    d.x = inp("x", [T, DM])
    d.ln1_w = inp("ln1_w", [L, DM])
    d.w_in = inp("w_in", [L, DM, INW])
    d.ret_log_decay = inp("ret_log_decay", [L, 8])
    d.ret_gn_w = inp("ret_gn_w", [L, 256])
    d.lru_conv_w = inp("lru_conv_w", [L, 4, 256])
    d.lru_conv_b = inp("lru_conv_b", [L, 256])
    d.lru_bd = inp("lru_bd", [L, 2, 2, 2, 128, 128])
    d.lru_gate_b = inp("lru_gate_b", [L, 2, 2, 256])
    d.lru_lambda = inp("lru_lambda", [L, 2, 256])
    d.attn_q_norm = inp("attn_q_norm", [L, 64])
    d.attn_k_norm = inp("attn_k_norm", [L, 64])
    d.w_out = inp("w_out", [L, DM, DM])
    d.ln2_w = inp("ln2_w", [L, DM])
    d.peer_wq = inp("peer_wq", [L, DM, 2048])
    d.peer_keysT = inp("peer_keysT", [L, 16, 128, 128])
    d.peer_u = [inp("peer_u%d" % i, [16384, DM]) for i in range(L)]
    d.peer_v = [inp("peer_v%d" % i, [16384, DM]) for i in range(L)]
    d.lnf_w = inp("lnf_w", [1, DM])
    d.c_tab = inp("c_tab", [T, 6, 64])
    d.c_mat = inp("c_mat", [128, 5, 128])
    d.c_rw = inp("c_rw", [64, 2, 128])
    d.c_cw = inp("c_cw", [128, 2])
    d.out = nc.dram_tensor("out", [T, DM], F32, kind="ExternalOutput").ap()
    d.XR = scr("XR", [T, DM])
    d.X2 = scr("X2", [T, DM])
    d.XN = scr("XN", [T, DM])
    d.TM = scr("TM", [T, INW])
    d.FT = scr("FT", [1664, T])
    d.MIXT = scr("MIXT", [DM, T])
    d.IDX = scr("IDX", [T, 128], I32)
    d.GW = scr("GW", [T, 128])
    d.UVB = [scr("UVB%d" % i, [16384, 2048], BF16) for i in range(L)]
    return d


def load_consts(P, d, cfg):
    nc = P.nc
    C = D()
    C.mat = P.sb("c_mat", [128, 5, 128])
    P.dma("sp", C.mat.ap[:], d.c_mat, writes=[C.mat])
    C.ident = C.mat.ap[:, 0, :]
    C.rw = P.sb("c_rw", [64, 2, 128])
    P.dma("sp", C.rw.ap[:], d.c_rw, writes=[C.rw])
    C.cw = P.sb("c_cw", [128, 2])
    P.dma("sp", C.cw.ap[:], d.c_cw, writes=[C.cw])
    return C


def transpose_blocks(P, C, src_buf, src_aps, pT, dst_buf, dst_aps, evac_engs=("act", "dve")):
    nc = P.nc
    n = len(src_aps)
    for g0 in range(0, n, 8):
        g1 = min(n, g0 + 8)
        for k in range(g0, g1):
            s = (k - g0) * 128
            P.op("pe", lambda: nc.tensor.transpose(pT.ap[:, s:s + 128], src_aps[k], C.ident),
                 reads=[src_buf, C.mat], writes=[pT])
        for k in range(g0, g1):
            s = (k - g0) * 128
            e = evac_engs[k % len(evac_engs)]
            if e == "act":
                P.op("act", lambda: nc.scalar.copy(dst_aps[k], pT.ap[:, s:s + 128]), reads=[pT], writes=[dst_buf])
            else:
                P.op(e, lambda: P.E[e].tensor_copy(dst_aps[k], pT.ap[:, s:s + 128]), reads=[pT], writes=[dst_buf])


def rope(P, eng, out_ap, x_ap, cos_ap, ss_ap, H, R, tA, tB, rbufs, wbufs):
    nc = P.nc
    E = P.E[eng]
    hs = 32 // R
    W = H * 64
    x3 = x_ap.rearrange("p (h d) -> p h d", d=64)
    a3 = tA.ap[:, 0:W].rearrange("p (h d) -> p h d", d=64)
    P.op(eng, lambda: E.tensor_tensor(a3, x3, bc_mid(cos_ap, H), ALU.mult), reads=rbufs, writes=[tA])
    x5 = x_ap.rearrange("p (h r two d) -> p h r two d", r=R, two=2, d=hs)
    b5 = tB.ap[:, 0:W].rearrange("p (h r two d) -> p h r two d", r=R, two=2, d=hs)
    s4 = ss_ap.rearrange("p (r two d) -> p r two d", r=R, two=2, d=hs)
    for two in range(2):
        sv = s4[:, :, two, :]
        if R == 1:
            o_ = b5[:, :, 0, two, :]
            i_ = x5[:, :, 0, 1 - two, :]
            s_ = bc_mid(sv[:, 0, :], H)
        else:
            o_ = b5[:, :, :, two, :]
            i_ = x5[:, :, :, 1 - two, :]
            s_ = bc_mid(sv, H)
        P.op(eng, lambda: E.tensor_tensor(o_, i_, s_, ALU.mult), reads=rbufs, writes=[tB])
    P.op(eng, lambda: E.tensor_tensor(out_ap, tA.ap[:, 0:W], tB.ap[:, 0:W], ALU.add), reads=[tA, tB], writes=wbufs)


def rms_rstd(P, x_buf, x_ap, junk, ssq, n):
    nc = P.nc
    P.op("act", lambda: nc.scalar.activation(junk.ap[:, 0:n], x_ap, AF.Square, accum_out=ssq.ap[:, 0:1]),
         reads=[x_buf], writes=[junk, ssq])
    P.op("dve", lambda: nc.vector.tensor_scalar(ssq.ap[:, 0:1], ssq.ap[:, 0:1], 1.0 / n, EPS, ALU.mult, ALU.add),
         reads=[ssq], writes=[ssq])
    P.op("act", lambda: nc.scalar.activation(ssq.ap[:, 0:1], ssq.ap[:, 0:1], AF.Sqrt), reads=[ssq], writes=[ssq])
    P.op("dve", lambda: nc.vector.reciprocal(ssq.ap[:, 0:1], ssq.ap[:, 0:1]), reads=[ssq], writes=[ssq])


def stage_a(P, C, d, cfg, l, xsrc):
    nc = P.nc
    NT = cfg.NT
    with ExitStack() as st:
        win = P.sb("a_win", [128, 8, INW], stack=st)
        for c in range(8):
            P.dma("sp" if c % 2 == 0 else "act", win.ap[:, c, :], d.w_in[l, c * 128:(c + 1) * 128, :], writes=[win])
        lnw = P.sb("a_lnw", [128, DM], stack=st)
        P.dma("sp", lnw.ap[:], d.ln1_w[l:l + 1, :].to_broadcast([128, DM]), writes=[lnw])
        qkw = P.sb("a_qkw", [128, 2, 64], stack=st)
        P.dma("sp", qkw.ap[:, 0, :], d.attn_q_norm[l:l + 1, :].to_broadcast([128, 64]), writes=[qkw])
        P.dma("sp", qkw.ap[:, 1, :], d.attn_k_norm[l:l + 1, :].to_broadcast([128, 64]), writes=[qkw])
        P.op("dve", lambda: nc.vector.tensor_scalar(qkw.ap[:, 0, :], qkw.ap[:, 0, :], 0.125, None, ALU.mult),
             reads=[qkw], writes=[qkw])
        xts = [P.sb("a_x%d" % i, [128, DM], stack=st) for i in range(2)]
        tabs = [P.sb("a_tab%d" % i, [128, 6, 64], stack=st) for i in range(2)]
        h = P.sb("a_h", [128, DM], stack=st)
        junk = P.sb("a_junk", [128, DM], stack=st)
        ssq = P.sb("a_ssq", [128, 1], stack=st)
        hT = P.sb("a_hT", [128, 8, 128], stack=st)
        preps = [P.sb("a_prep%d" % i, [128, INW], stack=st) for i in range(2)]
        tA = P.sb("a_tA", [128, 512], stack=st)
        tB = P.sb("a_tB", [128, 512], stack=st)
        tC = P.sb("a_tC", [128, 640], stack=st)
        s10 = P.sb("a_s10", [128, 10], stack=st)
        tA2 = P.sb("a_tA2", [128, 512], stack=st)
        tB2 = P.sb("a_tB2", [128, 512], stack=st)
        ftts = [P.sb("a_ftt%d" % i, [128, 13, 128], stack=st) for i in range(2)]
        pT = P.ps("a_pT", [128, 1024], stack=st)
        pP = P.ps("a_pP", [128, 2560], stack=st)
        tmb = P.buf("TM")
        ftb = P.buf("FT")
        FTv = d.FT.rearrange("(b p) t -> p b t", p=128)

        for t in range(NT):
            xt = xts[t % 2]
            tab = tabs[t % 2]
            prep = preps[t % 2]
            ftt = ftts[t % 2]
            rows = slice(t * 128, (t + 1) * 128)
            P.dma("sp", xt.ap[:], xsrc[rows, :], writes=[xt])
            P.dma("sp", tab.ap[:], d.c_tab[rows, :, :], writes=[tab])
            rms_rstd(P, xt, xt.ap[:], junk, ssq, DM)
            P.op("dve", lambda: nc.vector.scalar_tensor_tensor(h.ap[:], xt.ap[:], ssq.ap[:, 0:1], lnw.ap[:], ALU.mult, ALU.mult),
                 reads=[xt, ssq, lnw], writes=[h])
            transpose_blocks(P, C, h, [h.ap[:, c * 128:(c + 1) * 128] for c in range(8)], pT,
                             hT, [hT.ap[:, c, :] for c in range(8)])
            for nb in range(5):
                n0, n1 = nb * 512, min(INW, nb * 512 + 512)
                for c in range(8):
                    P.op("pe", lambda: nc.tensor.matmul(pP.ap[:, n0:n1], hT.ap[:, c, :], win.ap[:, c, n0:n1],
                                                        start=(c == 0), stop=(c == 7)), reads=[hT, win], writes=[pP])
            for nb in range(5):
                n0, n1 = nb * 512, min(INW, nb * 512 + 512)
                if nb % 2 == 0:
                    P.op("act", lambda: nc.scalar.copy(prep.ap[:, n0:n1], pP.ap[:, n0:n1]), reads=[pP], writes=[prep])
                else:
                    P.op("dve", lambda: nc.vector.tensor_copy(prep.ap[:, n0:n1], pP.ap[:, n0:n1]), reads=[pP], writes=[prep])
            rope(P, "pool", prep.ap[:, 0:256], prep.ap[:, 0:256], tab.ap[:, 0, :], tab.ap[:, 1, :], 4, 1, tA, tB,
                 [prep, tab], [prep])
            rope(P, "pool", prep.ap[:, 256:512], prep.ap[:, 256:512], tab.ap[:, 2, :], tab.ap[:, 3, :], 4, 1, tA, tB,
                 [prep, tab], [prep])
            qk = prep.ap[:, 1536:2176]
            P.op("dve", lambda: nc.vector.tensor_tensor(tC.ap[:], qk, qk, ALU.mult), reads=[prep], writes=[tC])
            P.op("dve", lambda: nc.vector.tensor_reduce(s10.ap[:], tC.ap[:].rearrange("p (h d) -> p h d", d=64), AX.X, ALU.add),
                 reads=[tC], writes=[s10])
            P.op("dve", lambda: nc.vector.tensor_scalar(s10.ap[:], s10.ap[:], 1.0 / 64, EPS, ALU.mult, ALU.add),
                 reads=[s10], writes=[s10])
            P.op("act", lambda: nc.scalar.activation(s10.ap[:], s10.ap[:], AF.Sqrt), reads=[s10], writes=[s10])
            P.op("dve", lambda: nc.vector.reciprocal(s10.ap[:], s10.ap[:]), reads=[s10], writes=[s10])
            P.op("dve", lambda: nc.vector.tensor_tensor(tC.ap[:].rearrange("p (h d) -> p h d", d=64),
                                                        qk.rearrange("p (h d) -> p h d", d=64),
                                                        bc_last(s10.ap[:], 64), ALU.mult), reads=[prep, s10], writes=[tC])
            P.op("dve", lambda: nc.vector.tensor_tensor(tC.ap[:, 0:512].rearrange("p (h d) -> p h d", d=64),
                                                        tC.ap[:, 0:512].rearrange("p (h d) -> p h d", d=64),
                                                        bc_mid(qkw.ap[:, 0, :], 8), ALU.mult), reads=[tC, qkw], writes=[tC])
            P.op("dve", lambda: nc.vector.tensor_tensor(tC.ap[:, 512:640].rearrange("p (h d) -> p h d", d=64),
                                                        tC.ap[:, 512:640].rearrange("p (h d) -> p h d", d=64),
                                                        bc_mid(qkw.ap[:, 1, :], 2), ALU.mult), reads=[tC, qkw], writes=[tC])
            rope(P, "dve", prep.ap[:, 1536:2048], tC.ap[:, 0:512], tab.ap[:, 4, :], tab.ap[:, 5, :], 8, 2, tA2, tB2,
                 [tC, tab], [prep])
            rope(P, "dve", prep.ap[:, 2048:2176], tC.ap[:, 512:640], tab.ap[:, 4, :], tab.ap[:, 5, :], 2, 2, tA2, tB2,
                 [tC, tab], [prep])
            P.dma("pool", d.TM[rows, :], prep.ap[:], reads=[prep], writes=[tmb])
            cols = [k * 128 for k in range(4)] + [1024 + k * 128 for k in range(9)]
            transpose_blocks(P, C, prep, [prep.ap[:, c0:c0 + 128] for c0 in cols], pT,
                             ftt, [ftt.ap[:, k, :] for k in range(13)])
            P.dma("pool", FTv[:, :, t * 128:(t + 1) * 128], ftt.ap[:], reads=[ftt], writes=[ftb])
    P.barrier()
    return


def make_in_map(inputs, core, cfg):
    T, L = cfg.T, cfg.L
    f = lambda a: np.ascontiguousarray(np.asarray(a, dtype=np.float32))
    m = {}
    m["x"] = f(inputs["x"][core, :T])
    for k in ("ln1_w", "w_in", "ret_gn_w", "lru_conv_w", "lru_conv_b", "lru_gate_b", "lru_lambda",
              "attn_q_norm", "attn_k_norm", "w_out", "ln2_w", "peer_wq"):
        m[k] = f(inputs[k][:L])
    for i in range(L):
        m["peer_u%d" % i] = f(inputs["peer_u"][i])
        m["peer_v%d" % i] = f(inputs["peer_v"][i])
    m["ret_log_decay"] = f(np.asarray(inputs["ret_log_decay"])[:L].reshape(L, 8))
    gw = np.asarray(inputs["lru_gate_w"], dtype=np.float32)[:L]
    bd = np.zeros((L, 2, 2, 2, 128, 128), np.float32)
    for g in range(2):
        for b in range(2):
            bd[:, :, :, g, b * 64:(b + 1) * 64, b * 64:(b + 1) * 64] = gw[:, :, :, g * 2 + b]
    m["lru_bd"] = bd
    keys = np.asarray(inputs["peer_keys"], dtype=np.float32)[:L]
    m["peer_keysT"] = np.ascontiguousarray(keys.reshape(L, 16, 128, 128).transpose(0, 1, 3, 2))
    m["lnf_w"] = f(np.asarray(inputs["lnf_w"]).reshape(1, DM))
    m.update(host_constants(T))
    return m


def gelu_tanh(P, eng, out_ap, x_ap, tmp_ap, xb, ob, tb):
    nc = P.nc
    E = P.E[eng]
    P.op(eng, lambda: E.tensor_tensor(tmp_ap, x_ap, x_ap, ALU.mult), reads=[xb], writes=[tb])
    P.op(eng, lambda: E.tensor_scalar(tmp_ap, tmp_ap, 0.044715, 1.0, ALU.mult, ALU.add), reads=[tb], writes=[tb])
    P.op(eng, lambda: E.tensor_tensor(tmp_ap, tmp_ap, x_ap, ALU.mult), reads=[tb, xb], writes=[tb])
    P.op("act", lambda: nc.scalar.activation(tmp_ap, tmp_ap, AF.Sigmoid, scale=1.5957691216057308), reads=[tb], writes=[tb])
    P.op(eng, lambda: E.tensor_tensor(out_ap, x_ap, tmp_ap, ALU.mult), reads=[tb, xb], writes=[ob])


def stage_b(P, C, d, cfg, l):
    nc = P.nc
    NT = cfg.NT
    MX = d.MIXT.rearrange("(b p) t -> p b t", p=128)
    with ExitStack() as st:
        lg = P.sb("b_lg", [128, 8], stack=st)
        P.dma("sp", lg.ap[:], d.ret_log_decay[l:l + 1, :].to_broadcast([128, 8]), writes=[lg])
        gnw = P.sb("b_gnw", [128, 256], stack=st)
        P.dma("sp", gnw.ap[:], d.ret_gn_w[l:l + 1, :].to_broadcast([128, 256]), writes=[gnw])
        dm = P.sb("b_dm", [128, 4, 128], stack=st)
        t1 = P.sb("b_t1", [128, 128], stack=st)
        t2 = P.sb("b_t2", [128, 128], stack=st)
        qw = P.sb("b_qw", [64, 8, 128], stack=st)
        kw = P.sb("b_kw", [128, 8], stack=st)
        dec = P.sb("b_dec", [64, 8], stack=st)
        for h in range(4):
            P.op("act", lambda: nc.scalar.activation(t1.ap[:], C.mat.ap[:, 1, :], AF.Exp, scale=lg.ap[:, h:h + 1]),
                 reads=[C.mat, lg], writes=[t1])
            P.op("dve", lambda: nc.vector.tensor_tensor(t1.ap[:], t1.ap[:], C.mat.ap[:, 3, :], ALU.mult), reads=[t1, C.mat], writes=[t1])
            P.op("act", lambda: nc.scalar.activation(t2.ap[:], C.mat.ap[:, 2, :], AF.Exp, scale=lg.ap[:, 4 + h:5 + h]),
                 reads=[C.mat, lg], writes=[t2])
            P.op("dve", lambda: nc.vector.tensor_tensor(t2.ap[:], t2.ap[:], C.mat.ap[:, 4, :], ALU.mult), reads=[t2, C.mat], writes=[t2])
            P.op("dve", lambda: nc.vector.tensor_tensor(dm.ap[:, h, :], t1.ap[:], t2.ap[:], ALU.add), reads=[t1, t2], writes=[dm])
            for dr in range(2):
                k = dr * 4 + h
                P.op("act", lambda: nc.scalar.activation(qw.ap[:, k, :], C.rw.ap[:, dr, :], AF.Exp, scale=lg.ap[0:64, k:k + 1]),
                     reads=[C.rw, lg], writes=[qw])
                P.op("act", lambda: nc.scalar.activation(kw.ap[:, k:k + 1], C.cw.ap[:, dr:dr + 1], AF.Exp, scale=lg.ap[:, k:k + 1]),
                     reads=[C.cw, lg], writes=[kw])
        P.op("act", lambda: nc.scalar.activation(dec.ap[:], lg.ap[0:64, :], AF.Exp, scale=128.0), reads=[lg], writes=[dec])

        KV = P.sb("b_KV", [64, NT, 8, 64], stack=st)
        ST = P.sb("b_ST", [64, NT, 8, 64], stack=st)
        kvs = [P.sb("b_kv%d" % i, [128, 512], stack=st) for i in range(2)]
        kwk = P.sb("b_kwk", [128, 8, 64], stack=st)
        pKV = P.ps("b_pKV", [64, 8, 64], stack=st)
        for c in range(NT):
            rows = slice(c * 128, (c + 1) * 128)
            kv = kvs[c % 2]
            P.dma("sp", kv.ap[:], d.TM[rows, 256:768], writes=[kv])
            for k in range(8):
                h = k % 4
                e = "dve" if k % 2 == 0 else "pool"
                P.op(e, lambda: P.E[e].tensor_scalar(kwk.ap[:, k, :], kv.ap[:, h * 64:(h + 1) * 64], kw.ap[:, k:k + 1], None, ALU.mult),
                     reads=[kv, kw], writes=[kwk])
            for k in range(8):
                h = k % 4
                P.op("pe", lambda: nc.tensor.matmul(pKV.ap[:, k, :], kwk.ap[:, k, :], kv.ap[:, 256 + h * 64:256 + (h + 1) * 64],
                                                    start=True, stop=True), reads=[kwk, kv], writes=[pKV])
            P.op("act", lambda: nc.scalar.copy(KV.ap[:, c, :, :], pKV.ap[:]), reads=[pKV], writes=[KV])
        P.op("dve", lambda: nc.vector.memset(ST.ap[:, 0, 0:4, :], 0.0), writes=[ST])
        P.op("dve", lambda: nc.vector.memset(ST.ap[:, NT - 1, 4:8, :], 0.0), writes=[ST])
        for c in range(1, NT):
            for h in range(4):
                P.op("dve", lambda: nc.vector.scalar_tensor_tensor(ST.ap[:, c, h, :], ST.ap[:, c - 1, h, :], dec.ap[:, h:h + 1],
                                                                   KV.ap[:, c - 1, h, :], ALU.mult, ALU.add),
                     reads=[ST, KV, dec], writes=[ST])
            cb = NT - 1 - c
            for h in range(4, 8):
                P.op("dve", lambda: nc.vector.scalar_tensor_tensor(ST.ap[:, cb, h, :], ST.ap[:, cb + 1, h, :], dec.ap[:, h:h + 1],
                                                                   KV.ap[:, cb + 1, h, :], ALU.mult, ALU.add),
                     reads=[ST, KV, dec], writes=[ST])
        qks = [P.sb("b_qk%d" % i, [64, 8, 128], stack=st) for i in range(2)]
        vgs = [P.sb("b_vg%d" % i, [128, 512], stack=st) for i in range(2)]
        qwq = P.sb("b_qwq", [64, 8, 128], stack=st)
        SM = P.sb("b_SM", [128, 4, 128], stack=st)
        o = P.sb("b_o", [128, 256], stack=st)
        sq = P.sb("b_sq", [128, 256], stack=st)
        s4 = P.sb("b_s4", [128, 4], stack=st)
        sg = P.sb("b_sg", [128, 256], stack=st)
        mts = [P.sb("b_mt%d" % i, [128, 2, 128], stack=st) for i in range(2)]
        pS = P.ps("b_pS", [128, 4, 128], stack=st)
        pO = P.ps("b_pO", [128, 4, 64], stack=st)
        pT = P.ps("b_pT", [128, 1024], stack=st)
        mxb = P.buf("MIXT")
        FTq = d.FT[0:512, :].rearrange("(a dd) t -> dd a t", dd=64)
        for c in range(NT):
            rows = slice(c * 128, (c + 1) * 128)
            qk = qks[c % 2]
            vg = vgs[c % 2]
            mt = mts[c % 2]
            P.dma("sp", qk.ap[:], FTq[:, :, c * 128:(c + 1) * 128], writes=[qk])
            P.dma("sp", vg.ap[:], d.TM[rows, 512:1024], writes=[vg])
            for dr in range(2):
                P.op("pool", lambda: nc.gpsimd.tensor_tensor(qwq.ap[:, dr * 4:dr * 4 + 4, :], qk.ap[:, 0:4, :], qw.ap[:, dr * 4:dr * 4 + 4, :], ALU.mult),
                     reads=[qk, qw], writes=[qwq])
            for h in range(4):
                P.op("pe", lambda: nc.tensor.matmul(pS.ap[:, h, :], qk.ap[:, 4 + h, :], qk.ap[:, h, :], start=True, stop=True),
                     reads=[qk], writes=[pS])
            P.op("dve", lambda: nc.vector.tensor_tensor(SM.ap[:], pS.ap[:], dm.ap[:], ALU.mult), reads=[pS, dm], writes=[SM])
            for h in range(4):
                P.op("pe", lambda: nc.tensor.matmul(pO.ap[:, h, :], SM.ap[:, h, :], vg.ap[:, h * 64:(h + 1) * 64], start=True, stop=False),
                     reads=[SM, vg], writes=[pO])
                P.op("pe", lambda: nc.tensor.matmul(pO.ap[:, h, :], qwq.ap[:, h, :], ST.ap[:, c, h, :], start=False, stop=False),
                     reads=[qwq, ST], writes=[pO])
                P.op("pe", lambda: nc.tensor.matmul(pO.ap[:, h, :], qwq.ap[:, 4 + h, :], ST.ap[:, c, 4 + h, :], start=False, stop=True),
                     reads=[qwq, ST], writes=[pO])
            P.op("act", lambda: nc.scalar.copy(o.ap[:], pO.ap[:].rearrange("p h e -> p (h e)")), reads=[pO], writes=[o])
            P.op("dve", lambda: nc.vector.tensor_tensor(sq.ap[:], o.ap[:], o.ap[:], ALU.mult), reads=[o], writes=[sq])
            P.op("dve", lambda: nc.vector.tensor_reduce(s4.ap[:], sq.ap[:].rearrange("p (h e) -> p h e", e=64), AX.X, ALU.add),
                 reads=[sq], writes=[s4])
            P.op("dve", lambda: nc.vector.tensor_scalar(s4.ap[:], s4.ap[:], 1.0 / 64, EPS, ALU.mult, ALU.add), reads=[s4], writes=[s4])
            P.op("act", lambda: nc.scalar.activation(s4.ap[:], s4.ap[:], AF.Sqrt), reads=[s4], writes=[s4])
            P.op("dve", lambda: nc.vector.reciprocal(s4.ap[:], s4.ap[:]), reads=[s4], writes=[s4])
            P.op("dve", lambda: nc.vector.tensor_tensor(o.ap[:].rearrange("p (h e) -> p h e", e=64),
                                                        o.ap[:].rearrange("p (h e) -> p h e", e=64), bc_last(s4.ap[:], 64), ALU.mult),
                 reads=[o, s4], writes=[o])
            P.op("dve", lambda: nc.vector.tensor_tensor(o.ap[:], o.ap[:], gnw.ap[:], ALU.mult), reads=[o, gnw], writes=[o])
            P.op("act", lambda: nc.scalar.activation(sg.ap[:], vg.ap[:, 256:512], AF.Sigmoid), reads=[vg], writes=[sg])
            P.op("pool", lambda: nc.gpsimd.tensor_tensor(sg.ap[:], sg.ap[:], vg.ap[:, 256:512], ALU.mult), reads=[sg, vg], writes=[sg])
            P.op("dve", lambda: nc.vector.tensor_tensor(o.ap[:], o.ap[:], sg.ap[:], ALU.mult), reads=[o, sg], writes=[o])
            transpose_blocks(P, C, o, [o.ap[:, 0:128], o.ap[:, 128:256]], pT, mt, [mt.ap[:, 0, :], mt.ap[:, 1, :]])
            P.dma("pool", MX[:, 0:2, c * 128:(c + 1) * 128], mt.ap[:], reads=[mt], writes=[mxb])
    P.barrier()


def stage_c(P, C, d, cfg, l):
    nc = P.nc
    T = cfg.T
    with ExitStack() as st:
        xp = P.sb("c_xp", [128, T + 3], stack=st)
        xc = P.sb("c_xc", [128, T], stack=st)
        rr = P.sb("c_rr", [128, T], stack=st)
        ii = P.sb("c_ii", [128, T], stack=st)
        aa = P.sb("c_aa", [128, T], stack=st)
        bb = P.sb("c_bb", [128, T], stack=st)
        hf = P.sb("c_hf", [128, T], stack=st)
        gg = P.sb("c_gg", [128, T], stack=st)
        cwt = P.sb("c_cwt", [128, 5], stack=st)
        gb = P.sb("c_gb", [128, 4], stack=st)
        lam = P.sb("c_lam", [128, 2], stack=st)
        nsp = P.sb("c_nsp", [128, 4], stack=st)
        bd = P.sb("c_bd", [128, 4, 128], stack=st)
        pG = [P.ps("c_pG%d" % i, [128, 512], stack=st) for i in range(2)]
        mxb = P.buf("MIXT")
        col = lambda ap1d: ap1d.rearrange("(c o) -> c o", o=1)
        for g in range(2):
            ch = slice(g * 128, (g + 1) * 128)
            for j in range(4):
                P.dma("sp", cwt.ap[:, j:j + 1], col(d.lru_conv_w[l, j, ch]), writes=[cwt])
            P.dma("sp", cwt.ap[:, 4:5], col(d.lru_conv_b[l, ch]), writes=[cwt])
            for dr in range(2):
                P.dma("sp", lam.ap[:, dr:dr + 1], col(d.lru_lambda[l, dr, ch]), writes=[lam])
                for gt in range(2):
                    P.dma("sp", gb.ap[:, dr * 2 + gt:dr * 2 + gt + 1], col(d.lru_gate_b[l, dr, gt, ch]), writes=[gb])
                    P.dma("sp", bd.ap[:, dr * 2 + gt, :], d.lru_bd[l, dr, gt, g, :, :], writes=[bd])
            P.op("dve", lambda: nc.vector.memset(xp.ap[:, 0:2], 0.0), writes=[xp])
            P.op("dve", lambda: nc.vector.memset(xp.ap[:, T + 2:T + 3], 0.0), writes=[xp])
            P.dma("sp", xp.ap[:, 2:T + 2], d.FT[512 + g * 128:512 + (g + 1) * 128, :], writes=[xp])
            P.dma("act", gg.ap[:], d.FT[768 + g * 128:768 + (g + 1) * 128, :], writes=[gg])
            P.op("act", lambda: nc.scalar.activation(nsp.ap[:, 0:2], lam.ap[:], AF.Exp, scale=-1.0), reads=[lam], writes=[nsp])
            P.op("act", lambda: nc.scalar.activation(nsp.ap[:, 0:2], nsp.ap[:, 0:2], AF.Ln, bias=1.0), reads=[nsp], writes=[nsp])
            P.op("dve", lambda: nc.vector.tensor_scalar(nsp.ap[:, 2:4], nsp.ap[:, 0:2], -16.0, None, ALU.mult), reads=[nsp], writes=[nsp])
            P.op("dve", lambda: nc.vector.tensor_scalar(nsp.ap[:, 0:2], nsp.ap[:, 0:2], -8.0, None, ALU.mult), reads=[nsp], writes=[nsp])
            P.op("dve", lambda: nc.vector.tensor_scalar(xc.ap[:], xp.ap[:, 0:T], cwt.ap[:, 0:1], cwt.ap[:, 4:5], ALU.mult, ALU.add),
                 reads=[xp, cwt], writes=[xc])
            for j in range(1, 4):
                P.op("dve", lambda: nc.vector.scalar_tensor_tensor(xc.ap[:], xp.ap[:, j:j + T], cwt.ap[:, j:j + 1], xc.ap[:], ALU.mult, ALU.add),
                     reads=[xp, cwt, xc], writes=[xc])
            for dr in range(2):
                for gt in range(2):
                    dst = rr if gt == 0 else ii
                    for s in range(0, T, 512):
                        pg = pG[(s // 512) % 2]
                        P.op("pe", lambda: nc.tensor.matmul(pg.ap[:], bd.ap[:, dr * 2 + gt, :], xc.ap[:, s:s + 512], start=True, stop=True),
                             reads=[bd, xc], writes=[pg])
                        P.op("act", lambda: nc.scalar.activation(dst.ap[:, s:s + 512], pg.ap[:], AF.Sigmoid,
                                                                 bias=gb.ap[:, dr * 2 + gt:dr * 2 + gt + 1]),
                             reads=[pg, gb], writes=[dst])
                P.op("act", lambda: nc.scalar.activation(aa.ap[:], rr.ap[:], AF.Exp, scale=nsp.ap[:, dr:dr + 1]), reads=[rr, nsp], writes=[aa])
                P.op("act", lambda: nc.scalar.activation(rr.ap[:], rr.ap[:], AF.Exp, scale=nsp.ap[:, 2 + dr:3 + dr]), reads=[rr, nsp], writes=[rr])
                P.op("dve", lambda: nc.vector.tensor_scalar(rr.ap[:], rr.ap[:], -1.0, 1.0, ALU.mult, ALU.add), reads=[rr], writes=[rr])
                P.op("act", lambda: nc.scalar.activation(rr.ap[:], rr.ap[:], AF.Sqrt), reads=[rr], writes=[rr])
                P.op("dve", lambda: nc.vector.tensor_tensor(bb.ap[:], rr.ap[:], ii.ap[:], ALU.mult), reads=[rr, ii], writes=[bb])
                P.op("pool", lambda: nc.gpsimd.tensor_tensor(bb.ap[:], bb.ap[:], xc.ap[:], ALU.mult), reads=[bb, xc], writes=[bb])
                if dr == 0:
                    P.op("dve", lambda: nc.vector.tensor_tensor_scan(hf.ap[:], aa.ap[:], bb.ap[:], 0.0, ALU.mult, ALU.add),
                         reads=[aa, bb], writes=[hf])
                else:
                    P.op("dve", lambda: nc.vector.tensor_tensor_scan(ii.ap[:, ::-1], aa.ap[:, ::-1], bb.ap[:, ::-1], 0.0, ALU.mult, ALU.add),
                         reads=[aa, bb], writes=[ii])
                    P.op("dve", lambda: nc.vector.tensor_tensor(hf.ap[:], hf.ap[:], ii.ap[:], ALU.add), reads=[hf, ii], writes=[hf])
            gelu_tanh(P, "pool", gg.ap[:], gg.ap[:], aa.ap[:], gg, gg, aa)
            P.op("dve", lambda: nc.vector.tensor_tensor(hf.ap[:], hf.ap[:], gg.ap[:], ALU.mult), reads=[hf, gg], writes=[hf])
            P.dma("sp", d.MIXT[256 + g * 128:256 + (g + 1) * 128, :], hf.ap[:], reads=[hf], writes=[mxb])
    P.barrier()


ATT_DT = BF16


def stage_d(P, C, d, cfg, l):
    nc = P.nc
    T, NT = cfg.T, cfg.NT
    NQG = T // 512
    MX = d.MIXT.rearrange("(b p) t -> p b t", p=128)
    with ExitStack() as st:
        kT = P.sb("d_kT", [64, 2, T], ATT_DT, stack=st)
        vaug = P.sb("d_vaug", [128, NT, 2, 65], ATT_DT, stack=st)
        stg = P.sb("d_stg", [64, 8, 512], stack=st)
        vst = P.sb("d_vst", [128, NT, 128], stack=st)
        FTk = d.FT[1536:1664, :].rearrange("(a dd) t -> dd a t", dd=64)
        for s in range(0, T, 512):
            P.dma("sp", stg.ap[:, 0:2, :], FTk[:, :, s:s + 512], writes=[stg])
            P.op("dve", lambda: nc.vector.tensor_copy(kT.ap[:, :, s:s + 512], stg.ap[:, 0:2, :]), reads=[stg], writes=[kT])
        P.dma("sp", vst.ap[:], d.TM[:, 2176:2304].rearrange("(n p) c -> p n c", p=128), writes=[vst])
        P.op("dve", lambda: nc.vector.memset(vaug.ap[:, :, :, 64:65], 1.0), writes=[vaug])
        P.op("dve", lambda: nc.vector.tensor_copy(vaug.ap[:, :, :, 0:64], vst.ap[:].rearrange("p n (k e) -> p n k e", e=64)),
             reads=[vst], writes=[vaug])
        qTb = P.sb("d_qTb", [64, 8, 512], ATT_DT, stack=st)
        pts = [P.sb("d_pt%d" % i, [128, 512], ATT_DT, stack=st) for i in range(2)]
        oat = P.sb("d_oat", [128, 4, 512], stack=st)
        rs = P.sb("d_rs", [128, 4], stack=st)
        mt = P.sb("d_mt", [128, 4, 512], stack=st)
        pS = [P.ps("d_pS%d" % i, [128, 512], stack=st) for i in range(2)]
        pO = [P.ps("d_pO%d" % i, [128, 512], stack=st) for i in range(4)]
        pT = P.ps("d_pT", [128, 1024], stack=st)
        mxb = P.buf("MIXT")
        FTq = d.FT[1024:1536, :].rearrange("(a dd) t -> dd a t", dd=64)
        for qg in range(NQG):
            P.dma("sp", stg.ap[:], FTq[:, :, qg * 512:(qg + 1) * 512], writes=[stg])
            P.op("dve", lambda: nc.vector.tensor_copy(qTb.ap[:], stg.ap[:]), reads=[stg], writes=[qTb])
            for h in range(8):
                kvh = h // 4
                for jt in range(NT):
                    ps = pS[jt % 2]
                    pt = pts[jt % 2]
                    P.op("pe", lambda: nc.tensor.matmul(ps.ap[:], kT.ap[:, kvh, jt * 128:(jt + 1) * 128], qTb.ap[:, h, :], start=True, stop=True),
                         reads=[kT, qTb], writes=[ps])
                    P.op("act", lambda: nc.scalar.activation(pt.ap[:], ps.ap[:], AF.Exp), reads=[ps], writes=[pt])
                    for b in range(4):
                        P.op("pe", lambda: nc.tensor.matmul(pO[b].ap[:, 0:65], pt.ap[:, b * 128:(b + 1) * 128], vaug.ap[:, jt, kvh, :],
                                                            start=(jt == 0), stop=(jt == NT - 1)), reads=[pt, vaug], writes=[pO[b]])
                for b in range(4):
                    P.op("dve", lambda: nc.vector.reciprocal(rs.ap[:, b:b + 1], pO[b].ap[:, 64:65]), reads=[pO[b]], writes=[rs])
                    P.op("dve", lambda: nc.vector.tensor_scalar(oat.ap[:, b, h * 64:(h + 1) * 64], pO[b].ap[:, 0:64], rs.ap[:, b:b + 1], None, ALU.mult),
                         reads=[pO[b], rs], writes=[oat])
            srcs = [oat.ap[:, b, fb * 128:(fb + 1) * 128] for fb in range(4) for b in range(4)]
            dsts = [mt.ap[:, fb, b * 128:(b + 1) * 128] for fb in range(4) for b in range(4)]
            transpose_blocks(P, C, oat, srcs, pT, mt, dsts)
            P.dma("pool", MX[:, 4:8, qg * 512:(qg + 1) * 512], mt.ap[:], reads=[mt], writes=[mxb])
    P.barrier()


def stage_e(P, C, d, cfg, l, xsrc):
    nc = P.nc
    NT = cfg.NT
    MXv = d.MIXT.rearrange("(c p) t -> p c t", p=128)
    with ExitStack() as st:
        wout = P.sb("e_wout", [128, 8, DM], stack=st)
        wq = P.sb("e_wq", [128, 8, 2048], stack=st)
        keysT = P.sb("e_keysT", [128, 16, 128], stack=st)
        lnw = P.sb("e_lnw", [128, DM], stack=st)
        for c in range(8):
            P.dma("sp", wout.ap[:, c, :], d.w_out[l, c * 128:(c + 1) * 128, :], writes=[wout])
            P.dma("act", wq.ap[:, c, :], d.peer_wq[l, c * 128:(c + 1) * 128, :], writes=[wq])
        P.dma("sp", keysT.ap[:], d.peer_keysT[l].rearrange("q k n -> k q n"), writes=[keysT])
        P.dma("sp", lnw.ap[:], d.ln2_w[l:l + 1, :].to_broadcast([128, DM]), writes=[lnw])
        mixs = [P.sb("e_mix%d" % i, [128, 8, 128], stack=st) for i in range(2)]
        xts = [P.sb("e_x%d" % i, [128, DM], stack=st) for i in range(2)]
        x2 = P.sb("e_x2", [128, DM], stack=st)
        xn = P.sb("e_xn", [128, DM], stack=st)
        junk = P.sb("e_junk", [128, DM], stack=st)
        ssq = P.sb("e_ssq", [128, 1], stack=st)
        xnT = P.sb("e_xnT", [128, 8, 128], stack=st)
        qT = P.sb("e_qT", [128, 16, 128], stack=st)
        sc = P.sb("e_sc", [128, 2048], stack=st)
        tmp = P.sb("e_tmp", [128, 256], stack=st)
        m01 = P.sb("e_m01", [128, 2, 16], stack=st)
        i01 = P.sb("e_i01", [128, 2, 16], U32, stack=st)
        i01f = P.sb("e_i01f", [128, 2, 16], stack=st)
        cand = P.sb("e_cand", [128, 16, 16], stack=st)
        cidx = P.sb("e_cidx", [128, 16, 16], stack=st)
        fv = P.sb("e_fv", [128, 16], stack=st)
        nmx = P.sb("e_nmx", [128, 1], stack=st)
        idxf = P.sb("e_idxf", [128, 128], stack=st)
        junk2 = P.sb("e_junk2", [128, 256], stack=st)
        ik = [P.buf("ik%d" % i) for i in range(128)]
        ee = P.sb("e_ee", [128, 8, 16], stack=st)
        esum = P.sb("e_esum", [128, 8], stack=st)
        idxi = P.sb("e_idxi", [128, 128], I32, stack=st)
        gw = P.sb("e_gw", [128, 128], stack=st)
        pX = P.ps("e_pX", [128, 1024], stack=st)
        pT = P.ps("e_pT", [128, 1024], stack=st)
        pS = P.ps("e_pS", [128, 2048], stack=st)
        x2b, xnb, idb, gwb = P.buf("X2"), P.buf("XN"), P.buf("IDX"), P.buf("GW")
        for t in range(NT):
            rows = slice(t * 128, (t + 1) * 128)
            mix = mixs[t % 2]
            xt = xts[t % 2]
            P.dma("sp", mix.ap[:], MXv[:, :, t * 128:(t + 1) * 128], writes=[mix])
            P.dma("sp", xt.ap[:], xsrc[rows, :], writes=[xt])
            for nb in range(2):
                for c in range(8):
                    P.op("pe", lambda: nc.tensor.matmul(pX.ap[:, nb * 512:(nb + 1) * 512], mix.ap[:, c, :], wout.ap[:, c, nb * 512:(nb + 1) * 512],
                                                        start=(c == 0), stop=(c == 7)), reads=[mix, wout], writes=[pX])
            P.op("dve", lambda: nc.vector.tensor_tensor(x2.ap[:], pX.ap[:], xt.ap[:], ALU.add), reads=[pX, xt], writes=[x2])
            P.dma("pool", d.X2[rows, :], x2.ap[:], reads=[x2], writes=[x2b])
            rms_rstd(P, x2, x2.ap[:], junk, ssq, DM)
            P.op("dve", lambda: nc.vector.scalar_tensor_tensor(xn.ap[:], x2.ap[:], ssq.ap[:, 0:1], lnw.ap[:], ALU.mult, ALU.mult),
                 reads=[x2, ssq, lnw], writes=[xn])
            P.dma("pool", d.XN[rows, :], xn.ap[:], reads=[xn], writes=[xnb])
            transpose_blocks(P, C, xn, [xn.ap[:, c * 128:(c + 1) * 128] for c in range(8)], pT,
                             xnT, [xnT.ap[:, c, :] for c in range(8)])
            for q4 in range(4):
                for qq in range(4):
                    qc = q4 * 4 + qq
                    for c in range(8):
                        P.op("pe", lambda: nc.tensor.matmul(pX.ap[:, qq * 128:(qq + 1) * 128], wq.ap[:, c, qc * 128:(qc + 1) * 128], xnT.ap[:, c, :],
                                                            start=(c == 0), stop=(c == 7)), reads=[wq, xnT], writes=[pX])
                if q4 % 2 == 0:
                    P.op("act", lambda: nc.scalar.copy(qT.ap[:, q4 * 4:q4 * 4 + 4, :], pX.ap[:, 0:512].rearrange("p (a b) -> p a b", b=128)),
                         reads=[pX], writes=[qT])
                else:
                    P.op("dve", lambda: nc.vector.tensor_copy(qT.ap[:, q4 * 4:q4 * 4 + 4, :], pX.ap[:, 0:512].rearrange("p (a b) -> p a b", b=128)),
                         reads=[pX], writes=[qT])
            for qc in range(16):
                P.op("pe", lambda: nc.tensor.matmul(pS.ap[:, qc * 128:(qc + 1) * 128], qT.ap[:, qc, :], keysT.ap[:, qc, :], start=True, stop=True),
                     reads=[qT, keysT], writes=[pS])
            for nb in range(4):
                P.op("act", lambda: nc.scalar.copy(sc.ap[:, nb * 512:(nb + 1) * 512], pS.ap[:, nb * 512:(nb + 1) * 512]), reads=[pS], writes=[sc])
            V = nc.vector
            for h in range(8):
                for p in range(2):
                    s_ = sc.ap[:, (h * 2 + p) * 128:(h * 2 + p + 1) * 128]
                    P.op("dve", lambda: V.max(m01.ap[:, p, 0:8], s_), reads=[sc], writes=[m01])
                    P.op("dve", lambda: V.match_replace(tmp.ap[:, 0:128], m01.ap[:, p, 0:8], s_, NEG), reads=[sc, m01], writes=[tmp])
                    P.op("dve", lambda: V.max(m01.ap[:, p, 8:16], tmp.ap[:, 0:128]), reads=[tmp], writes=[m01])
                    P.op("dve", lambda: V.max_index(i01.ap[:, p, 0:8], m01.ap[:, p, 0:8], s_), reads=[sc, m01], writes=[i01])
                    P.op("dve", lambda: V.max_index(i01.ap[:, p, 8:16], m01.ap[:, p, 8:16], s_), reads=[sc, m01], writes=[i01])
                P.op("dve", lambda: V.tensor_scalar(i01f.ap[:, 0, :], i01.ap[:, 0, :], 128.0, None, ALU.mult), reads=[i01], writes=[i01f])
                P.op("dve", lambda: V.tensor_copy(i01f.ap[:, 1, :], i01.ap[:, 1, :]), reads=[i01], writes=[i01f])
                P.op("dve", lambda: V.tensor_tensor(cand.ap[:], bc_last(m01.ap[:, 0, :], 16), bc_mid(m01.ap[:, 1, :], 16), ALU.add),
                     reads=[m01], writes=[cand])
                P.op("dve", lambda: V.tensor_tensor(cidx.ap[:], bc_last(i01f.ap[:, 0, :], 16), bc_mid(i01f.ap[:, 1, :], 16), ALU.add),
                     reads=[i01f], writes=[cidx])
                cf = cand.ap[:].rearrange("p a b -> p (a b)")
                xf = cidx.ap[:].rearrange("p a b -> p (a b)")
                P.op("dve", lambda: V.max(fv.ap[:, 0:8], cf), reads=[cand], writes=[fv])
                P.op("dve", lambda: V.match_replace(tmp.ap[:], fv.ap[:, 0:8], cf, NEG), reads=[cand, fv], writes=[tmp])
                P.op("dve", lambda: V.max(fv.ap[:, 8:16], tmp.ap[:]), reads=[tmp], writes=[fv])
                for k in range(16):
                    P.op("dve", lambda: V.scalar_tensor_tensor(junk2.ap[:], cf, fv.ap[:, k:k + 1], xf, ALU.is_equal, ALU.mult,
                                                               accum_out=idxf.ap[:, h * 16 + k:h * 16 + k + 1]),
                         reads=[cand, cidx, fv], writes=[ik[h * 16 + k]])
                P.op("dve", lambda: V.tensor_scalar(nmx.ap[:], fv.ap[:, 0:1], -1.0, None, ALU.mult), reads=[fv], writes=[nmx])
                P.op("act", lambda: nc.scalar.activation(ee.ap[:, h, :], fv.ap[:], AF.Exp, bias=nmx.ap[:, 0:1], accum_out=esum.ap[:, h:h + 1]),
                     reads=[fv, nmx], writes=[ee, esum])
            P.op("dve", lambda: V.reciprocal(esum.ap[:], esum.ap[:]), reads=[esum], writes=[esum])
            P.op("dve", lambda: V.tensor_tensor(gw.ap[:].rearrange("p (h k) -> p h k", k=16), ee.ap[:], bc_last(esum.ap[:], 16), ALU.mult),
                 reads=[ee, esum], writes=[gw])
            P.op("dve", lambda: V.tensor_scalar(idxf.ap[:], idxf.ap[:], 0.0, 16383.0, ALU.max, ALU.min), reads=ik, writes=ik + [idxf])
            P.op("dve", lambda: V.tensor_copy(idxi.ap[:], idxf.ap[:]), reads=[idxf], writes=[idxi])
            P.dma("pool", d.IDX[rows, :], idxi.ap[:], reads=[idxi], writes=[idb])
            P.dma("pool", d.GW[rows, :], gw.ap[:], reads=[gw], writes=[gwb])
    P.barrier()


def stage_f(P, C, d, cfg, l, last):
    nc = P.nc
    NT = cfg.NT
    NR = 6
    with ExitStack() as st:
        idxs = [P.sb("f_idx%d" % i, [128, 128], I32, stack=st) for i in range(2)]
        gws = [P.sb("f_gw%d" % i, [128, 128], stack=st) for i in range(2)]
        xns = [P.sb("f_xn%d" % i, [128, DM], stack=st) for i in range(2)]
        ys = [P.sb("f_y%d" % i, [128, DM], stack=st) for i in range(2)]
        ring = [P.sb("f_g%d" % i, [128, DM], stack=st) for i in range(NR)]
        junk = P.sb("f_junk", [128, DM], stack=st)
        hid = P.sb("f_hid", [128, 128], stack=st)
        cc = P.sb("f_cc", [128, 128], stack=st)
        tmp = P.sb("f_tmp", [128, 128], stack=st)
        xrb = P.buf("XR")
        U = d.peer_u[l]
        Vt = d.peer_v[l]
        gi = 0
        for t in range(NT):
            rows = slice(t * 128, (t + 1) * 128)
            idx, gw, xn, y = idxs[t % 2], gws[t % 2], xns[t % 2], ys[t % 2]
            P.dma("sp", idx.ap[:], d.IDX[rows, :], writes=[idx])
            P.dma("sp", gw.ap[:], d.GW[rows, :], writes=[gw])
            P.dma("sp", xn.ap[:], d.XN[rows, :], writes=[xn])
            P.dma("sp", y.ap[:], d.X2[rows, :], writes=[y])
            for k in range(128):
                g = ring[gi % NR]
                gi += 1
                P.dma("pool", None, None, reads=[idx], writes=[g],
                      fn=lambda: nc.gpsimd.indirect_dma_start(out=g.ap[:, :], out_offset=None, in_=U,
                                                              in_offset=bass.IndirectOffsetOnAxis(ap=idx.ap[:, k:k + 1], axis=0)))
                P.op("dve", lambda: nc.vector.scalar_tensor_tensor(junk.ap[:], g.ap[:], 1.0, xn.ap[:], ALU.mult, ALU.mult,
                                                                   accum_out=hid.ap[:, k:k + 1]),
                     reads=[g, xn], writes=[junk, hid])
            gelu_tanh(P, "dve", cc.ap[:], hid.ap[:], tmp.ap[:], hid, cc, tmp)
            P.op("dve", lambda: nc.vector.tensor_tensor(cc.ap[:], cc.ap[:], gw.ap[:], ALU.mult), reads=[cc, gw], writes=[cc])
            for k in range(128):
                g = ring[gi % NR]
                gi += 1
                P.dma("pool", None, None, reads=[idx], writes=[g],
                      fn=lambda: nc.gpsimd.indirect_dma_start(out=g.ap[:, :], out_offset=None, in_=Vt,
                                                              in_offset=bass.IndirectOffsetOnAxis(ap=idx.ap[:, k:k + 1], axis=0)))
                P.op("dve", lambda: nc.vector.scalar_tensor_tensor(y.ap[:], g.ap[:], cc.ap[:, k:k + 1], y.ap[:], ALU.mult, ALU.add),
                     reads=[g, cc, y], writes=[y])
            P.dma("sp", d.XR[rows, :], y.ap[:], reads=[y], writes=[xrb])
    P.barrier()


def stage_p(P, C, d, cfg, l):
    nc = P.nc
    J = 4
    NB = 16384 // (128 * J)
    Uv = d.peer_u[l].rearrange("(n p j) c -> n p j c", p=128, j=J)
    Vv = d.peer_v[l].rearrange("(n p j) c -> n p j c", p=128, j=J)
    Ov = d.UVB[l].rearrange("(n p j) c -> n p j c", p=128, j=J)
    with ExitStack() as st:
        us = [P.sb("p_u%d" % i, [128, J, DM], stack=st) for i in range(2)]
        vs = [P.sb("p_v%d" % i, [128, J, DM], stack=st) for i in range(2)]
        os_ = [P.sb("p_o%d" % i, [128, J, 2048], BF16, stack=st) for i in range(2)]
        ob = P.buf("UVB")
        for n in range(NB):
            u, v, o = us[n % 2], vs[n % 2], os_[n % 2]
            P.dma("sp", u.ap[:], Uv[n], writes=[u])
            P.dma("act", v.ap[:], Vv[n], writes=[v])
            P.op("act", lambda: nc.scalar.copy(o.ap[:, :, 0:1024], u.ap[:]), reads=[u], writes=[o])
            if n % 2 == 0:
                P.op("dve", lambda: nc.vector.tensor_copy(o.ap[:, :, 1024:2048], v.ap[:]), reads=[v], writes=[o])
            else:
                P.op("pool", lambda: nc.gpsimd.tensor_copy(o.ap[:, :, 1024:2048], v.ap[:]), reads=[v], writes=[o])
            P.dma("sp", Ov[n], o.ap[:], reads=[o], writes=[ob])
    P.barrier()


def stage_f2(P, C, d, cfg, l):
    nc = P.nc
    NT = cfg.NT
    NR = 20
    G = 8
    with ExitStack() as st:
        idxs = [P.sb("f_idx%d" % i, [128, 128], I32, stack=st) for i in range(2)]
        gws = [P.sb("f_gw%d" % i, [128, 128], stack=st) for i in range(2)]
        xns = [P.sb("f_xn%d" % i, [128, DM], stack=st) for i in range(2)]
        x2s = [P.sb("f_x2%d" % i, [128, DM], stack=st) for i in range(2)]
        ys = [P.sb("f_y%d" % i, [128, DM], stack=st) for i in range(2)]
        ring = [P.sb("f_g%d" % i, [128, 2048], BF16, stack=st) for i in range(NR)]
        dgs = [P.sb("f_dg%d" % i, [128, 128], BF16, stack=st) for i in range(NR)]
        junk = P.sb("f_junk", [128, DM], BF16, stack=st)
        hid = P.sb("f_hid", [128, 128], stack=st)
        hk = [P.buf("hk%d" % i) for i in range(128)]
        cc = P.sb("f_cc", [128, 128], stack=st)
        tmp = P.sb("f_tmp", [128, 128], stack=st)
        pys = [P.ps("f_py%d" % i, [128, DM], stack=st) for i in range(2)]
        xrb = P.buf("XR")
        UV = d.UVB[l]
        gi = 0
        for t in range(NT):
            rows = slice(t * 128, (t + 1) * 128)
            idx, gw, xn, x2, y, py = idxs[t % 2], gws[t % 2], xns[t % 2], x2s[t % 2], ys[t % 2], pys[t % 2]
            P.dma("sp", idx.ap[:], d.IDX[rows, :], writes=[idx])
            P.dma("sp", gw.ap[:], d.GW[rows, :], writes=[gw])
            P.dma("sp", xn.ap[:], d.XN[rows, :], writes=[xn])
            P.dma("sp", x2.ap[:], d.X2[rows, :], writes=[x2])
            for grp in range(128 // G):
                k0 = grp * G
                slots = []
                for k in range(k0, k0 + G):
                    j = gi % NR
                    gi += 1
                    g = ring[j]
                    slots.append(j)
                    P.dma("pool", None, None, reads=[idx], writes=[g],
                          fn=lambda: nc.gpsimd.indirect_dma_start(out=g.ap[:, :], out_offset=None, in_=UV,
                                                                  in_offset=bass.IndirectOffsetOnAxis(ap=idx.ap[:, k:k + 1], axis=0)))
                    P.op("dve", lambda: nc.vector.scalar_tensor_tensor(junk.ap[:], g.ap[:, 0:1024], 1.0, xn.ap[:], ALU.mult, ALU.mult,
                                                                       accum_out=hid.ap[:, k:k + 1]),
                         reads=[g, xn], writes=[hk[k]])
                hg = hk[k0:k0 + G]
                hs, cs, ts, gs = hid.ap[:, k0:k0 + G], cc.ap[:, k0:k0 + G], tmp.ap[:, k0:k0 + G], gw.ap[:, k0:k0 + G]
                V = nc.vector
                P.op("dve", lambda: V.tensor_tensor(ts, hs, hs, ALU.mult), reads=hg, writes=[tmp])
                P.op("dve", lambda: V.tensor_scalar(ts, ts, 0.044715, 1.0, ALU.mult, ALU.add), reads=[tmp], writes=[tmp])
                P.op("dve", lambda: V.tensor_tensor(ts, ts, hs, ALU.mult), reads=[tmp] + hg, writes=[tmp])
                P.op("act", lambda: nc.scalar.activation(ts, ts, AF.Sigmoid, scale=1.5957691216057308), reads=[tmp], writes=[tmp])
                P.op("dve", lambda: V.tensor_tensor(ts, ts, hs, ALU.mult), reads=[tmp] + hg, writes=[tmp])
                P.op("dve", lambda: V.tensor_tensor(cs, ts, gs, ALU.mult), reads=[tmp, gw], writes=[cc])
                for i, k in enumerate(range(k0, k0 + G)):
                    j = slots[i]
                    g, dg = ring[j], dgs[j]
                    P.op("act", lambda: nc.scalar.activation(dg.ap[:], C.ident, AF.Copy, scale=cc.ap[:, k:k + 1]),
                         reads=[C.mat, cc], writes=[dg])
                    for nb in range(2):
                        P.op("pe", lambda: nc.tensor.matmul(py.ap[:, nb * 512:(nb + 1) * 512], dg.ap[:], g.ap[:, 1024 + nb * 512:1024 + (nb + 1) * 512],
                                                            start=(k == 0), stop=(k == 127)), reads=[dg, g], writes=[py])
            P.op("dve", lambda: nc.vector.tensor_tensor(y.ap[:], py.ap[:], x2.ap[:], ALU.add), reads=[py, x2], writes=[y])
            P.dma("sp", d.XR[rows, :], y.ap[:], reads=[y], writes=[xrb])
    P.barrier()


def stage_final(P, C, d, cfg):
    nc = P.nc
    with ExitStack() as st:
        lnw = P.sb("z_lnw", [128, DM], stack=st)
        P.dma("sp", lnw.ap[:], d.lnf_w[0:1, :].to_broadcast([128, DM]), writes=[lnw])
        xts = [P.sb("z_x%d" % i, [128, DM], stack=st) for i in range(2)]
        os_ = [P.sb("z_o%d" % i, [128, DM], stack=st) for i in range(2)]
        junk = P.sb("z_junk", [128, DM], stack=st)
        ssq = P.sb("z_ssq", [128, 1], stack=st)
        ob = P.buf("out")
        for t in range(cfg.NT):
            rows = slice(t * 128, (t + 1) * 128)
            xt, o = xts[t % 2], os_[t % 2]
            P.dma("sp", xt.ap[:], d.XR[rows, :], writes=[xt])
            rms_rstd(P, xt, xt.ap[:], junk, ssq, DM)
            P.op("dve", lambda: nc.vector.scalar_tensor_tensor(o.ap[:], xt.ap[:], ssq.ap[:, 0:1], lnw.ap[:], ALU.mult, ALU.mult),
                 reads=[xt, ssq, lnw], writes=[o])
            P.dma("pool", d.out[rows, :], o.ap[:], reads=[o], writes=[ob])
    P.barrier()


def build_program(cfg):
    nc = bass.Bass("TRN2", target_bir_lowering=False)
    d = declare_dram(nc, cfg)
    with ExitStack() as st:
        P = Prog(nc, st)
        C = load_consts(P, d, cfg)
        for l in range(cfg.L):
            stage_p(P, C, d, cfg, l)
        for l in range(cfg.L):
            xsrc = d.x if l == 0 else d.XR
            stage_a(P, C, d, cfg, l, xsrc)
            stage_b(P, C, d, cfg, l)
            stage_c(P, C, d, cfg, l)
            stage_d(P, C, d, cfg, l)
            stage_e(P, C, d, cfg, l, xsrc)
            stage_f2(P, C, d, cfg, l)
        stage_final(P, C, d, cfg)
        P.finish()
    return nc, P


def kernel(**inputs):
    cfg = Cfg(T=4096, L=4)
    nc, P = build_program(cfg)
    in_maps = [make_in_map(inputs, c, cfg) for c in range(8)]
    res = run_bass_kernel_spmd(nc, in_maps, core_ids=list(range(8)))
    out = np.stack([np.asarray(res.results[c]["out"], dtype=np.float32) for c in range(8)], 0)
    return out
```
